# Optimizing a Trainium2 kernel written in Bass

```python
import math
import jax, jax.numpy as jnp
from jax import lax
import numpy as np

D_MODEL = 1024
BATCH = 1
SEQ = 16384
DEPTH = 1
DEC_BATCH = 8
DEC_SEQ = 64
PAST_LEN = 1024

CHUNK = 64
Q_BLOCK = 128
A_HEADS = 4
A_QK_DIM = 64
A_V_DIM = 2 * A_QK_DIM
B_HEADS = 8
B_DIM = 64
A_WIDTH = A_HEADS * A_V_DIM
B_WIDTH = B_HEADS * B_DIM
MIX_WIDTH = A_WIDTH + B_WIDTH
ROT_DIM = A_QK_DIM // 4
ROPE_THETA = 500000.0
EPS = 1e-6
A_QK_WIDTH = A_HEADS * 2 * A_QK_DIM
IN_WIDTH = 2 * A_QK_WIDTH + 2 * A_WIDTH + 4 * B_WIDTH + B_HEADS
FORGET_BIAS = 3.0

kernel_name = "hybrid_diff_fox_stream_step"

F32 = jnp.float32


def rms_norm(x, g):
    xf = x.astype(F32)
    y = xf * lax.rsqrt(jnp.mean(xf * xf, axis=-1, keepdims=True) + EPS)
    return y.astype(x.dtype) * g


def rope(x, pos):
    t = x.shape[1]
    half = ROT_DIM // 2
    inv = ROPE_THETA ** (-jnp.arange(0, ROT_DIM, 2, dtype=F32) / ROT_DIM)
    ang = pos.astype(F32)[:, None] * inv[None, :]
    shape = (1, t) + (1,) * (x.ndim - 3) + (half,)
    cos = jnp.cos(ang).reshape(shape).astype(x.dtype)
    sin = jnp.sin(ang).reshape(shape).astype(x.dtype)
    x1 = x[..., :half]
    x2 = x[..., half:ROT_DIM]
    return jnp.concatenate([x1 * cos - x2 * sin, x2 * cos + x1 * sin, x[..., ROT_DIM:]], axis=-1)


def split_offsets():
    sizes = [A_QK_WIDTH, A_QK_WIDTH, A_WIDTH, A_WIDTH, B_WIDTH, B_WIDTH, B_WIDTH, B_HEADS, B_WIDTH]
    offs = []
    s = 0
    for n in sizes[:-1]:
        s += n
        offs.append(s)
    return offs


def project(x, c, pos, norm_g, w_ada, b_ada, w_in, b_f, qn_a, kn_a, qn_b, kn_b):
    b, t = x.shape[:2]
    mod = jax.nn.silu(c) @ w_ada + b_ada
    shift, scale, gate = jnp.split(mod, 3, axis=-1)
    h = rms_norm(x, norm_g) * (1.0 + scale[:, None, :]) + shift[:, None, :]
    u = h @ w_in
    qa, ka, va, za, qb, kb, vb, fb, zb = jnp.split(u, split_offsets(), axis=-1)
    qa = rope(rms_norm(qa.reshape(b, t, A_HEADS, 2, A_QK_DIM), qn_a), pos)
    ka = rope(rms_norm(ka.reshape(b, t, A_HEADS, 2, A_QK_DIM), kn_a), pos)
    va = va.reshape(b, t, A_HEADS, A_V_DIM)
    qb = rms_norm(qb.reshape(b, t, B_HEADS, B_DIM), qn_b)
    kb = rms_norm(kb.reshape(b, t, B_HEADS, B_DIM), kn_b)
    vb = vb.reshape(b, t, B_HEADS, B_DIM)
    logf = jax.nn.log_sigmoid((fb + b_f).astype(F32))
    return gate, qa, ka, va, za, qb, kb, vb, zb, logf


def diff_attention(q, k, v, mask, lam):
    s = jnp.einsum('bqhjd,bkhjd->bhjqk', q, k).astype(F32) * (A_QK_DIM ** -0.5)
    p = jax.nn.softmax(jnp.where(mask, s, -jnp.inf), axis=-1)
    a = p[:, :, 0] - lam * p[:, :, 1]
    return jnp.einsum('bhqk,bkhe->bqhe', a.astype(v.dtype), v)


def forgetting_attention(q, k, v, mask, bias):
    s = jnp.einsum('bqhd,bkhd->bhqk', q, k).astype(F32) * (B_DIM ** -0.5) + bias
    p = jax.nn.softmax(jnp.where(mask, s, -jnp.inf), axis=-1)
    return jnp.einsum('bhqk,bkhd->bqhd', p.astype(v.dtype), v)


def merge(x, gate, oa, za, ob, zb, subln_g, lam_init, w_out):
    b, t = x.shape[:2]
    oa = rms_norm(oa, subln_g) * (1.0 - lam_init)
    ga = oa.reshape(b, t, A_WIDTH) * jax.nn.silu(za)
    gb = ob.reshape(b, t, B_WIDTH) * jax.nn.silu(zb)
    o = jnp.concatenate([ga, gb], axis=-1) @ w_out
    return x + gate[:, None, :] * o


def setup_inputs(seed: int = 0) -> dict:
    key = jax.random.key(seed)
    ks = jax.random.split(key, 32)
    nrm = jax.random.normal
    d = D_MODEL
    inp = {}
    inp['x_prompt'] = nrm(ks[0], (BATCH, SEQ, d), F32)
    inp['x_sample'] = nrm(ks[1], (DEC_BATCH, DEC_SEQ, d), F32)
    inp['cache_a_k'] = nrm(ks[2], (DEPTH, DEC_BATCH, PAST_LEN, A_HEADS, 2, A_QK_DIM), F32)
    inp['cache_a_v'] = nrm(ks[3], (DEPTH, DEC_BATCH, PAST_LEN, A_HEADS, A_V_DIM), F32)
    inp['cache_b_k'] = nrm(ks[4], (DEPTH, DEC_BATCH, PAST_LEN, B_HEADS, B_DIM), F32)
    inp['cache_b_v'] = nrm(ks[5], (DEPTH, DEC_BATCH, PAST_LEN, B_HEADS, B_DIM), F32)
    inp['cache_b_logf'] = jax.nn.log_sigmoid(FORGET_BIAS + nrm(ks[6], (DEPTH, DEC_BATCH, PAST_LEN, B_HEADS), F32))
    inp['c_prompt'] = nrm(ks[7], (BATCH, d), F32)
    inp['c_sample'] = nrm(ks[8], (DEC_BATCH, d), F32)
    inp['norm_g'] = 1.0 + 0.02 * nrm(ks[9], (DEPTH, d), F32)
    inp['w_ada'] = nrm(ks[10], (DEPTH, d, 3 * d), F32) * d ** -0.5
    inp['b_ada'] = 0.01 * nrm(ks[11], (DEPTH, 3 * d), F32)
    inp['w_in'] = nrm(ks[12], (DEPTH, d, IN_WIDTH), F32) * d ** -0.5
    inp['b_f'] = FORGET_BIAS + 0.1 * nrm(ks[13], (DEPTH, B_HEADS), F32)
    inp['qn_a'] = 1.0 + 0.02 * nrm(ks[14], (DEPTH, A_QK_DIM), F32)
    inp['kn_a'] = 1.0 + 0.02 * nrm(ks[15], (DEPTH, A_QK_DIM), F32)
    inp['lam_q1'] = 0.1 * nrm(ks[16], (DEPTH, A_QK_DIM), F32)
    inp['lam_k1'] = 0.1 * nrm(ks[17], (DEPTH, A_QK_DIM), F32)
    inp['lam_q2'] = 0.1 * nrm(ks[18], (DEPTH, A_QK_DIM), F32)
    inp['lam_k2'] = 0.1 * nrm(ks[19], (DEPTH, A_QK_DIM), F32)
    inp['subln_g'] = 1.0 + 0.02 * nrm(ks[20], (DEPTH, A_V_DIM), F32)
    inp['qn_b'] = 1.0 + 0.02 * nrm(ks[21], (DEPTH, B_DIM), F32)
    inp['kn_b'] = 1.0 + 0.02 * nrm(ks[22], (DEPTH, B_DIM), F32)
    inp['w_out'] = nrm(ks[23], (DEPTH, MIX_WIDTH, d), F32) * MIX_WIDTH ** -0.5
    return inp


def reference(x_prompt, x_sample, cache_a_k, cache_a_v, cache_b_k, cache_b_v, cache_b_logf,
              c_prompt, c_sample, norm_g, w_ada, b_ada, w_in, b_f, qn_a, kn_a,
              lam_q1, lam_k1, lam_q2, lam_k2, subln_g, qn_b, kn_b, w_out):
    xp = x_prompt
    xs = x_sample
    bp, seq = xp.shape[:2]
    bs, tdec = xs.shape[:2]
    past = cache_a_k.shape[2]
    n_blocks = seq // Q_BLOCK
    pos_p = jnp.arange(seq)
    pos_s = past + jnp.arange(tdec)
    kpos_s = jnp.arange(past + tdec)
    mask_a_s = (kpos_s // CHUNK)[None, :] <= (pos_s // CHUNK)[:, None]
    mask_b_s = kpos_s[None, :] <= pos_s[:, None]
    pa_k, pa_v, pb_k, pb_v, pb_f = [], [], [], [], []
    sa_k, sa_v, sb_k, sb_v, sb_f = [], [], [], [], []
    for l in range(DEPTH):
        lam_init = 0.8 - 0.6 * math.exp(-0.3 * l)
        lam = (jnp.exp(jnp.sum((lam_q1[l] * lam_k1[l]).astype(F32)))
               - jnp.exp(jnp.sum((lam_q2[l] * lam_k2[l]).astype(F32))) + lam_init)
        lw = (norm_g[l], w_ada[l], b_ada[l], w_in[l], b_f[l], qn_a[l], kn_a[l], qn_b[l], kn_b[l])

        gate, qa, ka, va, za, qb, kb, vb, zb, logf = project(xp, c_prompt, pos_p, *lw)
        cum_t = jnp.cumsum(logf, axis=1).transpose(0, 2, 1)

        def prompt_block(i):
            start = i * Q_BLOCK
            qpos = start + jnp.arange(Q_BLOCK)
            mask_a = (pos_p // CHUNK)[None, :] <= (qpos // CHUNK)[:, None]
            oa_blk = diff_attention(lax.dynamic_slice_in_dim(qa, start, Q_BLOCK, axis=1), ka, va, mask_a, lam)
            cq = lax.dynamic_slice_in_dim(cum_t, start, Q_BLOCK, axis=2)
            bias = cq[:, :, :, None] - cum_t[:, :, None, :]
            mask_b = pos_p[None, :] <= qpos[:, None]
            ob_blk = forgetting_attention(lax.dynamic_slice_in_dim(qb, start, Q_BLOCK, axis=1), kb, vb, mask_b, bias)
            return oa_blk, ob_blk

        oa_b, ob_b = lax.map(prompt_block, jnp.arange(n_blocks))
        oa = jnp.moveaxis(oa_b, 0, 1).reshape(bp, seq, A_HEADS, A_V_DIM)
        ob = jnp.moveaxis(ob_b, 0, 1).reshape(bp, seq, B_HEADS, B_DIM)
        xp = merge(xp, gate, oa, za, ob, zb, subln_g[l], lam_init, w_out[l])
        pa_k.append(ka); pa_v.append(va); pb_k.append(kb); pb_v.append(vb); pb_f.append(logf)

        gate_s, qa_s, ka_s, va_s, za_s, qb_s, kb_s, vb_s, zb_s, logf_s = project(xs, c_sample, pos_s, *lw)
        ka_all = jnp.concatenate([cache_a_k[l].astype(ka_s.dtype), ka_s], axis=1)
        va_all = jnp.concatenate([cache_a_v[l].astype(va_s.dtype), va_s], axis=1)
        oa_s = diff_attention(qa_s, ka_all, va_all, mask_a_s, lam)
        kb_all = jnp.concatenate([cache_b_k[l].astype(kb_s.dtype), kb_s], axis=1)
        vb_all = jnp.concatenate([cache_b_v[l].astype(vb_s.dtype), vb_s], axis=1)
        cum_s = jnp.cumsum(jnp.concatenate([cache_b_logf[l].astype(F32), logf_s], axis=1), axis=1).transpose(0, 2, 1)
        bias_s = cum_s[:, :, past:, None] - cum_s[:, :, None, :]
        ob_s = forgetting_attention(qb_s, kb_all, vb_all, mask_b_s, bias_s)
        xs = merge(xs, gate_s, oa_s, za_s, ob_s, zb_s, subln_g[l], lam_init, w_out[l])
        sa_k.append(ka_s); sa_v.append(va_s); sb_k.append(kb_s); sb_v.append(vb_s); sb_f.append(logf_s)

    return (xp, xs,
            jnp.stack(pa_k), jnp.stack(pa_v), jnp.stack(pb_k), jnp.stack(pb_v), jnp.stack(pb_f),
            jnp.stack(sa_k), jnp.stack(sa_v), jnp.stack(sb_k), jnp.stack(sb_v), jnp.stack(sb_f))
```

```python
import math
from contextlib import ExitStack

import numpy as np
import ml_dtypes
import concourse.bass as bass
import concourse.mybir as mybir
from concourse.bass_utils import run_bass_kernel_spmd

F32 = mybir.dt.float32
BF16 = mybir.dt.bfloat16
AF = mybir.ActivationFunctionType
ALU = mybir.AluOpType
AX = mybir.AxisListType

D = 1024
SEQ = 16384
NT = SEQ // 128
NOWN = 16
TDEC = 64
PAST = 1024
EPS = 1e-6
NKV = 2056
NQZ = 2048
LAM_INIT = 0.8 - 0.6 * math.exp(-0.3 * 0)
SNT = 9


class Sched:
    ROT = dict(ldc=8, ldw=2, ldx=4, ldr=2, ldm=1, ldk=4, stk=8, sto=8, stm=1, ldbr=1)

    def __init__(self):
        self.ops = []
        self.last_w = {}
        self.readers = {}
        self.nstream = {}

    @staticmethod
    def dom(op):
        return ("dma", op["stream"], op["slot"]) if op["stream"] is not None else op["eng"]

    def add(self, eng, fn, rd=(), wr=(), stream=None):
        idx = len(self.ops)
        deps = {}
        def need(j):
            d = self.dom(self.ops[j])
            if d not in deps or deps[d] < j:
                deps[d] = j
        if getattr(self, "serial", False) and idx > 0:
            need(idx - 1)
        for b in rd:
            if b in self.last_w:
                need(self.last_w[b])
        for b in wr:
            if b in self.last_w:
                need(self.last_w[b])
            for j in self.readers.get(b, {}).values():
                need(j)
        op = dict(eng=eng, fn=fn, deps=deps, stream=stream, sig=False, slot=None)
        if stream is not None:
            i = self.nstream.get(stream, 0)
            self.nstream[stream] = i + 1
            op["slot"] = i % self.ROT[stream]
        self.ops.append(op)
        mydom = self.dom(op)
        for b in wr:
            self.last_w[b] = idx
            self.readers[b] = {}
        for b in rd:
            if b in wr:
                continue
            self.readers.setdefault(b, {})[mydom] = idx

    def emit(self, nc, stack):
        ops = self.ops
        for op in ops:
            for d, j in op["deps"].items():
                if d == "pe" and op["eng"] == "pe" and op["stream"] is None:
                    continue
                ops[j]["sig"] = True
        cnt = {}
        for op in ops:
            d = self.dom(op)
            if op["stream"] is not None:
                cnt[d] = cnt.get(d, 0) + 16
                op["seq"] = cnt[d]
            elif op["sig"]:
                cnt[d] = cnt.get(d, 0) + 1
                op["seq"] = cnt[d]
        sems = {}
        for d in cnt:
            nm = "s_" + (d if isinstance(d, str) else "d_%s_%d" % (d[1], d[2]))
            sems[d] = stack.enter_context(nc.semaphore(nm))
        self.final = dict(cnt)
        engs = ["pe", "act", "dve", "pool", "sp"]
        per_eng = {e: [op for op in ops if op["eng"] == e] for e in engs}
        block = stack.enter_context(nc.Block())
        final = {d: v for d, v in cnt.items() if not isinstance(d, str)}

        def run(e, eobj):
            waited = {}
            for op in per_eng[e]:
                for d, j in op["deps"].items():
                    if d == "pe" and e == "pe" and op["stream"] is None:
                        continue
                    v = ops[j]["seq"]
                    if waited.get(d, 0) >= v:
                        continue
                    eobj.wait_ge(sems[d], v)
                    waited[d] = v
                if op["stream"] is not None and op["seq"] > 16:
                    d0 = self.dom(op)
                    if waited.get(d0, 0) < op["seq"] - 16:
                        eobj.wait_ge(sems[d0], op["seq"] - 16)
                        waited[d0] = op["seq"] - 16
                ins = op["fn"](eobj)
                if op["stream"] is not None:
                    ins.then_inc(sems[self.dom(op)], 16)
                elif op["sig"]:
                    ins.then_inc(sems[e], 1)
            if e == "sp":
                for d, v in final.items():
                    eobj.wait_ge(sems[d], v)

        @block.tensor
        def _(pe):
            run("pe", pe)

        @block.scalar
        def _(act):
            run("act", act)

        @block.vector
        def _(dve):
            run("dve", dve)

        @block.gpsimd
        def _(pool):
            run("pool", pool)

        @block.sync
        def _(sp):
            run("sp", sp)


def build_nc():
    nc = bass.Bass("TRN2", target_bir_lowering=False)
    S = Sched()
    st = ExitStack()

    def din(name, shape, dt=F32):
        return nc.dram_tensor(name, list(shape), dt, kind="ExternalInput").ap()

    def dout(name, shape, dt=F32):
        return nc.dram_tensor(name, list(shape), dt, kind="ExternalOutput").ap()

    def dscr(name, shape, dt):
        return nc.dram_tensor(name, list(shape), dt, kind="Internal").ap()

    x_all = din("x_all", [SEQ, D])
    x_own = din("x_own", [NOWN * 128, D])
    x_smp = din("x_smp", [TDEC, D])
    cvT = din("cvT", [128, 8, 2])
    w_ada = din("w_ada", [D, 3 * D])
    b_ada2 = din("b_ada2", [2, 3 * D])
    g_row2 = din("g_row2", [2, D])
    w_kv = din("w_kv", [D, NKV])
    w_qz = din("w_qz", [D, NQZ])
    w_out = din("w_out", [D, D])
    vecs = din("vecs", [128, 4, 64])
    lamv = din("lamv", [128, 4, 64])
    bf_b = din("bf_b", [128, 8])
    subg = din("subg", [128, 1])
    c_ak = din("c_ak", [PAST, 512])
    c_av = din("c_av", [PAST, 512])
    c_bk = din("c_bk", [PAST, 512])
    c_bv = din("c_bv", [PAST, 512])
    c_lf = din("c_lf", [PAST, 8])
    rope_all = din("rope_all", [NT, 128, 16])
    rope_own = din("rope_own", [NOWN, 128, 16])
    rope_smp = din("rope_smp", [TDEC, 16])
    k_identb = din("k_identb", [128, 128], BF16)
    k_identf = din("k_identf", [128, 128])
    k_U = din("k_U", [128, 128])
    k_E127 = din("k_E127", [128, 128])
    k_Psw = din("k_Psw", [128, 128])
    k_W1 = din("k_W1", [128, 256])
    k_Lt = din("k_Lt", [128, NOWN])
    k_sel2 = din("k_sel2", [2, 2, 128])
    k_bsel = din("k_bsel", [64, 128])
    k_maskA = din("k_maskA", [128, 8, 128], BF16)
    k_maskB = din("k_maskB", [128, 8, 128], BF16)
    k_maskS = din("k_maskS", [64, 64], BF16)

    o_y = dout("o_y", [NOWN * 128, D])
    o_ys = dout("o_ys", [TDEC, D])
    o_ak = dout("o_ak", [NOWN * 128, 512])
    o_av = dout("o_av", [NOWN * 128, 512])
    o_bk = dout("o_bk", [NOWN * 128, 512])
    o_bv = dout("o_bv", [NOWN * 128, 512])
    o_lf = dout("o_lf", [NOWN * 128, 8])
    s_ak = dout("s_ak", [TDEC, 512])
    s_av = dout("s_av", [TDEC, 512])
    s_bk = dout("s_bk", [TDEC, 512])
    s_bv = dout("s_bv", [TDEC, 512])
    s_lf = dout("s_lf", [TDEC, 8])

    def scr(prefix, ntl):
        return dict(
            KTA=dscr(prefix + "KTA", [4, 128, ntl * 128], BF16),
            VA=dscr(prefix + "VA", [4, 128, ntl, 128], BF16),
            KTB=dscr(prefix + "KTB", [8, 70, ntl * 128], BF16),
            VB=dscr(prefix + "VB", [4, 128, ntl, 192], BF16),
            pre=prefix,
        )
    modscr = dscr("modscr", [2, 3 * D], F32)
    SCP = scr("p", NT)
    SCS = scr("s", SNT)

    def sb(name, shape, dt=F32):
        return st.enter_context(nc.sbuf_tensor(name, list(shape), dt))

    identb = sb("identb", [128, 128], BF16)
    identf = sb("identf", [128, 128])
    Um = sb("Um", [128, 128])
    E127 = sb("E127", [128, 128])
    Psw = sb("Psw", [128, 128])
    W1 = sb("W1", [128, 256])
    Lt = sb("Lt", [128, NOWN])
    TSm = sb("TSm", [128, 8])
    sel2 = sb("sel2", [2, 2, 128])
    bsel = sb("bsel", [64, 128])
    Lsb = sb("Lsb", [64, 512])
    maskA = sb("maskA", [128, 8, 128], BF16)
    maskB = sb("maskB", [128, 8, 128], BF16)
    maskS = sb("maskS", [64, 64], BF16)
    ones_bf = sb("ones_bf", [128, 128], BF16)
    ones_f = sb("ones_f", [128, 128])
    vecs_sb = sb("vecs_sb", [128, 4, 64])
    lam_sb = sb("lam_sb", [128, 4, 64])
    bf_sb = sb("bf_sb", [128, 8])
    subg_sb = sb("subg_sb", [128, 1])
    small = sb("small", [128, 16])
    Wkv = sb("Wkv", [128, 8, NKV], BF16)
    Wqz = sb("Wqz", [128, 8, NQZ], BF16)
    Wout = sb("Wout", [128, 8, D], BF16)
    ZT = sb("ZT", [128, 8, 512])
    ZTf = ZT[:, :, :].rearrange("p a b -> p (a b)")
    wst = [ZTf[:, 0:2048], ZTf[:, 2048:4096]]
    wtail = sb("wtail", [128, 8, 8])
    gmod_b = sb("gmod_b", [128, D])
    shift_b = sb("shift_b", [128, D])
    gate_b = sb("gate_b", [128, D])
    mod_sb = ZTf[0:2, 0:3 * D]
    scT = sb("scT", [128, 8, 2])
    scT2 = sb("scT2", [128, 8, 2])

    xin = [sb("xin0", [128, D]), sb("xin1", [128, D])]
    sqj = sb("sqj", [128, D])
    xbf = sb("xbf", [128, D], BF16)
    xT = sb("xT", [128, 8, 128], BF16)
    KA = sb("KA", [128, 512])
    VAf = sb("VAf", [128, 512])
    KB = sb("KB", [128, 512])
    VBf = sb("VBf", [128, 512])
    rtab = sb("rtab", [128, 2, 16])
    rtmp = sb("rtmp", [128, 2, 4, 8, 8])
    st8 = sb("st8", [128, 2, 8, 8])
    sm2 = sb("sm2", [128, 2, 4])
    cumq_sb = sb("cumq_sb", [128, 8])
    logf = sb("logf", [128, 2, 8])
    cum = [sb("cum0", [128, 8]), sb("cum1", [128, 8])]
    TSs = sb("TSs", [128, 8])
    spl = sb("spl", [128, 2, 6, 8])
    splb = sb("splb", [128, 2, 3, 8], BF16)
    kabf = sb("kabf", [128, 512], BF16)
    Ktm = sb("Ktm", [128, 8, 70], BF16)
    Qtm = sb("Qtm", [128, 8, 70], BF16)
    KTas = sb("KTas", [128, 4, 128], BF16)
    KTbs = sb("KTbs", [70, 8, 128], BF16)
    vas = sb("vas", [128, 4, 128], BF16)
    vbs = sb("vbs", [128, 4, 192], BF16)
    QTA = sb("QTA", [128, 4, 512], BF16)
    QTB = sb("QTB", [70, 8, 512], BF16)
    GT = sb("GT", [128, 8, 512], BF16)
    zs = sb("zs", [128, D])
    kbuf = [sb("kbuf0", [128, 2, 1152], BF16), sb("kbuf1", [128, 2, 1152], BF16)]
    vbuf = [sb("vbuf0", [128, SNT, 192], BF16), sb("vbuf1", [128, SNT, 192], BF16)]
    Pt = [sb("Pt0", [128, 2, 512], BF16), sb("Pt1", [128, 2, 512], BF16)]
    mt = [KA, VAf, KB, VBf]
    mtk = ["KA", "VAf", "KB", "VBf"]

    psS = st.enter_context(nc.psum_tensor("psS", [128, 4, 512], F32))
    ps = [psS[:, i, :] for i in range(4)] + [st.enter_context(nc.psum_tensor("ps%d" % i, [128, 512], F32)) for i in range(4, 8)]

    def psb(i):
        return ps[i][:, :].bitcast(BF16)

    def _sn(stream, ap):
        return stream

    def dma_in(out, in_, wr, stream="ld", rd=()):
        S.add("sp", lambda e, o=out, i=in_: e.dma_start(out=o, in_=i), rd=rd, wr=wr, stream=_sn(stream, out))

    def dma_out(out, in_, rd, wr=(), stream="st"):
        S.add("sp", lambda e, o=out, i=in_: e.dma_start(out=o, in_=i), rd=rd, wr=wr, stream=_sn(stream, in_))

    def A(fn, rd, wr):
        S.add("act", fn, rd, wr)

    def V(fn, rd, wr):
        S.add("dve", fn, rd, wr)

    def G(fn, rd, wr):
        S.add("pool", fn, rd, wr)

    def PE(fn, rd, wr):
        S.add("pe", fn, rd, wr)

    def act(out, in_, func, rd, wr, bias=0.0, scale=1.0, accum=None):
        A(lambda e: e.activation(out=out, in_=in_, func=func, bias=bias, scale=scale, accum_out=accum), rd, wr)

    def mm(out, lhsT, rhs, start, stop, rd, wr):
        PE(lambda e: e.matmul(out, lhsT, rhs, start=start, stop=stop), rd, wr)

    def tr(out, in_, ident, rd, wr):
        PE(lambda e: e.transpose(out, in_, ident), rd, wr)

    ctoks = []
    wtoks = []
    for t_, d_ in ((identb, k_identb), (identf, k_identf), (Um, k_U), (E127, k_E127), (Psw, k_Psw),
                   (W1, k_W1), (sel2, k_sel2), (bsel, k_bsel), (maskA, k_maskA), (maskB, k_maskB), (maskS, k_maskS),
                   (vecs_sb, vecs), (lam_sb, lamv), (bf_sb, bf_b), (subg_sb, subg), (Lt, k_Lt)):
        sl = tuple(slice(None) for _ in range(len(d_.shape)))
        ctoks.append("c_%d" % len(ctoks))
        dma_in(t_[sl], d_[sl], wr=[ctoks[-1]], stream="ldc")
    V(lambda e: e.memset(ones_bf[:, :], 1.0), [], ["c_ones"])
    V(lambda e: e.memset(ones_f[:, :], 1.0), ["c_ones"], ["c_ones"])
    V(lambda e: e.memset(small[:, 13:14], 0.0), ctoks + ["c_ones"], ["const"])
    V(lambda e: e.memset(Ktm[:, :, 64:67], 1.0), [], ["Ktm"])
    V(lambda e: e.memset(Qtm[:, :, 67:70], 1.0), [], ["Qtm"])
    V(lambda e: e.memset(vbs[:, :, 64:128], 1.0), [], ["vbs"])
    V(lambda e: e.memset(TSs[:, :], 0.0), [], ["TSs"])

    V(lambda e: e.tensor_tensor(out=sqj[:, 0:64], in0=lam_sb[:, 0, :], in1=lam_sb[:, 1, :], op=ALU.mult), ["const"], ["sqj"])
    V(lambda e: e.tensor_tensor(out=sqj[:, 64:128], in0=lam_sb[:, 2, :], in1=lam_sb[:, 3, :], op=ALU.mult), ["const"], ["sqj"])
    V(lambda e: e.reduce_sum(out=small[:, 2:4], in_=sqj[:, 0:128].rearrange("p (a b) -> p a b", a=2), axis=AX.X), ["sqj"], ["small"])
    act(small[:, 4:6], small[:, 2:4], AF.Exp, ["small"], ["small2"])
    V(lambda e: e.tensor_tensor(out=small[:, 6:7], in0=small[:, 4:5], in1=small[:, 5:6], op=ALU.subtract), ["small2"], ["small3"])
    V(lambda e: e.tensor_scalar(out=small[:, 1:2], in0=small[:, 6:7], scalar1=LAM_INIT, scalar2=-1.0, op0=ALU.add, op1=ALU.mult), ["small3"], ["neglam"])
    V(lambda e: e.tensor_scalar(out=small[:, 7:8], in0=subg_sb[:, 0:1], scalar1=1.0 - LAM_INIT, scalar2=None, op0=ALU.mult), ["const"], ["subg2"])

    slot = 0
    dma_in(wtail[:, :, :], w_kv[:, 2048:2056].rearrange("(c p) n -> p c n", p=128), wr=["wtail"], stream="ldc")
    V(lambda e: e.tensor_copy(out=Wkv[:, :, 2048:2056], in_=wtail[:, :, :]), ["wtail"], ["w_tail"])
    wtoks.append("w_tail")
    for (Wt, wsrc, ncol) in ((Wkv, w_kv, 2048), (Wqz, w_qz, NQZ), (Wout, w_out, D)):
        for ch in range(8):
            s_ = slot % 2
            slot += 1
            dma_in(wst[s_][:, 0:ncol], wsrc[ch * 128:(ch + 1) * 128, 0:ncol], wr=["wst%d" % s_], stream="ldw")
            wtoks.append("w_%d" % len(wtoks))
            if ch % 2 == 0:
                V(lambda e, Wt=Wt, ch=ch, s_=s_, ncol=ncol: e.tensor_copy(out=Wt[:, ch, 0:ncol], in_=wst[s_][:, 0:ncol]), ["wst%d" % s_], [wtoks[-1]])
            else:
                A(lambda e, Wt=Wt, ch=ch, s_=s_, ncol=ncol: e.copy(out=Wt[:, ch, 0:ncol], in_=wst[s_][:, 0:ncol]), ["wst%d" % s_], [wtoks[-1]])

    dma_in(scT[:, :, :], cvT[:, :, :], wr=["scT"], stream="ldc")
    act(scT2[:, :, :], scT[:, :, :], AF.Exp, ["scT"], ["scT2"], scale=-1.0)
    V(lambda e: e.tensor_scalar(out=scT2[:, :, :], in0=scT2[:, :, :], scalar1=1.0, scalar2=None, op0=ALU.add), ["scT2"], ["scT2"])
    V(lambda e: e.reciprocal(out=scT2[:, :, :], in_=scT2[:, :, :]), ["scT2"], ["scT2"])
    V(lambda e: e.tensor_tensor(out=scT[:, :, :], in0=scT[:, :, :], in1=scT2[:, :, :], op=ALU.mult), ["scT2", "scT"], ["scT"])
    for ch in range(8):
        for half in range(2):
            s_ = slot % 2
            slot += 1
            dma_in(wst[s_][:, 0:1536], w_ada[ch * 128:(ch + 1) * 128, half * 1536:(half + 1) * 1536],
                   wr=["wst%d" % s_], stream="ldw")
            for g in range(3):
                mm(ps[half * 3 + g][0:2, :], scT[:, ch, :], wst[s_][:, g * 512:(g + 1) * 512],
                   ch == 0, ch == 7, ["scT", "wst%d" % s_], ["ps%d" % (half * 3 + g)])
    bst = [xin[0], xin[1], zs]
    for i in range(3):
        dma_in(bst[i][0:2, :], b_ada2[:, i * D:(i + 1) * D], wr=["bst%d" % i], stream="ldc")
    dma_in(sqj[0:2, :], g_row2[:, :], wr=["sqj"], stream="ldc")
    for g in range(6):
        V(lambda e, g=g: e.tensor_tensor(out=mod_sb[:, g * 512:(g + 1) * 512], in0=ps[g][0:2, :],
                                         in1=bst[g // 2][0:2, (g % 2) * 512:(g % 2 + 1) * 512], op=ALU.add),
          ["ps%d" % g, "bst%d" % (g // 2)], ["mod_sb", "wst0", "wst1"])
    V(lambda e: e.scalar_tensor_tensor(out=mod_sb[:, D:2 * D], in0=mod_sb[:, D:2 * D], scalar=1.0, in1=sqj[0:2, :],
                                       op0=ALU.add, op1=ALU.mult), ["mod_sb", "sqj"], ["mod_sb"])
    dma_out(modscr[:, :], mod_sb, rd=["mod_sb"], wr=["modscr"], stream="stm")
    S.add("sp", lambda e: e.dma_start(out=small[:, 11:12], in_=subg[:, 0:1]), ["bst0", "bst1", "bst2", "sqj", "wst0", "wst1", "modscr"] + wtoks,
          ["xin0", "xin1", "zs0", "zs1", "ZT", "W"], stream="ldbr")

    def set_mod(r):
        for i, dst in enumerate((shift_b, gmod_b, gate_b)):
            dma_in(dst[:, :], modscr[r:r + 1, i * D:(i + 1) * D].to_broadcast([128, D]), wr=["modb"], rd=["modscr"], stream="ldm")

    def flat(t3):
        return t3[:, :, :].rearrange("p a b -> p (a b)")
    k0f = flat(kbuf[0]).bitcast(F32)
    k1f = flat(kbuf[1])
    v0f = flat(vbuf[0])
    v1f = flat(vbuf[1])
    gtf = flat(GT).bitcast(F32)
    p0f = flat(Pt[0])
    p1f = flat(Pt[1])
    BUF = [
        dict(xin=xin[0], sqj=sqj[:, :], xbf=xbf[:, :], xT=xT[:, :, :], KA=KA[:, :], VAf=VAf[:, :], KB=KB[:, :], VBf=VBf[:, :],
             kabf=kabf[:, :], KTas=KTas[:, :, :], KTbs=KTbs[:, :, :], vas=vas[:, :, :], vbs=vbs[:, :, :], Ktm=Ktm[:, :, :]),
        dict(xin=xin[1], sqj=k0f[:, 0:1024], xbf=p0f, xT=p1f.rearrange("p (c t) -> p c t", c=8),
             KA=gtf[:, 0:512], VAf=gtf[:, 512:1024], KB=gtf[:, 1024:1536], VBf=gtf[:, 1536:2048],
             kabf=k1f[:, 0:512], KTas=k1f[:, 512:1024].rearrange("p (h t) -> p h t", h=4),
             vas=k1f[:, 1024:1536].rearrange("p (h e) -> p h e", h=4),
             Ktm=k1f[:, 1536:2096].rearrange("p (h d) -> p h d", h=8),
             KTbs=v0f[0:70, 0:1024].rearrange("p (h t) -> p h t", h=8),
             vbs=v1f[:, 0:768].rearrange("p (h e) -> p h e", h=4)),
    ]
    FAM_A = ["kbuf0", "kbuf1", "vbuf0", "vbuf1", "Pt0", "Pt1", "GT"]
    FAM_B = [n_ + "1" for n_ in ("sqj", "xbf", "xT", "KA", "VAf", "KB", "VBf", "kabf", "KTas", "KTbs", "vas", "vbs", "Ktm")]

    def tok(n_, s_):
        return n_ if s_ == 0 else n_ + "1"

    def bridge(to_proj):
        V(lambda e: e.memset(small[:, 12:13], 0.0), [], FAM_A + FAM_B)
        if to_proj:
            V(lambda e: e.memset(BUF[1]["Ktm"][:, :, 64:67], 1.0), [], ["Ktm1"])
            V(lambda e: e.memset(BUF[1]["vbs"][:, :, 64:128], 1.0), [], ["vbs1"])

    def stageL(job):
        if job["kind"] == "cache":
            return
        s_ = job["s"]
        dma_in(BUF[s_]["xin"][0:job["P"], :], job["xsrc"], wr=["xin%d" % s_], stream="ldx")
        job["xloaded"] = True

    def stageF(job):
        s_ = job["s"]
        B = BUF[s_]
        P = job["P"]
        if job["kind"] == "cache":
            r0 = job["r0"]
            dma_in(B["KA"][:, :], c_ak[r0:r0 + 128, :], wr=[tok("KA", s_)], stream="ldx")
            dma_in(B["VAf"][:, :], c_av[r0:r0 + 128, :], wr=[tok("VAf", s_)], stream="ldx")
            dma_in(B["KB"][:, :], c_bk[r0:r0 + 128, :], wr=[tok("KB", s_)], stream="ldx")
            dma_in(B["VBf"][:, :], c_bv[r0:r0 + 128, :], wr=[tok("VBf", s_)], stream="ldx")
            dma_in(logf[:, s_, :], c_lf[r0:r0 + 128, :], wr=[tok("logf", s_)], stream="ldx")
            return
        X = B["xin"]
        tk = "xin%d" % s_
        SQ = B["sqj"]
        if not job.get("xloaded"):
            stageL(job)
        dma_in(rtab[0:P, s_, :], job["rsrc"], wr=[tok("rtab", s_)], stream="ldr")
        act(SQ[0:P, :], X[0:P, :], AF.Square, [tk], [tok("sqj", s_), tok("ssq", s_)], accum=sm2[0:P, s_, 0:1])
        act(sm2[0:P, s_, 1:2], sm2[0:P, s_, 0:1], AF.Ln, [tok("ssq", s_)], [tok("ssq2", s_)], bias=EPS, scale=1.0 / D)
        act(sm2[0:P, s_, 2:3], sm2[0:P, s_, 1:2], AF.Exp, [tok("ssq2", s_)], [tok("rstd", s_)], scale=-0.5)
        V(lambda e: e.scalar_tensor_tensor(out=SQ[0:P, :], in0=X[0:P, :], scalar=sm2[0:P, s_, 2:3], in1=gmod_b[0:P, :],
                                           op0=ALU.mult, op1=ALU.mult), [tk, tok("rstd", s_), "modb", tok("sqj", s_)], [tok("sqj", s_)])
        G(lambda e: e.tensor_tensor(out=B["xbf"][0:P, :], in0=SQ[0:P, :], in1=shift_b[0:P, :], op=ALU.add),
          [tok("sqj", s_), "modb"], [tok("xbf", s_)])

    def stageF2(job):
        if job["kind"] == "cache":
            return
        s_ = job["s"]
        B = BUF[s_]
        P = job["P"]
        xTp = psb(0)
        for ch in range(8):
            tr(xTp[:, ch * 128:ch * 128 + P], B["xbf"][0:P, ch * 128:(ch + 1) * 128], identb[0:P, 0:P], [tok("xbf", s_), "const"], ["ps0"])
        A(lambda e: e.copy(out=B["xT"][:, :, 0:P], in_=xTp.rearrange("p (c t) -> p c t", c=8)[:, :, 0:P]), ["ps0"], [tok("xT", s_)])

    def stageM(job):
        if job["kind"] == "cache":
            return
        s_ = job["s"]
        B = BUF[s_]
        P = job["P"]
        for (Wt, c0, ncol, bank) in job["groups"]:
            for ch in range(8):
                mm(ps[bank][0:P, 0:ncol], B["xT"][:, ch, 0:P], Wt[:, ch, c0:c0 + ncol], ch == 0, ch == 7,
                   [tok("xT", s_), "W"], ["ps%d" % bank])

    def qk_norm(dst, dtok, bank, P, gain_idx, s_):
        pv = ps[bank]
        SQ = BUF[s_]["sqj"]
        act(SQ[0:P, 0:512], pv[0:P, :], AF.Square, ["ps%d" % bank], [tok("sqj", s_)])
        V(lambda e: e.reduce_sum(out=st8[0:P, s_, 0, :], in_=SQ[0:P, 0:512].rearrange("p (g d) -> p g d", g=8), axis=AX.X),
          [tok("sqj", s_)], [tok("st8a", s_)])
        act(st8[0:P, s_, 1, :], st8[0:P, s_, 0, :], AF.Ln, [tok("st8a", s_)], [tok("st8b", s_)], bias=EPS, scale=1.0 / 64)
        act(st8[0:P, s_, 2, :], st8[0:P, s_, 1, :], AF.Exp, [tok("st8b", s_)], [tok("st8c", s_)], scale=-0.5)
        V(lambda e: e.tensor_tensor(out=dst[0:P, :].rearrange("p (g d) -> p g d", g=8),
                                    in0=pv[0:P, :].rearrange("p (g d) -> p g d", g=8),
                                    in1=st8[0:P, s_, 2, :].unsqueeze(2).to_broadcast([P, 8, 64]), op=ALU.mult),
          ["ps%d" % bank, tok("st8c", s_)], [dtok])
        G(lambda e: e.tensor_tensor(out=dst[0:P, :].rearrange("p (g d) -> p g d", g=8),
                                    in0=dst[0:P, :].rearrange("p (g d) -> p g d", g=8),
                                    in1=vecs_sb[0:P, gain_idx, :].unsqueeze(1).to_broadcast([P, 8, 64]), op=ALU.mult),
          [dtok, "const"], [dtok])

    def rope(dst, tk, P, rsrc, s_):
        dv = dst[0:P, :].rearrange("p (g d) -> p g d", g=8)
        x1 = dv[:, :, 0:8]
        x2 = dv[:, :, 8:16]
        cs = rtab[0:P, s_, 0:8].unsqueeze(1).to_broadcast([P, 8, 8])
        sn = rtab[0:P, s_, 8:16].unsqueeze(1).to_broadcast([P, 8, 8])
        rt = tok("rtmp", s_)
        rtk = tok("rtab", s_)
        V(lambda e: e.tensor_tensor(out=rtmp[0:P, s_, 0], in0=x1, in1=cs, op=ALU.mult), [tk, rtk], [rt])
        V(lambda e: e.tensor_tensor(out=rtmp[0:P, s_, 1], in0=x2, in1=sn, op=ALU.mult), [tk, rtk], [rt])
        V(lambda e: e.tensor_tensor(out=rtmp[0:P, s_, 2], in0=x2, in1=cs, op=ALU.mult), [tk, rtk], [rt])
        V(lambda e: e.tensor_tensor(out=rtmp[0:P, s_, 3], in0=x1, in1=sn, op=ALU.mult), [tk, rtk], [rt])
        V(lambda e: e.tensor_tensor(out=x1, in0=rtmp[0:P, s_, 0], in1=rtmp[0:P, s_, 1], op=ALU.subtract), [rt], [tk])
        V(lambda e: e.tensor_tensor(out=x2, in0=rtmp[0:P, s_, 2], in1=rtmp[0:P, s_, 3], op=ALU.add), [rt], [tk])

    def split3(src, P, dsttile, dtok, c0, scale, stok, s_):
        sp_ = spl[:, s_]
        t1, t2 = tok("spl1", s_), tok("spl2", s_)
        hi = dsttile[0:P, :, c0:c0 + 1]
        mid = dsttile[0:P, :, c0 + 1:c0 + 2]
        lo = dsttile[0:P, :, c0 + 2:c0 + 3]
        sv = src.unsqueeze(2)
        V(lambda e: e.tensor_scalar(out=hi, in0=sv, scalar1=scale, scalar2=None, op0=ALU.mult), [stok], [dtok])
        V(lambda e: e.scalar_tensor_tensor(out=sp_[0:P, 1, :].unsqueeze(2), in0=sv, scalar=scale, in1=hi, op0=ALU.mult, op1=ALU.subtract),
          [stok, dtok], [t1])
        V(lambda e: e.tensor_copy(out=mid, in_=sp_[0:P, 1, :].unsqueeze(2)), [t1], [dtok])
        V(lambda e: e.tensor_tensor(out=sp_[0:P, 2, :].unsqueeze(2), in0=sp_[0:P, 1, :].unsqueeze(2), in1=mid, op=ALU.subtract), [t1, dtok], [t2])
        V(lambda e: e.tensor_copy(out=lo, in_=sp_[0:P, 2, :].unsqueeze(2)), [t2], [dtok])

    def pack_kv_cum(kt, P, first, s_):
        B = BUF[s_]
        lg = logf[:, s_, :]
        cprev = cum[(kt + 1) % 2]
        ccur = cum[kt % 2]
        mm(ps[5][0:P, 32:40], Um[0:P, 0:P], lg[0:P, :], True, first, ["const", tok("logf", s_)], ["ps5"])
        if not first:
            mm(ps[5][0:P, 32:40], E127[:, 0:P], cprev[:, :], False, True, ["const", "cum%d" % ((kt + 1) % 2)], ["ps5"])
        V(lambda e: e.tensor_copy(out=ccur[0:P, :], in_=ps[5][0:P, 32:40]), ["ps5"], ["cum%d" % (kt % 2)])
        split3(ccur[0:P, :], P, B["Ktm"], tok("Ktm", s_), 67, -8.0, "cum%d" % (kt % 2), s_)

    def pack_kv_cast(P, s_):
        B = BUF[s_]
        A(lambda e: e.copy(out=B["kabf"][0:P, :], in_=B["KA"][0:P, :]), [tok("KA", s_)], [tok("kabf", s_)])
        A(lambda e: e.copy(out=B["vas"][0:P, :, :], in_=B["VAf"][0:P, :].rearrange("p (h e) -> p h e", h=4)),
          [tok("VAf", s_)], [tok("vas", s_)])
        A(lambda e: e.copy(out=B["Ktm"][0:P, :, 0:64], in_=B["KB"][0:P, :].rearrange("p (h d) -> p h d", h=8)),
          [tok("KB", s_)], [tok("Ktm", s_)])
        vv = B["VBf"][0:P, :].rearrange("p (q a d) -> p q a d", q=4, a=2)
        A(lambda e: e.copy(out=B["vbs"][0:P, :, 0:64], in_=vv[:, :, 0, :]), [tok("VBf", s_)], [tok("vbs", s_)])
        A(lambda e: e.copy(out=B["vbs"][0:P, :, 128:192], in_=vv[:, :, 1, :]), [tok("VBf", s_)], [tok("vbs", s_)])

    def pack_kv_b(SC, kt, P, first, s_):
        pre = SC["pre"]
        B = BUF[s_]
        kp = psb(6)
        for h in range(4):
            tr(kp[:, h * 128:h * 128 + P], B["kabf"][0:P, h * 128:(h + 1) * 128], identb[0:P, 0:P], [tok("kabf", s_), "const"], ["ps6"])
        V(lambda e: e.tensor_copy(out=B["KTas"][:, :, 0:P], in_=kp[:, 0:512].rearrange("p (h t) -> p h t", h=4)[:, :, 0:P]),
          ["ps6"], [tok("KTas", s_)])
        dma_out(SC["KTA"][:, :, kt * 128:kt * 128 + P].rearrange("h p t -> p h t"), B["KTas"][:, :, 0:P], rd=[tok("KTas", s_)],
                wr=[pre + "KTA%d" % kt], stream="stk")
        dma_out(SC["VA"][:, 0:P, kt, :].rearrange("h p e -> p h e"), B["vas"][0:P, :, :], rd=[tok("vas", s_)], wr=[pre + "VA%d" % kt], stream="stk")
        kp2 = psb(7)
        for h in range(8):
            tr(kp2[0:70, h * 128:h * 128 + P], B["Ktm"][0:P, h, :], identb[0:P, 0:P], [tok("Ktm", s_), "const"], ["ps7"])
        V(lambda e: e.tensor_copy(out=B["KTbs"][:, :, 0:P], in_=kp2[0:70, :].rearrange("p (h t) -> p h t", h=8)[:, :, 0:P]),
          ["ps7"], [tok("KTbs", s_)])
        dma_out(SC["KTB"][:, :, kt * 128:kt * 128 + P].rearrange("h p t -> p h t"), B["KTbs"][:, :, 0:P], rd=[tok("KTbs", s_)],
                wr=[pre + "KTB%d" % kt], stream="stk")
        dma_out(SC["VB"][:, 0:P, kt, :].rearrange("h p e -> p h e"), B["vbs"][0:P, :, :], rd=[tok("vbs", s_)], wr=[pre + "VB%d" % kt], stream="stk")

    G_KV = [(Wkv, 0, 512, 1), (Wkv, 512, 512, 2), (Wkv, 1024, 512, 3), (Wkv, 1536, 512, 4), (Wkv, 2048, 8, 5)]
    G_QZ = [(Wqz, 0, 512, 1), (Wqz, 512, 512, 2), (Wqz, 1024, 512, 3), (Wqz, 1536, 512, 4)]

    def stageP1(job):
        s_ = job["s"]
        B = BUF[s_]
        P = job["P"]
        if job["kind"] == "cache":
            return
        if job["kind"] == "kv":
            qk_norm(B["KA"], tok("KA", s_), 1, P, 1, s_)
            A(lambda e: e.copy(out=B["VAf"][0:P, :], in_=ps[2][0:P, :]), ["ps2"], [tok("VAf", s_)])
            qk_norm(B["KB"], tok("KB", s_), 3, P, 3, s_)
            A(lambda e: e.copy(out=B["VBf"][0:P, :], in_=ps[4][0:P, :]), ["ps4"], [tok("VBf", s_)])
            V(lambda e: e.tensor_tensor(out=st8[0:P, s_, 3, :], in0=ps[5][0:P, 0:8], in1=bf_sb[0:P, :], op=ALU.add),
              ["ps5", "const"], [tok("st8d", s_)])
        else:
            qk_norm(B["KA"], tok("KA", s_), 1, P, 0, s_)
            qk_norm(B["KB"], tok("KB", s_), 3, P, 2, s_)
            for half, bank in ((0, 2), (1, 4)):
                zc = zs[0:P, half * 512:(half + 1) * 512]
                act(zc, ps[bank][0:P, :], AF.Exp, ["ps%d" % bank], ["zs%d" % half], scale=-1.0)
                act(zc, zc, AF.Ln, ["zs%d" % half], ["zs%d" % half], bias=1.0)
                act(zc, zc, AF.Exp, ["zs%d" % half], ["zs%d" % half], scale=-1.0)
                V(lambda e, zc=zc, bank=bank: e.tensor_tensor(out=zc, in0=ps[bank][0:P, :], in1=zc, op=ALU.mult),
                  ["zs%d" % half, "ps%d" % bank], ["zs%d" % half])

    def stageP2a(job):
        s_ = job["s"]
        B = BUF[s_]
        P = job["P"]
        lg = logf[:, s_, :]
        if job["kind"] in ("kv", "cache"):
            if job["kind"] == "kv":
                act(st8[0:P, s_, 4, :], st8[0:P, s_, 3, :], AF.Exp, [tok("st8d", s_)], [tok("st8e", s_)], scale=-1.0)
                act(st8[0:P, s_, 5, :], st8[0:P, s_, 4, :], AF.Ln, [tok("st8e", s_)], [tok("st8f", s_)], bias=1.0)
                V(lambda e: e.tensor_scalar(out=lg[0:P, :], in0=st8[0:P, s_, 5, :], scalar1=-1.0, scalar2=None, op0=ALU.mult),
                  [tok("st8f", s_)], [tok("logf", s_)])
                rope(B["KA"], tok("KA", s_), P, job["rsrc"], s_)
            if job.get("SC") is not None:
                pack_kv_cast(P, s_)
            if job.get("outs") is not None:
                dak, dav, dbk, dbv, dlf = job["outs"]
                row0 = job["row0"]
                dma_out(dak[row0:row0 + P, :], B["KA"][0:P, :], rd=[tok("KA", s_)], stream="sto")
                dma_out(dav[row0:row0 + P, :], B["VAf"][0:P, :], rd=[tok("VAf", s_)], stream="sto")
                dma_out(dbk[row0:row0 + P, :], B["KB"][0:P, :], rd=[tok("KB", s_)], stream="sto")
                dma_out(dbv[row0:row0 + P, :], B["VBf"][0:P, :], rd=[tok("VBf", s_)], stream="sto")
                dma_out(dlf[row0:row0 + P, :], lg[0:P, :], rd=[tok("logf", s_)], stream="sto")
        else:
            rope(B["KA"], tok("KA", s_), P, job["rsrc"], s_)
            A(lambda e: e.copy(out=B["kabf"][0:P, :], in_=B["KA"][0:P, :]), [tok("KA", s_)], [tok("kabf", s_)])
            G(lambda e: e.tensor_copy(out=Qtm[0:P, :, 0:64], in_=B["KB"][0:P, :].rearrange("p (h d) -> p h d", h=8)), [tok("KB", s_)], ["Qtm"])
            split3(cumq_sb[0:P, :], P, Qtm, "Qtm", 64, 8.0, "cumq", s_)

    def stageP2c(job):
        s_ = job["s"]
        B = BUF[s_]
        P = job["P"]
        lg = logf[:, s_, :]
        if job["kind"] in ("kv", "cache"):
            if job.get("ts_kt") is not None:
                kt = job["ts_kt"]
                mm(ps[5][:, 64:72], W1[:, 127 - kt:255 - kt], lg[:, :], True, True, ["const", tok("logf", s_)], ["ps5"])
                V(lambda e: e.tensor_tensor(out=TSs[:, :], in0=TSs[:, :], in1=ps[5][:, 64:72], op=ALU.add), ["ps5", "TSs"], ["TSs"])
            if job.get("SC") is not None:
                pack_kv_cum(job["kt"], P, job["first"], s_)
            if job.get("cumq") == "own":
                t = job["t"]
                mm(ps[5][:, 96:104], Um[:, :], lg[:, :], True, False, ["const", tok("logf", s_)], ["ps5"])
                V(lambda e, t=t: e.tensor_scalar(out=TSm[:, :], in0=TSs[:, :], scalar1=Lt[:, t:t + 1], scalar2=None, op0=ALU.mult),
                  ["TSs", "const"], ["TSm"])
                mm(ps[5][:, 96:104], ones_f[:, :], TSm[:, :], False, True, ["const", "TSm"], ["ps5"])
                V(lambda e: e.tensor_copy(out=cumq_sb[:, :], in_=ps[5][:, 96:104]), ["ps5"], ["cumq"])
            elif job.get("cumq") == "sample":
                V(lambda e: e.tensor_copy(out=cumq_sb[0:P, :], in_=cum[job["kt"] % 2][0:P, :]), ["cum%d" % (job["kt"] % 2)], ["cumq"])
        else:
            col0 = job["col0"]
            for half in range(2):
                for q in range(4):
                    ci = half * 4 + q
                    tr(ps[0][:, q * 128:q * 128 + P], zs[0:P, ci * 128:(ci + 1) * 128], identf[0:P, 0:P], ["zs%d" % half, "const"], ["ps0"])
                V(lambda e, half=half: e.tensor_copy(out=ZT[:, half * 4:(half + 1) * 4, col0:col0 + P],
                                                     in_=ps[0][:, :].rearrange("p (q t) -> p q t", q=4)[:, :, 0:P]), ["ps0"], ["ZT"])

    def stageP2b(job):
        s_ = job["s"]
        B = BUF[s_]
        P = job["P"]
        if job["kind"] in ("kv", "cache"):
            if job.get("SC") is not None:
                pack_kv_b(job["SC"], job["kt"], P, job["first"], s_)
        else:
            col0 = job["col0"]
            kp = psb(6)
            for h in range(4):
                tr(kp[:, h * 128:h * 128 + P], B["kabf"][0:P, h * 128:(h + 1) * 128], identb[0:P, 0:P], [tok("kabf", s_), "const"], ["ps6"])
            V(lambda e: e.tensor_copy(out=QTA[:, :, col0:col0 + P], in_=kp[:, 0:512].rearrange("p (h t) -> p h t", h=4)[:, :, 0:P]),
              ["ps6"], ["QTA"])
            kp2 = psb(7)
            for h in range(8):
                tr(kp2[0:70, h * 128:h * 128 + P], Qtm[0:P, h, :], identb[0:P, 0:P], ["Qtm", "const"], ["ps7"])
            V(lambda e: e.tensor_copy(out=QTB[:, :, col0:col0 + P], in_=kp2[0:70, :].rearrange("p (h t) -> p h t", h=8)[:, :, 0:P]),
              ["ps7"], ["QTB"])

    jslot = [0]
    xslot = [0]

    def run_jobs(jobs):
        n = len(jobs)
        for jb in jobs:
            jb["s"] = jslot[0] % 2
            jslot[0] += 1
        stageL(jobs[0])
        if n > 1:
            stageL(jobs[1])
        stageF(jobs[0])
        stageF2(jobs[0])
        stageM(jobs[0])
        for k in range(n):
            if k + 2 < n:
                stageL(jobs[k + 2])
            if k + 1 < n:
                stageF(jobs[k + 1])
            stageP1(jobs[k])
            if k + 1 < n:
                stageF2(jobs[k + 1])
            if k >= 1:
                stageP2c(jobs[k - 1])
            if k >= 2:
                stageP2b(jobs[k - 2])
            stageP2a(jobs[k])
            if k + 1 < n:
                stageM(jobs[k + 1])
        stageP2c(jobs[n - 1])
        if n >= 2:
            stageP2b(jobs[n - 2])
        stageP2b(jobs[n - 1])

    abuf = [0]

    def attention(SC, chunks, N_full, unit):
        pre = SC["pre"]
        isA = unit < 4
        h = unit if isA else unit - 4
        its = []
        for ci, (kt0, nkt, c0, zone, lastP) in enumerate(chunks):
            b = abuf[0] % 2
            abuf[0] += 1
            for r in range(nkt):
                its.append(dict(b=b, r=r, KP=(lastP if r == nkt - 1 else 128), c0=c0,
                                mk=(zone[r] if zone is not None else None),
                                load=((kt0, nkt, lastP) if r == 0 else None)))
        n = len(its)

        def stageA(it, idx):
            b = it["b"]
            kb_, vb_ = kbuf[b], vbuf[b]
            if it["load"] is not None:
                kt0, nkt, lastP = it["load"]
                ncols = (nkt - 1) * 128 + lastP
                if isA:
                    dma_in(kb_[:, 0, 0:ncols], SC["KTA"][h, :, kt0 * 128:kt0 * 128 + ncols],
                           rd=[pre + "KTA%d" % k for k in range(kt0, kt0 + nkt)], wr=["kbuf%d" % b], stream="ldk")
                    nfull = nkt if lastP == 128 else nkt - 1
                    dma_in(vb_[:, 0:nfull, 0:128], SC["VA"][h, :, kt0:kt0 + nfull, :],
                           rd=[pre + "VA%d" % k for k in range(kt0, kt0 + nkt)], wr=["vbuf%d" % b], stream="ldk")
                    if nfull < nkt:
                        dma_in(vb_[0:lastP, nfull:nkt, 0:128], SC["VA"][h, 0:lastP, kt0 + nfull:kt0 + nkt, :],
                               rd=[pre + "VA%d" % k for k in range(kt0, kt0 + nkt)], wr=["vbuf%d" % b], stream="ldk")
                else:
                    dma_in(kb_[0:70, :, 0:ncols], SC["KTB"][2 * h:2 * h + 2, :, kt0 * 128:kt0 * 128 + ncols].rearrange("a p t -> p a t"),
                           rd=[pre + "KTB%d" % k for k in range(kt0, kt0 + nkt)], wr=["kbuf%d" % b], stream="ldk")
                    nfull = nkt if lastP == 128 else nkt - 1
                    dma_in(vb_[:, 0:nfull, :], SC["VB"][h, :, kt0:kt0 + nfull, :],
                           rd=[pre + "VB%d" % k for k in range(kt0, kt0 + nkt)], wr=["vbuf%d" % b], stream="ldk")
                    if nfull < nkt:
                        dma_in(vb_[0:lastP, nfull:nkt, :], SC["VB"][h, 0:lastP, kt0 + nfull:kt0 + nkt, :],
                               rd=[pre + "VB%d" % k for k in range(kt0, kt0 + nkt)], wr=["vbuf%d" % b], stream="ldk")
            r, KP, c0 = it["r"], it["KP"], it["c0"]
            N = N_full - c0
            sb_ = abuf[0] % 2
            abuf[0] += 1
            it["sb"] = sb_
            P_ = Pt[sb_]
            for u in range(2):
                sbank = sb_ * 2 + u
                if isA:
                    lhsT = kb_[u * 64:(u + 1) * 64, 0, r * 128:r * 128 + KP]
                    rhs = QTA[u * 64:(u + 1) * 64, h, c0:N_full]
                else:
                    lhsT = kb_[0:70, u, r * 128:r * 128 + KP]
                    rhs = QTB[0:70, 2 * h + u, c0:N_full]
                mm(ps[sbank][0:KP, 0:N], lhsT, rhs, True, True, ["kbuf%d" % b, "QTA" if isA else "QTB"], ["ps%d" % sbank])
            mk = it["mk"]
            if mk is not None:
                mw = mk.shape[-1]
                for u in range(2):
                    sbank = sb_ * 2 + u
                    V(lambda e, sbank=sbank, KP=KP, mk=mk, mw=mw: e.tensor_tensor(
                        out=ps[sbank][0:KP, 0:mw], in0=ps[sbank][0:KP, 0:mw], in1=mk, op=ALU.add),
                      ["ps%d" % sbank, "const"], ["ps%d" % sbank])
            act(P_[0:KP, :, 0:N], psS[0:KP, 2 * sb_:2 * sb_ + 2, 0:N], AF.Exp, ["ps%d" % (2 * sb_), "ps%d" % (2 * sb_ + 1)],
                ["Pt%d" % sb_], scale=0.125)

        def stageB(it, first, last):
            b, r, KP, c0, sb_ = it["b"], it["r"], it["KP"], it["c0"], it["sb"]
            vb_ = vbuf[b]
            P_ = Pt[sb_]
            N = N_full - c0
            for u in range(2):
                if isA:
                    vl = vb_[0:KP, r, 0:128]
                else:
                    vl = vb_[0:KP, r, u * 64:u * 64 + 128]
                mm(ps[4 + u][:, c0:N_full], vl, P_[0:KP, u, 0:N], first, last, ["vbuf%d" % b, "Pt%d" % sb_], ["ps%d" % (4 + u)])
            if isA:
                for u in range(2):
                    PE(lambda e, u=u, KP=KP, P_=P_, N=N, c0=c0: e.matmul(
                        ps[6 + u][32 * u:32 * u + 32, c0:N_full], ones_bf[0:KP, 0:32], P_[0:KP, u, 0:N],
                        start=first, stop=last, tile_position=(0, 32 * u)),
                       ["const", "Pt%d" % sb_], ["ps%d" % (6 + u)])

        for idx in range(n + 1):
            if idx < n:
                stageA(its[idx], idx)
            if idx >= 1:
                stageB(its[idx - 1], idx - 1 == 0, idx - 1 == n - 1)

    def merge_unit(unit, N):
        if unit < 4:
            V(lambda e: e.tensor_copy(out=Lsb[0:32, 0:N], in_=ps[6][0:32, 0:N]), ["ps6"], ["Lsb"])
            V(lambda e: e.tensor_copy(out=Lsb[32:64, 0:N], in_=ps[7][32:64, 0:N]), ["ps7"], ["Lsb"])
            mm(ps[0][:, 0:N], bsel[0:32, :], Lsb[0:32, 0:N], True, True, ["const", "Lsb"], ["ps0"])
            mm(ps[1][:, 0:N], bsel[32:64, :], Lsb[32:64, 0:N], True, True, ["const", "Lsb"], ["ps1"])
            act(mt[0][:, 0:N], ps[0][:, 0:N], AF.Ln, ["ps0"], ["KA"])
            act(mt[0][:, 0:N], mt[0][:, 0:N], AF.Exp, ["KA"], ["KA"], scale=-1.0)
            act(mt[1][:, 0:N], ps[1][:, 0:N], AF.Ln, ["ps1"], ["VAf"])
            act(mt[1][:, 0:N], mt[1][:, 0:N], AF.Exp, ["VAf"], ["VAf"], scale=-1.0)
            V(lambda e: e.tensor_tensor(out=mt[0][:, 0:N], in0=ps[4][:, 0:N], in1=mt[0][:, 0:N], op=ALU.mult), ["ps4", "KA"], ["KA"])
            V(lambda e: e.tensor_tensor(out=mt[1][:, 0:N], in0=ps[5][:, 0:N], in1=mt[1][:, 0:N], op=ALU.mult), ["ps5", "VAf"], ["VAf"])
            V(lambda e: e.scalar_tensor_tensor(out=mt[2][:, 0:N], in0=mt[1][:, 0:N], scalar=small[:, 1:2], in1=mt[0][:, 0:N],
                                               op0=ALU.mult, op1=ALU.add), ["KA", "VAf", "neglam"], ["KB"])
            act(mt[3][:, 0:N], mt[2][:, 0:N], AF.Square, ["KB"], ["VBf"])
            mm(ps[0][:, 0:N], ones_f[:, :], mt[3][:, 0:N], True, True, ["const", "VBf"], ["ps0"])
            act(mt[0][:, 0:N], ps[0][:, 0:N], AF.Ln, ["ps0"], ["KA"], bias=EPS, scale=1.0 / 128)
            act(mt[1][:, 0:N], mt[0][:, 0:N], AF.Exp, ["KA"], ["VAf"], scale=-0.5)
            V(lambda e: e.scalar_tensor_tensor(out=mt[2][:, 0:N], in0=mt[2][:, 0:N], scalar=small[:, 7:8], in1=mt[1][:, 0:N],
                                               op0=ALU.mult, op1=ALU.mult), ["KB", "VAf", "subg2"], ["KB"])
            V(lambda e: e.tensor_tensor(out=GT[:, unit, 0:N], in0=mt[2][:, 0:N], in1=ZT[:, unit, 0:N], op=ALU.mult),
              ["KB", "ZT"], ["GT"])
        else:
            V(lambda e: e.tensor_copy(out=mt[0][0:64, 0:N], in_=ps[5][0:64, 0:N]), ["ps5"], ["KA"])
            V(lambda e: e.tensor_copy(out=mt[0][64:128, 0:N], in_=ps[4][64:128, 0:N]), ["ps4"], ["KA"])
            mm(ps[0][:, 0:N], Psw[:, :], mt[0][:, 0:N], True, True, ["const", "KA"], ["ps0"])
            act(mt[1][:, 0:N], ps[0][:, 0:N], AF.Ln, ["ps0"], ["VAf"])
            act(mt[1][:, 0:N], mt[1][:, 0:N], AF.Exp, ["VAf"], ["VAf"], scale=-1.0)
            V(lambda e: e.tensor_tensor(out=mt[2][0:64, 0:N], in0=ps[4][0:64, 0:N], in1=mt[1][0:64, 0:N], op=ALU.mult), ["ps4", "VAf"], ["KB"])
            V(lambda e: e.tensor_tensor(out=mt[2][64:128, 0:N], in0=ps[5][64:128, 0:N], in1=mt[1][64:128, 0:N], op=ALU.mult), ["ps5", "VAf"], ["KB"])
            V(lambda e: e.tensor_tensor(out=GT[:, unit, 0:N], in0=mt[2][:, 0:N], in1=ZT[:, unit, 0:N], op=ALU.mult),
              ["KB", "ZT"], ["GT"])

    def out_proj(xsrc, ydst, P, col0):
        s_ = xslot[0] % 2
        xslot[0] += 1
        X = xin[s_]
        tk = "xin%d" % s_
        dma_in(X[0:P, :], xsrc, wr=[tk], stream="ldx")
        for half in range(2):
            bank = 1 + half
            for ch in range(8):
                mm(ps[bank][0:P, :], GT[:, ch, col0:col0 + P], Wout[:, ch, half * 512:(half + 1) * 512], ch == 0, ch == 7,
                   ["GT", "W"], ["ps%d" % bank])
            V(lambda e, half=half, bank=bank: e.tensor_tensor(out=zs[0:P, half * 512:(half + 1) * 512], in0=ps[bank][0:P, :],
                                                              in1=gate_b[0:P, half * 512:(half + 1) * 512], op=ALU.mult),
              ["ps%d" % bank, "modb"], ["zs%d" % half])
            V(lambda e, half=half: e.tensor_tensor(out=zs[0:P, half * 512:(half + 1) * 512], in0=zs[0:P, half * 512:(half + 1) * 512],
                                                   in1=X[0:P, half * 512:(half + 1) * 512], op=ALU.add),
              ["zs%d" % half, tk], ["zs%d" % half])
        dma_out(ydst, zs[0:P, :], rd=["zs0", "zs1"], wr=["yout"], stream="sto")

    bridge(True)
    set_mod(1)
    jobs = [dict(kind="cache", P=128, r0=kt * 128, SC=SCS, kt=kt, first=(kt == 0)) for kt in range(8)]
    jobs.append(dict(kind="kv", P=TDEC, xsrc=x_smp[:, :], groups=G_KV, rsrc=rope_smp[:, :], SC=SCS, kt=8, first=False,
                     outs=(s_ak, s_av, s_bk, s_bv, s_lf), row0=0, cumq="sample"))
    jobs.append(dict(kind="qz", P=TDEC, xsrc=x_smp[:, :], groups=G_QZ, rsrc=rope_smp[:, :], col0=0))
    run_jobs(jobs)
    bridge(False)
    zoneS_A = [None] * 9
    zoneS_B = [None] * 8 + [maskS[:, :]]
    for unit in range(8):
        attention(SCS, [(0, 9, 0, zoneS_A if unit < 4 else zoneS_B, TDEC)], TDEC, unit)
        merge_unit(unit, TDEC)
    out_proj(x_smp[:, :], o_ys[:, :], TDEC, 0)
    bridge(True)

    set_mod(0)
    run_jobs([dict(kind="kv", P=128, xsrc=x_all[kt * 128:(kt + 1) * 128, :], groups=G_KV, rsrc=rope_all[kt, :, :],
                   SC=SCP, kt=kt, first=(kt == 0), ts_kt=kt) for kt in range(NT)])

    for j in range(4):
        jobs = []
        for m in range(4):
            t = 4 * j + m
            rows = slice(t * 128, (t + 1) * 128)
            jobs.append(dict(kind="kv", P=128, xsrc=x_own[rows, :], groups=G_KV, rsrc=rope_own[t, :, :],
                             outs=(o_ak, o_av, o_bk, o_bv, o_lf), row0=t * 128, cumq="own", t=t))
            jobs.append(dict(kind="qz", P=128, xsrc=x_own[rows, :], groups=G_QZ, rsrc=rope_own[t, :, :], col0=m * 128))
        run_jobs(jobs)
        bridge(False)
        for unit in range(8):
            chunks = [(8 * ch, 8, 0, None, 128) for ch in range(4 * j)]
            mtab = maskA if unit < 4 else maskB
            for m in range(4):
                chunks.append((8 * (4 * j + m), 8, m * 128, [mtab[:, r, :] for r in range(8)], 128))
            attention(SCP, chunks, 512, unit)
            merge_unit(unit, 512)
        for m in range(4):
            t = 4 * j + m
            out_proj(x_own[t * 128:(t + 1) * 128, :], o_y[t * 128:(t + 1) * 128, :], 128, m * 128)
        bridge(True)

    S.emit(nc, st)
    st.close()
    return nc


_NC_CACHE = {}


def _rope_table(pos):
    half = 8
    inv = (np.float32(500000.0) ** (-np.arange(0, 16, 2, dtype=np.float32) / np.float32(16))).astype(np.float32)
    ang = pos.astype(np.float32)[:, None] * inv[None, :]
    return np.concatenate([np.cos(ang), np.sin(ang)], axis=1).astype(np.float32)


def kernel(x_prompt, x_sample, cache_a_k, cache_a_v, cache_b_k, cache_b_v, cache_b_logf,
           c_prompt, c_sample, norm_g, w_ada, b_ada, w_in, b_f, qn_a, kn_a,
           lam_q1, lam_k1, lam_q2, lam_k2, subln_g, qn_b, kn_b, w_out):
    f = lambda a: np.ascontiguousarray(np.asarray(a, dtype=np.float32))
    xp = f(x_prompt)[0]
    xs = f(x_sample)
    w_in0 = f(w_in)[0]
    o = np.cumsum([0, 512, 512, 512, 512, 512, 512, 512, 8, 512])
    seg = lambda i: w_in0[:, o[i]:o[i + 1]]
    w_kv = np.ascontiguousarray(np.concatenate([seg(1), seg(2), seg(5), seg(6), seg(7)], axis=1))
    w_qz = np.ascontiguousarray(np.concatenate([seg(0), seg(3), seg(4), seg(8)], axis=1))
    rep = lambda v, n=128: np.ascontiguousarray(np.broadcast_to(f(v).reshape(1, -1), (n, f(v).size)))
    vecs = np.ascontiguousarray(np.stack([rep(qn_a[0]), rep(kn_a[0]), rep(qn_b[0]), rep(kn_b[0])], axis=1))
    lamv = np.ascontiguousarray(np.stack([rep(lam_q1[0]), rep(lam_k1[0]), rep(lam_q2[0]), rep(lam_k2[0])], axis=1))
    bf_b = rep(b_f[0])
    subg = f(subln_g)[0].reshape(128, 1).copy()
    bf16 = ml_dtypes.bfloat16
    eye = np.eye(128, dtype=np.float32)
    ar = np.arange(128)
    Umat = (ar[:, None] <= ar[None, :]).astype(np.float32)
    E127 = np.zeros((128, 128), np.float32); E127[127, :] = 1.0
    Psw = np.zeros((128, 128), np.float32); Psw[ar, (ar + 64) % 128] = 1.0
    W1 = np.zeros((128, 256), np.float32); W1[:, 127] = 1.0
    sel2 = np.zeros((2, 2, 128), np.float32); sel2[0, 0, :] = 1.0; sel2[1, 1, :] = 1.0
    bsel = np.zeros((64, 128), np.float32); bsel[0, :] = 1.0; bsel[32, :] = 1.0
    tril = (ar[:, None] <= ar[None, :])
    chunk_ok = ((ar[:, None] // 64) <= (ar[None, :] // 64))
    maskS = (((np.arange(64)[:, None] <= np.arange(64)[None, :]).astype(np.float32) - 1.0) * 30000.0).astype(bf16)
    rope_all = _rope_table(np.arange(SEQ)).reshape(NT, 128, 16)
    rope_smp = _rope_table(PAST + np.arange(TDEC))
    xt = xp.reshape(NT, 128, D)
    in_maps = []
    for c in range(8):
        own = np.arange(NOWN) * 8 + c
        Lt = np.zeros((128, NOWN), np.float32)
        mA = np.zeros((128, 8, 128), np.float32)
        mB = np.zeros((128, 8, 128), np.float32)
        for t in range(NOWN):
            Lt[:8 * t + c, t] = 1.0
        for r in range(8):
            if r < c:
                mA[:, r, :] = 1.0; mB[:, r, :] = 1.0
            elif r == c:
                mA[:, r, :] = chunk_ok; mB[:, r, :] = tril
        cvT = np.stack([f(c_prompt)[0], f(c_sample)[c]], axis=1).reshape(8, 128, 2).transpose(1, 0, 2)
        in_maps.append(dict(
            x_all=xp, x_own=np.ascontiguousarray(xt[own].reshape(NOWN * 128, D)), x_smp=np.ascontiguousarray(xs[c]),
            cvT=np.ascontiguousarray(cvT), w_ada=f(w_ada)[0], b_ada2=np.ascontiguousarray(np.broadcast_to(f(b_ada)[0][None], (2, 3 * D))),
            g_row2=np.ascontiguousarray(np.broadcast_to(f(norm_g)[0][None], (2, D))),
            w_kv=w_kv, w_qz=w_qz, w_out=f(w_out)[0], vecs=vecs, lamv=lamv, bf_b=bf_b, subg=subg,
            c_ak=np.ascontiguousarray(f(cache_a_k)[0, c].reshape(PAST, 512)),
            c_av=np.ascontiguousarray(f(cache_a_v)[0, c].reshape(PAST, 512)),
            c_bk=np.ascontiguousarray(f(cache_b_k)[0, c].reshape(PAST, 512)),
            c_bv=np.ascontiguousarray(f(cache_b_v)[0, c].reshape(PAST, 512)),
            c_lf=np.ascontiguousarray(f(cache_b_logf)[0, c].reshape(PAST, 8)),
            rope_all=rope_all, rope_own=np.ascontiguousarray(rope_all[own]), rope_smp=rope_smp,
            k_identb=eye.astype(bf16), k_identf=eye, k_U=Umat, k_E127=E127, k_Psw=Psw, k_W1=W1, k_Lt=Lt, k_sel2=sel2, k_bsel=bsel,
            k_maskA=((mA - 1.0) * 30000.0).astype(bf16), k_maskB=((mB - 1.0) * 30000.0).astype(bf16), k_maskS=maskS,
        ))
    if "nc" not in _NC_CACHE:
        _NC_CACHE["nc"] = build_nc()
    res = run_bass_kernel_spmd(_NC_CACHE["nc"], in_maps, core_ids=list(range(8)))
    R = res.results

    def gather_p(key, width):
        out = np.zeros((NT, 128, width), np.float32)
        for c in range(8):
            out[np.arange(NOWN) * 8 + c] = np.asarray(R[c][key], dtype=np.float32).reshape(NOWN, 128, width)
        return out.reshape(SEQ, width)

    def gather_s(key, width):
        return np.stack([np.asarray(R[c][key], dtype=np.float32).reshape(TDEC, width) for c in range(8)], axis=0)

    y_p = gather_p("o_y", D).reshape(1, SEQ, D)
    y_s = gather_s("o_ys", D)
    return (y_p, y_s,
            gather_p("o_ak", 512).reshape(1, 1, SEQ, 4, 2, 64), gather_p("o_av", 512).reshape(1, 1, SEQ, 4, 128),
            gather_p("o_bk", 512).reshape(1, 1, SEQ, 8, 64), gather_p("o_bv", 512).reshape(1, 1, SEQ, 8, 64),
            gather_p("o_lf", 8).reshape(1, 1, SEQ, 8),
            gather_s("s_ak", 512).reshape(1, 8, TDEC, 4, 2, 64), gather_s("s_av", 512).reshape(1, 8, TDEC, 4, 128),
            gather_s("s_bk", 512).reshape(1, 8, TDEC, 8, 64), gather_s("s_bv", 512).reshape(1, 8, TDEC, 8, 64),
            gather_s("s_lf", 8).reshape(1, 8, TDEC, 8))
```

```python
import math
from contextlib import ExitStack

import numpy as np
import ml_dtypes
import concourse.bass as bass
import concourse.mybir as mybir
from concourse.bass_utils import run_bass_kernel_spmd

F32 = mybir.dt.float32
BF16 = mybir.dt.bfloat16
AF = mybir.ActivationFunctionType
ALU = mybir.AluOpType
AX = mybir.AxisListType

D = 1024
SEQ = 16384
NT = SEQ // 128
NOWN = 16
TDEC = 64
PAST = 1024
EPS = 1e-6
NKV = 2056
NQZ = 2048
LAM_INIT = 0.8 - 0.6 * math.exp(-0.3 * 0)
SNT = 9


class Sched:
    ROT = dict(ldc=8, ldw=2, ldx=4, ldr=2, ldm=1, ldk=4, stk=8, sto=8, stm=1, ldbr=1)

    def __init__(self):
        self.ops = []
        self.last_w = {}
        self.readers = {}
        self.nstream = {}

    @staticmethod
    def dom(op):
        return ("dma", op["stream"], op["slot"]) if op["stream"] is not None else op["eng"]

    def add(self, eng, fn, rd=(), wr=(), stream=None):
        idx = len(self.ops)
        deps = {}
        def need(j):
            d = self.dom(self.ops[j])
            if d not in deps or deps[d] < j:
                deps[d] = j
        if getattr(self, "serial", False) and idx > 0:
            need(idx - 1)
        for b in rd:
            if b in self.last_w:
                need(self.last_w[b])
        for b in wr:
            if b in self.last_w:
                need(self.last_w[b])
            for j in self.readers.get(b, {}).values():
                need(j)
        op = dict(eng=eng, fn=fn, deps=deps, stream=stream, sig=False, slot=None)
        if stream is not None:
            i = self.nstream.get(stream, 0)
            self.nstream[stream] = i + 1
            op["slot"] = i % self.ROT[stream]
        self.ops.append(op)
        mydom = self.dom(op)
        for b in wr:
            self.last_w[b] = idx
            self.readers[b] = {}
        for b in rd:
            if b in wr:
                continue
            self.readers.setdefault(b, {})[mydom] = idx

    def emit(self, nc, stack):
        ops = self.ops
        for op in ops:
            for d, j in op["deps"].items():
                if d == "pe" and op["eng"] == "pe" and op["stream"] is None:
                    continue
                ops[j]["sig"] = True
        cnt = {}
        for op in ops:
            d = self.dom(op)
            if op["stream"] is not None:
                cnt[d] = cnt.get(d, 0) + 16
                op["seq"] = cnt[d]
            elif op["sig"]:
                cnt[d] = cnt.get(d, 0) + 1
                op["seq"] = cnt[d]
        sems = {}
        for d in cnt:
            nm = "s_" + (d if isinstance(d, str) else "d_%s_%d" % (d[1], d[2]))
            sems[d] = stack.enter_context(nc.semaphore(nm))
        self.final = dict(cnt)
        engs = ["pe", "act", "dve", "pool", "sp"]
        per_eng = {e: [op for op in ops if op["eng"] == e] for e in engs}
        block = stack.enter_context(nc.Block())
        final = {d: v for d, v in cnt.items() if not isinstance(d, str)}

        def run(e, eobj):
            waited = {}
            for op in per_eng[e]:
                for d, j in op["deps"].items():
                    if d == "pe" and e == "pe" and op["stream"] is None:
                        continue
                    v = ops[j]["seq"]
                    if waited.get(d, 0) >= v:
                        continue
                    eobj.wait_ge(sems[d], v)
                    waited[d] = v
                if op["stream"] is not None and op["seq"] > 16:
                    d0 = self.dom(op)
                    if waited.get(d0, 0) < op["seq"] - 16:
                        eobj.wait_ge(sems[d0], op["seq"] - 16)
                        waited[d0] = op["seq"] - 16
                ins = op["fn"](eobj)
                if op["stream"] is not None:
                    ins.then_inc(sems[self.dom(op)], 16)
                elif op["sig"]:
                    ins.then_inc(sems[e], 1)
            if e == "sp":
                for d, v in final.items():
                    eobj.wait_ge(sems[d], v)

        @block.tensor
        def _(pe):
            run("pe", pe)

        @block.scalar
        def _(act):
            run("act", act)

        @block.vector
        def _(dve):
            run("dve", dve)

        @block.gpsimd
        def _(pool):
            run("pool", pool)

        @block.sync
        def _(sp):
            run("sp", sp)


def build_nc():
    nc = bass.Bass("TRN2", target_bir_lowering=False)
    S = Sched()
    st = ExitStack()

    def din(name, shape, dt=F32):
        return nc.dram_tensor(name, list(shape), dt, kind="ExternalInput").ap()

    def dout(name, shape, dt=F32):
        return nc.dram_tensor(name, list(shape), dt, kind="ExternalOutput").ap()

    def dscr(name, shape, dt):
        return nc.dram_tensor(name, list(shape), dt, kind="Internal").ap()

    x_all = din("x_all", [SEQ, D])
    x_own = din("x_own", [NOWN * 128, D])
    x_smp = din("x_smp", [TDEC, D])
    cvT = din("cvT", [128, 8, 2])
    w_ada = din("w_ada", [D, 3 * D])
    b_ada2 = din("b_ada2", [2, 3 * D])
    g_row2 = din("g_row2", [2, D])
    w_kv = din("w_kv", [D, NKV])
    w_qz = din("w_qz", [D, NQZ])
    w_out = din("w_out", [D, D])
    vecs = din("vecs", [128, 4, 64])
    lamv = din("lamv", [128, 4, 64])
    bf_b = din("bf_b", [128, 8])
    subg = din("subg", [128, 1])
    c_ak = din("c_ak", [PAST, 512])
    c_av = din("c_av", [PAST, 512])
    c_bk = din("c_bk", [PAST, 512])
    c_bv = din("c_bv", [PAST, 512])
    c_lf = din("c_lf", [PAST, 8])
    rope_all = din("rope_all", [NT, 128, 16])
    rope_own = din("rope_own", [NOWN, 128, 16])
    rope_smp = din("rope_smp", [TDEC, 16])
    k_identb = din("k_identb", [128, 128], BF16)
    k_identf = din("k_identf", [128, 128])
    k_U = din("k_U", [128, 128])
    k_E127 = din("k_E127", [128, 128])
    k_Psw = din("k_Psw", [128, 128])
    k_W1 = din("k_W1", [128, 256])
    k_Lt = din("k_Lt", [128, NOWN])
    k_sel2 = din("k_sel2", [2, 2, 128])
    k_bsel = din("k_bsel", [64, 128])
    k_maskA = din("k_maskA", [128, 8, 128], BF16)
    k_maskB = din("k_maskB", [128, 8, 128], BF16)
    k_maskS = din("k_maskS", [64, 64], BF16)

    o_y = dout("o_y", [NOWN * 128, D])
    o_ys = dout("o_ys", [TDEC, D])
    o_ak = dout("o_ak", [NOWN * 128, 512])
    o_av = dout("o_av", [NOWN * 128, 512])
    o_bk = dout("o_bk", [NOWN * 128, 512])
    o_bv = dout("o_bv", [NOWN * 128, 512])
    o_lf = dout("o_lf", [NOWN * 128, 8])
    s_ak = dout("s_ak", [TDEC, 512])
    s_av = dout("s_av", [TDEC, 512])
    s_bk = dout("s_bk", [TDEC, 512])
    s_bv = dout("s_bv", [TDEC, 512])
    s_lf = dout("s_lf", [TDEC, 8])

    def scr(prefix, ntl):
        return dict(
            KTA=dscr(prefix + "KTA", [4, 128, ntl * 128], BF16),
            VA=dscr(prefix + "VA", [4, 128, ntl, 128], BF16),
            KTB=dscr(prefix + "KTB", [8, 70, ntl * 128], BF16),
            VB=dscr(prefix + "VB", [4, 128, ntl, 192], BF16),
            pre=prefix,
        )
    modscr = dscr("modscr", [2, 3 * D], F32)
    SCP = scr("p", NT)
    SCS = scr("s", SNT)

    def sb(name, shape, dt=F32):
        return st.enter_context(nc.sbuf_tensor(name, list(shape), dt))

    identb = sb("identb", [128, 128], BF16)
    identf = sb("identf", [128, 128])
    Um = sb("Um", [128, 128])
    E127 = sb("E127", [128, 128])
    Psw = sb("Psw", [128, 128])
    W1 = sb("W1", [128, 256])
    Lt = sb("Lt", [128, NOWN])
    TSm = sb("TSm", [128, 8])
    sel2 = sb("sel2", [2, 2, 128])
    bsel = sb("bsel", [64, 128])
    Lsb = sb("Lsb", [64, 512])
    maskA = sb("maskA", [128, 8, 128], BF16)
    maskB = sb("maskB", [128, 8, 128], BF16)
    maskS = sb("maskS", [64, 64], BF16)
    ones_bf = sb("ones_bf", [128, 128], BF16)
    ones_f = sb("ones_f", [128, 128])
    vecs_sb = sb("vecs_sb", [128, 4, 64])
    lam_sb = sb("lam_sb", [128, 4, 64])
    bf_sb = sb("bf_sb", [128, 8])
    subg_sb = sb("subg_sb", [128, 1])
    small = sb("small", [128, 16])
    Wkv = sb("Wkv", [128, 8, NKV], BF16)
    Wqz = sb("Wqz", [128, 8, NQZ], BF16)
    Wout = sb("Wout", [128, 8, D], BF16)
    ZT = sb("ZT", [128, 8, 512])
    ZTf = ZT[:, :, :].rearrange("p a b -> p (a b)")
    wst = [ZTf[:, 0:2048], ZTf[:, 2048:4096]]
    wtail = sb("wtail", [128, 8, 8])
    gmod_b = sb("gmod_b", [128, D])
    shift_b = sb("shift_b", [128, D])
    gate_b = sb("gate_b", [128, D])
    mod_sb = ZTf[0:2, 0:3 * D]
    scT = sb("scT", [128, 8, 2])
    scT2 = sb("scT2", [128, 8, 2])

    xin = [sb("xin0", [128, D]), sb("xin1", [128, D])]
    sqj = sb("sqj", [128, D])
    xbf = sb("xbf", [128, D], BF16)
    xT = sb("xT", [128, 8, 128], BF16)
    KAB = sb("KAB", [128, 2, 512])
    KA = KAB[:, 0, :]
    KB = KAB[:, 1, :]
    VAf = sb("VAf", [128, 512])
    VBf = sb("VBf", [128, 512])
    rtab = sb("rtab", [128, 2, 16])
    rtmp = sb("rtmp", [128, 2, 4, 8, 8])
    st8 = sb("st8", [128, 2, 12, 8])
    sm2 = sb("sm2", [128, 2, 4])
    cumq_sb = sb("cumq_sb", [128, 8])
    logf = sb("logf", [128, 2, 8])
    cum = [sb("cum0", [128, 8]), sb("cum1", [128, 8])]
    TSs = sb("TSs", [128, 8])
    spl = sb("spl", [128, 2, 6, 8])
    splb = sb("splb", [128, 2, 3, 8], BF16)
    kabf = sb("kabf", [128, 512], BF16)
    Ktm = sb("Ktm", [128, 8, 70], BF16)
    Qtm = sb("Qtm", [128, 8, 70], BF16)
    KTas = sb("KTas", [128, 4, 128], BF16)
    KTbs = sb("KTbs", [70, 8, 128], BF16)
    vas = sb("vas", [128, 4, 128], BF16)
    vbs = sb("vbs", [128, 4, 192], BF16)
    QTA = sb("QTA", [128, 4, 512], BF16)
    QTB = sb("QTB", [70, 8, 512], BF16)
    GT = sb("GT", [128, 8, 512], BF16)
    zs = sb("zs", [128, D])
    kbuf = [sb("kbuf0", [128, 2, 1152], BF16), sb("kbuf1", [128, 2, 1152], BF16)]
    vbuf = [sb("vbuf0", [128, SNT, 192], BF16), sb("vbuf1", [128, SNT, 192], BF16)]
    Pt = [sb("Pt0", [128, 2, 512], BF16), sb("Pt1", [128, 2, 512], BF16)]
    mt = [KA, VAf, KB, VBf]
    mtk = ["KA", "VAf", "KB", "VBf"]

    psS = st.enter_context(nc.psum_tensor("psS", [128, 4, 512], F32))
    ps = [psS[:, i, :] for i in range(4)] + [st.enter_context(nc.psum_tensor("ps%d" % i, [128, 512], F32)) for i in range(4, 8)]

    def psb(i):
        return ps[i][:, :].bitcast(BF16)

    def _sn(stream, ap):
        return stream

    def dma_in(out, in_, wr, stream="ld", rd=()):
        S.add("sp", lambda e, o=out, i=in_: e.dma_start(out=o, in_=i), rd=rd, wr=wr, stream=_sn(stream, out))

    def dma_out(out, in_, rd, wr=(), stream="st"):
        S.add("sp", lambda e, o=out, i=in_: e.dma_start(out=o, in_=i), rd=rd, wr=wr, stream=_sn(stream, in_))

    def A(fn, rd, wr):
        S.add("act", fn, rd, wr)

    def V(fn, rd, wr):
        S.add("dve", fn, rd, wr)

    def G(fn, rd, wr):
        S.add("pool", fn, rd, wr)

    def PE(fn, rd, wr):
        S.add("pe", fn, rd, wr)

    def act(out, in_, func, rd, wr, bias=0.0, scale=1.0, accum=None):
        A(lambda e: e.activation(out=out, in_=in_, func=func, bias=bias, scale=scale, accum_out=accum), rd, wr)

    def mm(out, lhsT, rhs, start, stop, rd, wr):
        PE(lambda e: e.matmul(out, lhsT, rhs, start=start, stop=stop), rd, wr)

    def tr(out, in_, ident, rd, wr):
        PE(lambda e: e.transpose(out, in_, ident), rd, wr)

    ctoks = []
    wtoks = []
    for t_, d_ in ((identb, k_identb), (identf, k_identf), (Um, k_U), (E127, k_E127), (Psw, k_Psw),
                   (W1, k_W1), (sel2, k_sel2), (bsel, k_bsel), (maskA, k_maskA), (maskB, k_maskB), (maskS, k_maskS),
                   (vecs_sb, vecs), (lam_sb, lamv), (bf_sb, bf_b), (subg_sb, subg), (Lt, k_Lt)):
        sl = tuple(slice(None) for _ in range(len(d_.shape)))
        ctoks.append("c_%d" % len(ctoks))
        dma_in(t_[sl], d_[sl], wr=[ctoks[-1]], stream="ldc")
    V(lambda e: e.memset(ones_bf[:, :], 1.0), [], ["c_ones"])
    V(lambda e: e.memset(ones_f[:, :], 1.0), ["c_ones"], ["c_ones"])
    V(lambda e: e.memset(small[:, 13:14], 0.0), ctoks + ["c_ones"], ["const"])
    V(lambda e: e.memset(Ktm[:, :, 64:67], 1.0), [], ["Ktm"])
    V(lambda e: e.memset(Qtm[:, :, 67:70], 1.0), [], ["Qtm"])
    V(lambda e: e.memset(vbs[:, :, 64:128], 1.0), [], ["vbs"])
    V(lambda e: e.memset(TSs[:, :], 0.0), [], ["TSs"])

    V(lambda e: e.tensor_tensor(out=sqj[:, 0:64], in0=lam_sb[:, 0, :], in1=lam_sb[:, 1, :], op=ALU.mult), ["const"], ["sqj"])
    V(lambda e: e.tensor_tensor(out=sqj[:, 64:128], in0=lam_sb[:, 2, :], in1=lam_sb[:, 3, :], op=ALU.mult), ["const"], ["sqj"])
    V(lambda e: e.reduce_sum(out=small[:, 2:4], in_=sqj[:, 0:128].rearrange("p (a b) -> p a b", a=2), axis=AX.X), ["sqj"], ["small"])
    act(small[:, 4:6], small[:, 2:4], AF.Exp, ["small"], ["small2"])
    V(lambda e: e.tensor_tensor(out=small[:, 6:7], in0=small[:, 4:5], in1=small[:, 5:6], op=ALU.subtract), ["small2"], ["small3"])
    V(lambda e: e.tensor_scalar(out=small[:, 1:2], in0=small[:, 6:7], scalar1=LAM_INIT, scalar2=-1.0, op0=ALU.add, op1=ALU.mult), ["small3"], ["neglam"])
    V(lambda e: e.tensor_scalar(out=small[:, 7:8], in0=subg_sb[:, 0:1], scalar1=1.0 - LAM_INIT, scalar2=None, op0=ALU.mult), ["const"], ["subg2"])

    slot = 0
    dma_in(wtail[:, :, :], w_kv[:, 2048:2056].rearrange("(c p) n -> p c n", p=128), wr=["wtail"], stream="ldc")
    V(lambda e: e.tensor_copy(out=Wkv[:, :, 2048:2056], in_=wtail[:, :, :]), ["wtail"], ["w_tail"])
    wtoks.append("w_tail")
    for (Wt, wsrc, ncol) in ((Wkv, w_kv, 2048), (Wqz, w_qz, NQZ), (Wout, w_out, D)):
        for ch in range(8):
            s_ = slot % 2
            slot += 1
            dma_in(wst[s_][:, 0:ncol], wsrc[ch * 128:(ch + 1) * 128, 0:ncol], wr=["wst%d" % s_], stream="ldw")
            wtoks.append("w_%d" % len(wtoks))
            if ch % 2 == 0:
                V(lambda e, Wt=Wt, ch=ch, s_=s_, ncol=ncol: e.tensor_copy(out=Wt[:, ch, 0:ncol], in_=wst[s_][:, 0:ncol]), ["wst%d" % s_], [wtoks[-1]])
            else:
                A(lambda e, Wt=Wt, ch=ch, s_=s_, ncol=ncol: e.copy(out=Wt[:, ch, 0:ncol], in_=wst[s_][:, 0:ncol]), ["wst%d" % s_], [wtoks[-1]])

    dma_in(scT[:, :, :], cvT[:, :, :], wr=["scT"], stream="ldc")
    act(scT2[:, :, :], scT[:, :, :], AF.Exp, ["scT"], ["scT2"], scale=-1.0)
    V(lambda e: e.tensor_scalar(out=scT2[:, :, :], in0=scT2[:, :, :], scalar1=1.0, scalar2=None, op0=ALU.add), ["scT2"], ["scT2"])
    V(lambda e: e.reciprocal(out=scT2[:, :, :], in_=scT2[:, :, :]), ["scT2"], ["scT2"])
    V(lambda e: e.tensor_tensor(out=scT[:, :, :], in0=scT[:, :, :], in1=scT2[:, :, :], op=ALU.mult), ["scT2", "scT"], ["scT"])
    for ch in range(8):
        for half in range(2):
            s_ = slot % 2
            slot += 1
            dma_in(wst[s_][:, 0:1536], w_ada[ch * 128:(ch + 1) * 128, half * 1536:(half + 1) * 1536],
                   wr=["wst%d" % s_], stream="ldw")
            for g in range(3):
                mm(ps[half * 3 + g][0:2, :], scT[:, ch, :], wst[s_][:, g * 512:(g + 1) * 512],
                   ch == 0, ch == 7, ["scT", "wst%d" % s_], ["ps%d" % (half * 3 + g)])
    bst = [xin[0], xin[1], zs]
    for i in range(3):
        dma_in(bst[i][0:2, :], b_ada2[:, i * D:(i + 1) * D], wr=["bst%d" % i], stream="ldc")
    dma_in(sqj[0:2, :], g_row2[:, :], wr=["sqj"], stream="ldc")
    for g in range(6):
        V(lambda e, g=g: e.tensor_tensor(out=mod_sb[:, g * 512:(g + 1) * 512], in0=ps[g][0:2, :],
                                         in1=bst[g // 2][0:2, (g % 2) * 512:(g % 2 + 1) * 512], op=ALU.add),
          ["ps%d" % g, "bst%d" % (g // 2)], ["mod_sb", "wst0", "wst1"])
    V(lambda e: e.scalar_tensor_tensor(out=mod_sb[:, D:2 * D], in0=mod_sb[:, D:2 * D], scalar=1.0, in1=sqj[0:2, :],
                                       op0=ALU.add, op1=ALU.mult), ["mod_sb", "sqj"], ["mod_sb"])
    dma_out(modscr[:, :], mod_sb, rd=["mod_sb"], wr=["modscr"], stream="stm")
    S.add("sp", lambda e: e.dma_start(out=small[:, 11:12], in_=subg[:, 0:1]), ["bst0", "bst1", "bst2", "sqj", "wst0", "wst1", "modscr"] + wtoks,
          ["xin0", "xin1", "zs0", "zs1", "ZT", "W"], stream="ldbr")

    def set_mod(r):
        for i, dst in enumerate((shift_b, gmod_b, gate_b)):
            dma_in(dst[:, :], modscr[r:r + 1, i * D:(i + 1) * D].to_broadcast([128, D]), wr=["modb"], rd=["modscr"], stream="ldm")

    def flat(t3):
        return t3[:, :, :].rearrange("p a b -> p (a b)")
    k0f = flat(kbuf[0]).bitcast(F32)
    k1f = flat(kbuf[1])
    v0f = flat(vbuf[0])
    v1f = flat(vbuf[1])
    gtf = flat(GT).bitcast(F32)
    p0f = flat(Pt[0])
    p1f = flat(Pt[1])
    BUF = [
        dict(xin=xin[0], sqj=sqj[:, :], xbf=xbf[:, :], xT=xT[:, :, :], KA=KA[:, :], VAf=VAf[:, :], KB=KB[:, :], VBf=VBf[:, :],
             KAB=KAB[:, :, :].rearrange("p a n -> p (a n)"),
             kabf=kabf[:, :], KTas=KTas[:, :, :], KTbs=KTbs[:, :, :], vas=vas[:, :, :], vbs=vbs[:, :, :], Ktm=Ktm[:, :, :]),
        dict(xin=xin[1], sqj=k0f[:, 0:1024], xbf=p0f, xT=p1f.rearrange("p (c t) -> p c t", c=8),
             KA=gtf[:, 0:512], KB=gtf[:, 512:1024], VAf=gtf[:, 1024:1536], VBf=gtf[:, 1536:2048], KAB=gtf[:, 0:1024],
             kabf=k1f[:, 0:512], KTas=k1f[:, 512:1024].rearrange("p (h t) -> p h t", h=4),
             vas=k1f[:, 1024:1536].rearrange("p (h e) -> p h e", h=4),
             Ktm=k1f[:, 1536:2096].rearrange("p (h d) -> p h d", h=8),
             KTbs=v0f[0:70, 0:1024].rearrange("p (h t) -> p h t", h=8),
             vbs=v1f[:, 0:768].rearrange("p (h e) -> p h e", h=4)),
    ]
    FAM_A = ["kbuf0", "kbuf1", "vbuf0", "vbuf1", "Pt0", "Pt1", "GT"]
    FAM_B = [n_ + "1" for n_ in ("sqj", "xbf", "xT", "KA", "VAf", "KB", "VBf", "kabf", "KTas", "KTbs", "vas", "vbs", "Ktm")]

    def tok(n_, s_):
        return n_ if s_ == 0 else n_ + "1"

    def bridge(to_proj):
        V(lambda e: e.memset(small[:, 12:13], 0.0), [], FAM_A + FAM_B)
        if to_proj:
            V(lambda e: e.memset(BUF[1]["Ktm"][:, :, 64:67], 1.0), [], ["Ktm1"])
            V(lambda e: e.memset(BUF[1]["vbs"][:, :, 64:128], 1.0), [], ["vbs1"])

    def stageL(job):
        if job["kind"] == "cache":
            return
        s_ = job["s"]
        dma_in(BUF[s_]["xin"][0:job["P"], :], job["xsrc"], wr=["xin%d" % s_], stream="ldx")
        job["xloaded"] = True

    def stageF(job):
        s_ = job["s"]
        B = BUF[s_]
        P = job["P"]
        if job["kind"] == "cache":
            r0 = job["r0"]
            dma_in(B["KA"][:, :], c_ak[r0:r0 + 128, :], wr=[tok("KA", s_)], stream="ldx")
            dma_in(B["VAf"][:, :], c_av[r0:r0 + 128, :], wr=[tok("VAf", s_)], stream="ldx")
            dma_in(B["KB"][:, :], c_bk[r0:r0 + 128, :], wr=[tok("KB", s_)], stream="ldx")
            dma_in(B["VBf"][:, :], c_bv[r0:r0 + 128, :], wr=[tok("VBf", s_)], stream="ldx")
            dma_in(logf[:, s_, :], c_lf[r0:r0 + 128, :], wr=[tok("logf", s_)], stream="ldx")
            return
        X = B["xin"]
        tk = "xin%d" % s_
        SQ = B["sqj"]
        if not job.get("xloaded"):
            stageL(job)
        dma_in(rtab[0:P, s_, :], job["rsrc"], wr=[tok("rtab", s_)], stream="ldr")
        act(SQ[0:P, :], X[0:P, :], AF.Square, [tk], [tok("sqj", s_), tok("ssq", s_)], accum=sm2[0:P, s_, 0:1])
        act(sm2[0:P, s_, 1:2], sm2[0:P, s_, 0:1], AF.Ln, [tok("ssq", s_)], [tok("ssq2", s_)], bias=EPS, scale=1.0 / D)
        act(sm2[0:P, s_, 2:3], sm2[0:P, s_, 1:2], AF.Exp, [tok("ssq2", s_)], [tok("rstd", s_)], scale=-0.5)
        V(lambda e: e.scalar_tensor_tensor(out=SQ[0:P, :], in0=X[0:P, :], scalar=sm2[0:P, s_, 2:3], in1=gmod_b[0:P, :],
                                           op0=ALU.mult, op1=ALU.mult), [tk, tok("rstd", s_), "modb", tok("sqj", s_)], [tok("sqj", s_)])
        G(lambda e: e.tensor_tensor(out=B["xbf"][0:P, :], in0=SQ[0:P, :], in1=shift_b[0:P, :], op=ALU.add),
          [tok("sqj", s_), "modb"], [tok("xbf", s_)])

    def stageF2(job):
        if job["kind"] == "cache":
            return
        s_ = job["s"]
        B = BUF[s_]
        P = job["P"]
        xTp = psb(0)
        for ch in range(8):
            tr(xTp[:, ch * 128:ch * 128 + P], B["xbf"][0:P, ch * 128:(ch + 1) * 128], identb[0:P, 0:P], [tok("xbf", s_), "const"], ["ps0"])
        A(lambda e: e.copy(out=B["xT"][:, :, 0:P], in_=xTp.rearrange("p (c t) -> p c t", c=8)[:, :, 0:P]), ["ps0"], [tok("xT", s_)])

    def stageM(job):
        if job["kind"] == "cache":
            return
        s_ = job["s"]
        B = BUF[s_]
        P = job["P"]
        for (Wt, c0, ncol, bank) in job["groups"]:
            for ch in range(8):
                mm(ps[bank][0:P, 0:ncol], B["xT"][:, ch, 0:P], Wt[:, ch, c0:c0 + ncol], ch == 0, ch == 7,
                   [tok("xT", s_), "W"], ["ps%d" % bank])

    def qk_norm(dst, dtok, bank, P, gain_idx, s_):
        pv = ps[bank]
        SQ = BUF[s_]["sqj"]
        act(SQ[0:P, 0:512], pv[0:P, :], AF.Square, ["ps%d" % bank], [tok("sqj", s_)])
        V(lambda e: e.reduce_sum(out=st8[0:P, s_, 0, :], in_=SQ[0:P, 0:512].rearrange("p (g d) -> p g d", g=8), axis=AX.X),
          [tok("sqj", s_)], [tok("st8a", s_)])
        act(st8[0:P, s_, 1, :], st8[0:P, s_, 0, :], AF.Ln, [tok("st8a", s_)], [tok("st8b", s_)], bias=EPS, scale=1.0 / 64)
        act(st8[0:P, s_, 2, :], st8[0:P, s_, 1, :], AF.Exp, [tok("st8b", s_)], [tok("st8c", s_)], scale=-0.5)
        V(lambda e: e.tensor_tensor(out=dst[0:P, :].rearrange("p (g d) -> p g d", g=8),
                                    in0=pv[0:P, :].rearrange("p (g d) -> p g d", g=8),
                                    in1=st8[0:P, s_, 2, :].unsqueeze(2).to_broadcast([P, 8, 64]), op=ALU.mult),
          ["ps%d" % bank, tok("st8c", s_)], [dtok])
        G(lambda e: e.tensor_tensor(out=dst[0:P, :].rearrange("p (g d) -> p g d", g=8),
                                    in0=dst[0:P, :].rearrange("p (g d) -> p g d", g=8),
                                    in1=vecs_sb[0:P, gain_idx, :].unsqueeze(1).to_broadcast([P, 8, 64]), op=ALU.mult),
          [dtok, "const"], [dtok])

    def qk_norm2(P, gain_a, gain_b, s_):
        B = BUF[s_]
        SQ = B["sqj"]
        pv = psS[0:P, 1:3, :].rearrange("p a n -> p (a n)")
        dst = B["KAB"]
        ta, tb = tok("KA", s_), tok("KB", s_)
        act(SQ[0:P, :], pv, AF.Square, ["ps1", "ps2"], [tok("sqj", s_)])
        V(lambda e: e.reduce_sum(out=st8[0:P, s_, 0:2, :].rearrange("p a g -> p (a g)"),
                                 in_=SQ[0:P, :].rearrange("p (g d) -> p g d", g=16), axis=AX.X),
          [tok("sqj", s_)], [tok("st8a", s_)])
        act(st8[0:P, s_, 2:4, :], st8[0:P, s_, 0:2, :], AF.Ln, [tok("st8a", s_)], [tok("st8b", s_)], bias=EPS, scale=1.0 / 64)
        act(st8[0:P, s_, 4:6, :], st8[0:P, s_, 2:4, :], AF.Exp, [tok("st8b", s_)], [tok("st8c", s_)], scale=-0.5)
        V(lambda e: e.tensor_tensor(out=dst[0:P, :].rearrange("p (g d) -> p g d", g=16),
                                    in0=pv.rearrange("p (g d) -> p g d", g=16),
                                    in1=st8[0:P, s_, 4:6, :].rearrange("p a g -> p (a g)").unsqueeze(2).to_broadcast([P, 16, 64]),
                                    op=ALU.mult),
          ["ps1", "ps2", tok("st8c", s_)], [ta, tb])
        for (dd, dt_, gi) in ((B["KA"], ta, gain_a), (B["KB"], tb, gain_b)):
            G(lambda e, dd=dd, gi=gi: e.tensor_tensor(out=dd[0:P, :].rearrange("p (g d) -> p g d", g=8),
                                                      in0=dd[0:P, :].rearrange("p (g d) -> p g d", g=8),
                                                      in1=vecs_sb[0:P, gi, :].unsqueeze(1).to_broadcast([P, 8, 64]), op=ALU.mult),
              [dt_, "const"], [dt_])

    def rope(dst, tk, P, rsrc, s_):
        dv = dst[0:P, :].rearrange("p (g d) -> p g d", g=8)
        x1 = dv[:, :, 0:8]
        x2 = dv[:, :, 8:16]
        cs = rtab[0:P, s_, 0:8].unsqueeze(1).to_broadcast([P, 8, 8])
        sn = rtab[0:P, s_, 8:16].unsqueeze(1).to_broadcast([P, 8, 8])
        rt = tok("rtmp", s_)
        rtk = tok("rtab", s_)
        V(lambda e: e.tensor_tensor(out=rtmp[0:P, s_, 0], in0=x1, in1=cs, op=ALU.mult), [tk, rtk], [rt])
        V(lambda e: e.tensor_tensor(out=rtmp[0:P, s_, 1], in0=x2, in1=sn, op=ALU.mult), [tk, rtk], [rt])
        V(lambda e: e.tensor_tensor(out=rtmp[0:P, s_, 2], in0=x2, in1=cs, op=ALU.mult), [tk, rtk], [rt])
        V(lambda e: e.tensor_tensor(out=rtmp[0:P, s_, 3], in0=x1, in1=sn, op=ALU.mult), [tk, rtk], [rt])
        V(lambda e: e.tensor_tensor(out=x1, in0=rtmp[0:P, s_, 0], in1=rtmp[0:P, s_, 1], op=ALU.subtract), [rt], [tk])
        V(lambda e: e.tensor_tensor(out=x2, in0=rtmp[0:P, s_, 2], in1=rtmp[0:P, s_, 3], op=ALU.add), [rt], [tk])

    def split3(src, P, dsttile, dtok, c0, scale, stok, s_):
        sp_ = spl[:, s_]
        t1, t2 = tok("spl1", s_), tok("spl2", s_)
        hi = dsttile[0:P, :, c0:c0 + 1]
        mid = dsttile[0:P, :, c0 + 1:c0 + 2]
        lo = dsttile[0:P, :, c0 + 2:c0 + 3]
        sv = src.unsqueeze(2)
        V(lambda e: e.tensor_scalar(out=hi, in0=sv, scalar1=scale, scalar2=None, op0=ALU.mult), [stok], [dtok])
        V(lambda e: e.scalar_tensor_tensor(out=sp_[0:P, 1, :].unsqueeze(2), in0=sv, scalar=scale, in1=hi, op0=ALU.mult, op1=ALU.subtract),
          [stok, dtok], [t1])
        V(lambda e: e.tensor_copy(out=mid, in_=sp_[0:P, 1, :].unsqueeze(2)), [t1], [dtok])
        V(lambda e: e.tensor_tensor(out=sp_[0:P, 2, :].unsqueeze(2), in0=sp_[0:P, 1, :].unsqueeze(2), in1=mid, op=ALU.subtract), [t1, dtok], [t2])
        V(lambda e: e.tensor_copy(out=lo, in_=sp_[0:P, 2, :].unsqueeze(2)), [t2], [dtok])

    def pack_kv_cum(kt, P, first, s_):
        B = BUF[s_]
        lg = logf[:, s_, :]
        cprev = cum[(kt + 1) % 2]
        ccur = cum[kt % 2]
        mm(ps[5][0:P, 32:40], Um[0:P, 0:P], lg[0:P, :], True, first, ["const", tok("logf", s_)], ["ps5"])
        if not first:
            mm(ps[5][0:P, 32:40], E127[:, 0:P], cprev[:, :], False, True, ["const", "cum%d" % ((kt + 1) % 2)], ["ps5"])
        V(lambda e: e.tensor_copy(out=ccur[0:P, :], in_=ps[5][0:P, 32:40]), ["ps5"], ["cum%d" % (kt % 2)])
        split3(ccur[0:P, :], P, B["Ktm"], tok("Ktm", s_), 67, -8.0, "cum%d" % (kt % 2), s_)

    def pack_kv_cast(P, s_):
        B = BUF[s_]
        A(lambda e: e.copy(out=B["kabf"][0:P, :], in_=B["KA"][0:P, :]), [tok("KA", s_)], [tok("kabf", s_)])
        A(lambda e: e.copy(out=B["vas"][0:P, :, :], in_=B["VAf"][0:P, :].rearrange("p (h e) -> p h e", h=4)),
          [tok("VAf", s_)], [tok("vas", s_)])
        A(lambda e: e.copy(out=B["Ktm"][0:P, :, 0:64], in_=B["KB"][0:P, :].rearrange("p (h d) -> p h d", h=8)),
          [tok("KB", s_)], [tok("Ktm", s_)])
        vv = B["VBf"][0:P, :].rearrange("p (q a d) -> p q a d", q=4, a=2)
        G(lambda e: e.tensor_copy(out=B["vbs"][0:P, :, 0:64], in_=vv[:, :, 0, :]), [tok("VBf", s_)], [tok("vbs", s_)])
        G(lambda e: e.tensor_copy(out=B["vbs"][0:P, :, 128:192], in_=vv[:, :, 1, :]), [tok("VBf", s_)], [tok("vbs", s_)])

    def pack_kv_b(SC, kt, P, first, s_):
        pre = SC["pre"]
        B = BUF[s_]
        kp = psb(6)
        for h in range(4):
            tr(kp[:, h * 128:h * 128 + P], B["kabf"][0:P, h * 128:(h + 1) * 128], identb[0:P, 0:P], [tok("kabf", s_), "const"], ["ps6"])
        V(lambda e: e.tensor_copy(out=B["KTas"][:, :, 0:P], in_=kp[:, 0:512].rearrange("p (h t) -> p h t", h=4)[:, :, 0:P]),
          ["ps6"], [tok("KTas", s_)])
        dma_out(SC["KTA"][:, :, kt * 128:kt * 128 + P].rearrange("h p t -> p h t"), B["KTas"][:, :, 0:P], rd=[tok("KTas", s_)],
                wr=[pre + "KTA%d" % kt], stream="stk")
        dma_out(SC["VA"][:, 0:P, kt, :].rearrange("h p e -> p h e"), B["vas"][0:P, :, :], rd=[tok("vas", s_)], wr=[pre + "VA%d" % kt], stream="stk")
        kp2 = psb(7)
        for h in range(8):
            tr(kp2[0:70, h * 128:h * 128 + P], B["Ktm"][0:P, h, :], identb[0:P, 0:P], [tok("Ktm", s_), "const"], ["ps7"])
        V(lambda e: e.tensor_copy(out=B["KTbs"][:, :, 0:P], in_=kp2[0:70, :].rearrange("p (h t) -> p h t", h=8)[:, :, 0:P]),
          ["ps7"], [tok("KTbs", s_)])
        dma_out(SC["KTB"][:, :, kt * 128:kt * 128 + P].rearrange("h p t -> p h t"), B["KTbs"][:, :, 0:P], rd=[tok("KTbs", s_)],
                wr=[pre + "KTB%d" % kt], stream="stk")
        dma_out(SC["VB"][:, 0:P, kt, :].rearrange("h p e -> p h e"), B["vbs"][0:P, :, :], rd=[tok("vbs", s_)], wr=[pre + "VB%d" % kt], stream="stk")

    G_KV = [(Wkv, 0, 512, 1), (Wkv, 1024, 512, 2), (Wkv, 512, 512, 3), (Wkv, 1536, 512, 4), (Wkv, 2048, 8, 5)]
    G_QZ = [(Wqz, 0, 512, 1), (Wqz, 1024, 512, 2), (Wqz, 512, 512, 3), (Wqz, 1536, 512, 4)]

    def stageP1(job):
        s_ = job["s"]
        B = BUF[s_]
        P = job["P"]
        if job["kind"] == "cache":
            return
        if job["kind"] == "kv":
            qk_norm2(P, 1, 3, s_)
            A(lambda e: e.copy(out=B["VAf"][0:P, :], in_=ps[3][0:P, :]), ["ps3"], [tok("VAf", s_)])
            A(lambda e: e.copy(out=B["VBf"][0:P, :], in_=ps[4][0:P, :]), ["ps4"], [tok("VBf", s_)])
            V(lambda e: e.tensor_tensor(out=st8[0:P, s_, 6, :], in0=ps[5][0:P, 0:8], in1=bf_sb[0:P, :], op=ALU.add),
              ["ps5", "const"], [tok("st8d", s_)])
        else:
            qk_norm2(P, 0, 2, s_)
            for half, bank in ((0, 3), (1, 4)):
                zc = zs[0:P, half * 512:(half + 1) * 512]
                act(zc, ps[bank][0:P, :], AF.Exp, ["ps%d" % bank], ["zs%d" % half], scale=-1.0)
                act(zc, zc, AF.Ln, ["zs%d" % half], ["zs%d" % half], bias=1.0)
                act(zc, zc, AF.Exp, ["zs%d" % half], ["zs%d" % half], scale=-1.0)
                V(lambda e, zc=zc, bank=bank: e.tensor_tensor(out=zc, in0=ps[bank][0:P, :], in1=zc, op=ALU.mult),
                  ["zs%d" % half, "ps%d" % bank], ["zs%d" % half])

    def stageP2a(job):
        s_ = job["s"]
        B = BUF[s_]
        P = job["P"]
        lg = logf[:, s_, :]
        if job["kind"] in ("kv", "cache"):
            if job["kind"] == "kv":
                act(st8[0:P, s_, 7, :], st8[0:P, s_, 6, :], AF.Exp, [tok("st8d", s_)], [tok("st8e", s_)], scale=-1.0)
                act(st8[0:P, s_, 8, :], st8[0:P, s_, 7, :], AF.Ln, [tok("st8e", s_)], [tok("st8f", s_)], bias=1.0)
                V(lambda e: e.tensor_scalar(out=lg[0:P, :], in0=st8[0:P, s_, 8, :], scalar1=-1.0, scalar2=None, op0=ALU.mult),
                  [tok("st8f", s_)], [tok("logf", s_)])
                rope(B["KA"], tok("KA", s_), P, job["rsrc"], s_)
            if job.get("SC") is not None:
                pack_kv_cast(P, s_)
            if job.get("outs") is not None:
                dak, dav, dbk, dbv, dlf = job["outs"]
                row0 = job["row0"]
                dma_out(dak[row0:row0 + P, :], B["KA"][0:P, :], rd=[tok("KA", s_)], stream="sto")
                dma_out(dav[row0:row0 + P, :], B["VAf"][0:P, :], rd=[tok("VAf", s_)], stream="sto")
                dma_out(dbk[row0:row0 + P, :], B["KB"][0:P, :], rd=[tok("KB", s_)], stream="sto")
                dma_out(dbv[row0:row0 + P, :], B["VBf"][0:P, :], rd=[tok("VBf", s_)], stream="sto")
                dma_out(dlf[row0:row0 + P, :], lg[0:P, :], rd=[tok("logf", s_)], stream="sto")
        else:
            rope(B["KA"], tok("KA", s_), P, job["rsrc"], s_)
            A(lambda e: e.copy(out=B["kabf"][0:P, :], in_=B["KA"][0:P, :]), [tok("KA", s_)], [tok("kabf", s_)])
            G(lambda e: e.tensor_copy(out=Qtm[0:P, :, 0:64], in_=B["KB"][0:P, :].rearrange("p (h d) -> p h d", h=8)), [tok("KB", s_)], ["Qtm"])
            split3(cumq_sb[0:P, :], P, Qtm, "Qtm", 64, 8.0, "cumq", s_)

    def stageP2c(job):
        s_ = job["s"]
        B = BUF[s_]
        P = job["P"]
        lg = logf[:, s_, :]
        if job["kind"] in ("kv", "cache"):
            if job.get("ts_kt") is not None:
                kt = job["ts_kt"]
                mm(ps[5][:, 64:72], W1[:, 127 - kt:255 - kt], lg[:, :], True, True, ["const", tok("logf", s_)], ["ps5"])
                V(lambda e: e.tensor_tensor(out=TSs[:, :], in0=TSs[:, :], in1=ps[5][:, 64:72], op=ALU.add), ["ps5", "TSs"], ["TSs"])
            if job.get("SC") is not None:
                pack_kv_cum(job["kt"], P, job["first"], s_)
            if job.get("cumq") == "own":
                t = job["t"]
                mm(ps[5][:, 96:104], Um[:, :], lg[:, :], True, False, ["const", tok("logf", s_)], ["ps5"])
                V(lambda e, t=t: e.tensor_scalar(out=TSm[:, :], in0=TSs[:, :], scalar1=Lt[:, t:t + 1], scalar2=None, op0=ALU.mult),
                  ["TSs", "const"], ["TSm"])
                mm(ps[5][:, 96:104], ones_f[:, :], TSm[:, :], False, True, ["const", "TSm"], ["ps5"])
                V(lambda e: e.tensor_copy(out=cumq_sb[:, :], in_=ps[5][:, 96:104]), ["ps5"], ["cumq"])
            elif job.get("cumq") == "sample":
                V(lambda e: e.tensor_copy(out=cumq_sb[0:P, :], in_=cum[job["kt"] % 2][0:P, :]), ["cum%d" % (job["kt"] % 2)], ["cumq"])
        else:
            col0 = job["col0"]
            for half in range(2):
                for q in range(4):
                    ci = half * 4 + q
                    tr(ps[0][:, q * 128:q * 128 + P], zs[0:P, ci * 128:(ci + 1) * 128], identf[0:P, 0:P], ["zs%d" % half, "const"], ["ps0"])
                V(lambda e, half=half: e.tensor_copy(out=ZT[:, half * 4:(half + 1) * 4, col0:col0 + P],
                                                     in_=ps[0][:, :].rearrange("p (q t) -> p q t", q=4)[:, :, 0:P]), ["ps0"], ["ZT"])

    def stageP2b(job):
        s_ = job["s"]
        B = BUF[s_]
        P = job["P"]
        if job["kind"] in ("kv", "cache"):
            if job.get("SC") is not None:
                pack_kv_b(job["SC"], job["kt"], P, job["first"], s_)
        else:
            col0 = job["col0"]
            kp = psb(6)
            for h in range(4):
                tr(kp[:, h * 128:h * 128 + P], B["kabf"][0:P, h * 128:(h + 1) * 128], identb[0:P, 0:P], [tok("kabf", s_), "const"], ["ps6"])
            V(lambda e: e.tensor_copy(out=QTA[:, :, col0:col0 + P], in_=kp[:, 0:512].rearrange("p (h t) -> p h t", h=4)[:, :, 0:P]),
              ["ps6"], ["QTA"])
            kp2 = psb(7)
            for h in range(8):
                tr(kp2[0:70, h * 128:h * 128 + P], Qtm[0:P, h, :], identb[0:P, 0:P], ["Qtm", "const"], ["ps7"])
            V(lambda e: e.tensor_copy(out=QTB[:, :, col0:col0 + P], in_=kp2[0:70, :].rearrange("p (h t) -> p h t", h=8)[:, :, 0:P]),
              ["ps7"], ["QTB"])

    jslot = [0]
    xslot = [0]

    def run_jobs(jobs):
        n = len(jobs)
        for jb in jobs:
            jb["s"] = jslot[0] % 2
            jslot[0] += 1
        stageL(jobs[0])
        if n > 1:
            stageL(jobs[1])
        stageF(jobs[0])
        stageF2(jobs[0])
        stageM(jobs[0])
        for k in range(n):
            if k + 2 < n:
                stageL(jobs[k + 2])
            if k + 1 < n:
                stageF(jobs[k + 1])
            stageP1(jobs[k])
            if k + 1 < n:
                stageF2(jobs[k + 1])
            if k >= 1:
                stageP2c(jobs[k - 1])
            if k >= 2:
                stageP2b(jobs[k - 2])
            stageP2a(jobs[k])
            if k + 1 < n:
                stageM(jobs[k + 1])
        stageP2c(jobs[n - 1])
        if n >= 2:
            stageP2b(jobs[n - 2])
        stageP2b(jobs[n - 1])

    abuf = [0]

    def attention(SC, chunks, N_full, unit):
        pre = SC["pre"]
        isA = unit < 4
        h = unit if isA else unit - 4
        its = []
        for ci, (kt0, nkt, c0, zone, lastP) in enumerate(chunks):
            b = abuf[0] % 2
            abuf[0] += 1
            for r in range(nkt):
                its.append(dict(b=b, r=r, KP=(lastP if r == nkt - 1 else 128), c0=c0,
                                mk=(zone[r] if zone is not None else None),
                                load=((kt0, nkt, lastP) if r == 0 else None)))
        n = len(its)

        def stageA(it, idx):
            b = it["b"]
            kb_, vb_ = kbuf[b], vbuf[b]
            if it["load"] is not None:
                kt0, nkt, lastP = it["load"]
                ncols = (nkt - 1) * 128 + lastP
                if isA:
                    dma_in(kb_[:, 0, 0:ncols], SC["KTA"][h, :, kt0 * 128:kt0 * 128 + ncols],
                           rd=[pre + "KTA%d" % k for k in range(kt0, kt0 + nkt)], wr=["kbuf%d" % b], stream="ldk")
                    nfull = nkt if lastP == 128 else nkt - 1
                    dma_in(vb_[:, 0:nfull, 0:128], SC["VA"][h, :, kt0:kt0 + nfull, :],
                           rd=[pre + "VA%d" % k for k in range(kt0, kt0 + nkt)], wr=["vbuf%d" % b], stream="ldk")
                    if nfull < nkt:
                        dma_in(vb_[0:lastP, nfull:nkt, 0:128], SC["VA"][h, 0:lastP, kt0 + nfull:kt0 + nkt, :],
                               rd=[pre + "VA%d" % k for k in range(kt0, kt0 + nkt)], wr=["vbuf%d" % b], stream="ldk")
                else:
                    dma_in(kb_[0:70, :, 0:ncols], SC["KTB"][2 * h:2 * h + 2, :, kt0 * 128:kt0 * 128 + ncols].rearrange("a p t -> p a t"),
                           rd=[pre + "KTB%d" % k for k in range(kt0, kt0 + nkt)], wr=["kbuf%d" % b], stream="ldk")
                    nfull = nkt if lastP == 128 else nkt - 1
                    dma_in(vb_[:, 0:nfull, :], SC["VB"][h, :, kt0:kt0 + nfull, :],
                           rd=[pre + "VB%d" % k for k in range(kt0, kt0 + nkt)], wr=["vbuf%d" % b], stream="ldk")
                    if nfull < nkt:
                        dma_in(vb_[0:lastP, nfull:nkt, :], SC["VB"][h, 0:lastP, kt0 + nfull:kt0 + nkt, :],
                               rd=[pre + "VB%d" % k for k in range(kt0, kt0 + nkt)], wr=["vbuf%d" % b], stream="ldk")
            r, KP, c0 = it["r"], it["KP"], it["c0"]
            N = N_full - c0
            sb_ = abuf[0] % 2
            abuf[0] += 1
            it["sb"] = sb_
            P_ = Pt[sb_]
            for u in range(2):
                sbank = sb_ * 2 + u
                if isA:
                    lhsT = kb_[u * 64:(u + 1) * 64, 0, r * 128:r * 128 + KP]
                    rhs = QTA[u * 64:(u + 1) * 64, h, c0:N_full]
                else:
                    lhsT = kb_[0:70, u, r * 128:r * 128 + KP]
                    rhs = QTB[0:70, 2 * h + u, c0:N_full]
                mm(ps[sbank][0:KP, 0:N], lhsT, rhs, True, True, ["kbuf%d" % b, "QTA" if isA else "QTB"], ["ps%d" % sbank])
            mk = it["mk"]
            if mk is not None:
                mw = mk.shape[-1]
                for u in range(2):
                    sbank = sb_ * 2 + u
                    V(lambda e, sbank=sbank, KP=KP, mk=mk, mw=mw: e.tensor_tensor(
                        out=ps[sbank][0:KP, 0:mw], in0=ps[sbank][0:KP, 0:mw], in1=mk, op=ALU.add),
                      ["ps%d" % sbank, "const"], ["ps%d" % sbank])
            act(P_[0:KP, :, 0:N], psS[0:KP, 2 * sb_:2 * sb_ + 2, 0:N], AF.Exp, ["ps%d" % (2 * sb_), "ps%d" % (2 * sb_ + 1)],
                ["Pt%d" % sb_], scale=0.125)

        def stageB(it, first, last):
            b, r, KP, c0, sb_ = it["b"], it["r"], it["KP"], it["c0"], it["sb"]
            vb_ = vbuf[b]
            P_ = Pt[sb_]
            N = N_full - c0
            for u in range(2):
                if isA:
                    vl = vb_[0:KP, r, 0:128]
                else:
                    vl = vb_[0:KP, r, u * 64:u * 64 + 128]
                mm(ps[4 + u][:, c0:N_full], vl, P_[0:KP, u, 0:N], first, last, ["vbuf%d" % b, "Pt%d" % sb_], ["ps%d" % (4 + u)])
            if isA:
                for u in range(2):
                    PE(lambda e, u=u, KP=KP, P_=P_, N=N, c0=c0: e.matmul(
                        ps[6 + u][32 * u:32 * u + 32, c0:N_full], ones_bf[0:KP, 0:32], P_[0:KP, u, 0:N],
                        start=first, stop=last, tile_position=(0, 32 * u)),
                       ["const", "Pt%d" % sb_], ["ps%d" % (6 + u)])

        for idx in range(n + 1):
            if idx < n:
                stageA(its[idx], idx)
            if idx >= 1:
                stageB(its[idx - 1], idx - 1 == 0, idx - 1 == n - 1)

    def merge_unit(unit, N):
        if unit < 4:
            V(lambda e: e.tensor_copy(out=Lsb[0:32, 0:N], in_=ps[6][0:32, 0:N]), ["ps6"], ["Lsb"])
            V(lambda e: e.tensor_copy(out=Lsb[32:64, 0:N], in_=ps[7][32:64, 0:N]), ["ps7"], ["Lsb"])
            mm(ps[0][:, 0:N], bsel[0:32, :], Lsb[0:32, 0:N], True, True, ["const", "Lsb"], ["ps0"])
            mm(ps[1][:, 0:N], bsel[32:64, :], Lsb[32:64, 0:N], True, True, ["const", "Lsb"], ["ps1"])
            act(mt[0][:, 0:N], ps[0][:, 0:N], AF.Ln, ["ps0"], ["KA"])
            act(mt[0][:, 0:N], mt[0][:, 0:N], AF.Exp, ["KA"], ["KA"], scale=-1.0)
            act(mt[1][:, 0:N], ps[1][:, 0:N], AF.Ln, ["ps1"], ["VAf"])
            act(mt[1][:, 0:N], mt[1][:, 0:N], AF.Exp, ["VAf"], ["VAf"], scale=-1.0)
            V(lambda e: e.tensor_tensor(out=mt[0][:, 0:N], in0=ps[4][:, 0:N], in1=mt[0][:, 0:N], op=ALU.mult), ["ps4", "KA"], ["KA"])
            V(lambda e: e.tensor_tensor(out=mt[1][:, 0:N], in0=ps[5][:, 0:N], in1=mt[1][:, 0:N], op=ALU.mult), ["ps5", "VAf"], ["VAf"])
            V(lambda e: e.scalar_tensor_tensor(out=mt[2][:, 0:N], in0=mt[1][:, 0:N], scalar=small[:, 1:2], in1=mt[0][:, 0:N],
                                               op0=ALU.mult, op1=ALU.add), ["KA", "VAf", "neglam"], ["KB"])
            act(mt[3][:, 0:N], mt[2][:, 0:N], AF.Square, ["KB"], ["VBf"])
            mm(ps[0][:, 0:N], ones_f[:, :], mt[3][:, 0:N], True, True, ["const", "VBf"], ["ps0"])
            act(mt[0][:, 0:N], ps[0][:, 0:N], AF.Ln, ["ps0"], ["KA"], bias=EPS, scale=1.0 / 128)
            act(mt[1][:, 0:N], mt[0][:, 0:N], AF.Exp, ["KA"], ["VAf"], scale=-0.5)
            V(lambda e: e.scalar_tensor_tensor(out=mt[2][:, 0:N], in0=mt[2][:, 0:N], scalar=small[:, 7:8], in1=mt[1][:, 0:N],
                                               op0=ALU.mult, op1=ALU.mult), ["KB", "VAf", "subg2"], ["KB"])
            V(lambda e: e.tensor_tensor(out=GT[:, unit, 0:N], in0=mt[2][:, 0:N], in1=ZT[:, unit, 0:N], op=ALU.mult),
              ["KB", "ZT"], ["GT"])
        else:
            V(lambda e: e.tensor_copy(out=mt[0][0:64, 0:N], in_=ps[5][0:64, 0:N]), ["ps5"], ["KA"])
            V(lambda e: e.tensor_copy(out=mt[0][64:128, 0:N], in_=ps[4][64:128, 0:N]), ["ps4"], ["KA"])
            mm(ps[0][:, 0:N], Psw[:, :], mt[0][:, 0:N], True, True, ["const", "KA"], ["ps0"])
            act(mt[1][:, 0:N], ps[0][:, 0:N], AF.Ln, ["ps0"], ["VAf"])
            act(mt[1][:, 0:N], mt[1][:, 0:N], AF.Exp, ["VAf"], ["VAf"], scale=-1.0)
            V(lambda e: e.tensor_tensor(out=mt[2][0:64, 0:N], in0=ps[4][0:64, 0:N], in1=mt[1][0:64, 0:N], op=ALU.mult), ["ps4", "VAf"], ["KB"])
            V(lambda e: e.tensor_tensor(out=mt[2][64:128, 0:N], in0=ps[5][64:128, 0:N], in1=mt[1][64:128, 0:N], op=ALU.mult), ["ps5", "VAf"], ["KB"])
            V(lambda e: e.tensor_tensor(out=GT[:, unit, 0:N], in0=mt[2][:, 0:N], in1=ZT[:, unit, 0:N], op=ALU.mult),
              ["KB", "ZT"], ["GT"])

    def out_proj(xsrc, ydst, P, col0):
        s_ = xslot[0] % 2
        xslot[0] += 1
        X = xin[s_]
        tk = "xin%d" % s_
        dma_in(X[0:P, :], xsrc, wr=[tk], stream="ldx")
        for half in range(2):
            bank = 1 + half
            for ch in range(8):
                mm(ps[bank][0:P, :], GT[:, ch, col0:col0 + P], Wout[:, ch, half * 512:(half + 1) * 512], ch == 0, ch == 7,
                   ["GT", "W"], ["ps%d" % bank])
            V(lambda e, half=half, bank=bank: e.tensor_tensor(out=zs[0:P, half * 512:(half + 1) * 512], in0=ps[bank][0:P, :],
                                                              in1=gate_b[0:P, half * 512:(half + 1) * 512], op=ALU.mult),
              ["ps%d" % bank, "modb"], ["zs%d" % half])
            V(lambda e, half=half: e.tensor_tensor(out=zs[0:P, half * 512:(half + 1) * 512], in0=zs[0:P, half * 512:(half + 1) * 512],
                                                   in1=X[0:P, half * 512:(half + 1) * 512], op=ALU.add),
              ["zs%d" % half, tk], ["zs%d" % half])
        dma_out(ydst, zs[0:P, :], rd=["zs0", "zs1"], wr=["yout"], stream="sto")

    bridge(True)
    set_mod(1)
    jobs = [dict(kind="cache", P=128, r0=kt * 128, SC=SCS, kt=kt, first=(kt == 0)) for kt in range(8)]
    jobs.append(dict(kind="kv", P=TDEC, xsrc=x_smp[:, :], groups=G_KV, rsrc=rope_smp[:, :], SC=SCS, kt=8, first=False,
                     outs=(s_ak, s_av, s_bk, s_bv, s_lf), row0=0, cumq="sample"))
    jobs.append(dict(kind="qz", P=TDEC, xsrc=x_smp[:, :], groups=G_QZ, rsrc=rope_smp[:, :], col0=0))
    run_jobs(jobs)
    bridge(False)
    zoneS_A = [None] * 9
    zoneS_B = [None] * 8 + [maskS[:, :]]
    for unit in range(8):
        attention(SCS, [(0, 9, 0, zoneS_A if unit < 4 else zoneS_B, TDEC)], TDEC, unit)
        merge_unit(unit, TDEC)
    out_proj(x_smp[:, :], o_ys[:, :], TDEC, 0)
    bridge(True)

    set_mod(0)
    run_jobs([dict(kind="kv", P=128, xsrc=x_all[kt * 128:(kt + 1) * 128, :], groups=G_KV, rsrc=rope_all[kt, :, :],
                   SC=SCP, kt=kt, first=(kt == 0), ts_kt=kt) for kt in range(NT)])

    for j in range(4):
        jobs = []
        for m in range(4):
            t = 4 * j + m
            rows = slice(t * 128, (t + 1) * 128)
            jobs.append(dict(kind="kv", P=128, xsrc=x_own[rows, :], groups=G_KV, rsrc=rope_own[t, :, :],
                             outs=(o_ak, o_av, o_bk, o_bv, o_lf), row0=t * 128, cumq="own", t=t))
            jobs.append(dict(kind="qz", P=128, xsrc=x_own[rows, :], groups=G_QZ, rsrc=rope_own[t, :, :], col0=m * 128))
        run_jobs(jobs)
        bridge(False)
        for unit in range(8):
            chunks = [(8 * ch, 8, 0, None, 128) for ch in range(4 * j)]
            mtab = maskA if unit < 4 else maskB
            for m in range(4):
                chunks.append((8 * (4 * j + m), 8, m * 128, [mtab[:, r, :] for r in range(8)], 128))
            attention(SCP, chunks, 512, unit)
            merge_unit(unit, 512)
        for m in range(4):
            t = 4 * j + m
            out_proj(x_own[t * 128:(t + 1) * 128, :], o_y[t * 128:(t + 1) * 128, :], 128, m * 128)
        bridge(True)

    S.emit(nc, st)
    st.close()
    return nc


_NC_CACHE = {}


def _rope_table(pos):
    half = 8
    inv = (np.float32(500000.0) ** (-np.arange(0, 16, 2, dtype=np.float32) / np.float32(16))).astype(np.float32)
    ang = pos.astype(np.float32)[:, None] * inv[None, :]
    return np.concatenate([np.cos(ang), np.sin(ang)], axis=1).astype(np.float32)


def kernel(x_prompt, x_sample, cache_a_k, cache_a_v, cache_b_k, cache_b_v, cache_b_logf,
           c_prompt, c_sample, norm_g, w_ada, b_ada, w_in, b_f, qn_a, kn_a,
           lam_q1, lam_k1, lam_q2, lam_k2, subln_g, qn_b, kn_b, w_out):
    f = lambda a: np.ascontiguousarray(np.asarray(a, dtype=np.float32))
    xp = f(x_prompt)[0]
    xs = f(x_sample)
    w_in0 = f(w_in)[0]
    o = np.cumsum([0, 512, 512, 512, 512, 512, 512, 512, 8, 512])
    seg = lambda i: w_in0[:, o[i]:o[i + 1]]
    w_kv = np.ascontiguousarray(np.concatenate([seg(1), seg(2), seg(5), seg(6), seg(7)], axis=1))
    w_qz = np.ascontiguousarray(np.concatenate([seg(0), seg(3), seg(4), seg(8)], axis=1))
    rep = lambda v, n=128: np.ascontiguousarray(np.broadcast_to(f(v).reshape(1, -1), (n, f(v).size)))
    vecs = np.ascontiguousarray(np.stack([rep(qn_a[0]), rep(kn_a[0]), rep(qn_b[0]), rep(kn_b[0])], axis=1))
    lamv = np.ascontiguousarray(np.stack([rep(lam_q1[0]), rep(lam_k1[0]), rep(lam_q2[0]), rep(lam_k2[0])], axis=1))
    bf_b = rep(b_f[0])
    subg = f(subln_g)[0].reshape(128, 1).copy()
    bf16 = ml_dtypes.bfloat16
    eye = np.eye(128, dtype=np.float32)
    ar = np.arange(128)
    Umat = (ar[:, None] <= ar[None, :]).astype(np.float32)
    E127 = np.zeros((128, 128), np.float32); E127[127, :] = 1.0
    Psw = np.zeros((128, 128), np.float32); Psw[ar, (ar + 64) % 128] = 1.0
    W1 = np.zeros((128, 256), np.float32); W1[:, 127] = 1.0
    sel2 = np.zeros((2, 2, 128), np.float32); sel2[0, 0, :] = 1.0; sel2[1, 1, :] = 1.0
    bsel = np.zeros((64, 128), np.float32); bsel[0, :] = 1.0; bsel[32, :] = 1.0
    tril = (ar[:, None] <= ar[None, :])
    chunk_ok = ((ar[:, None] // 64) <= (ar[None, :] // 64))
    maskS = (((np.arange(64)[:, None] <= np.arange(64)[None, :]).astype(np.float32) - 1.0) * 30000.0).astype(bf16)
    rope_all = _rope_table(np.arange(SEQ)).reshape(NT, 128, 16)
    rope_smp = _rope_table(PAST + np.arange(TDEC))
    xt = xp.reshape(NT, 128, D)
    in_maps = []
    for c in range(8):
        own = np.arange(NOWN) * 8 + c
        Lt = np.zeros((128, NOWN), np.float32)
        mA = np.zeros((128, 8, 128), np.float32)
        mB = np.zeros((128, 8, 128), np.float32)
        for t in range(NOWN):
            Lt[:8 * t + c, t] = 1.0
        for r in range(8):
            if r < c:
                mA[:, r, :] = 1.0; mB[:, r, :] = 1.0
            elif r == c:
                mA[:, r, :] = chunk_ok; mB[:, r, :] = tril
        cvT = np.stack([f(c_prompt)[0], f(c_sample)[c]], axis=1).reshape(8, 128, 2).transpose(1, 0, 2)
        in_maps.append(dict(
            x_all=xp, x_own=np.ascontiguousarray(xt[own].reshape(NOWN * 128, D)), x_smp=np.ascontiguousarray(xs[c]),
            cvT=np.ascontiguousarray(cvT), w_ada=f(w_ada)[0], b_ada2=np.ascontiguousarray(np.broadcast_to(f(b_ada)[0][None], (2, 3 * D))),
            g_row2=np.ascontiguousarray(np.broadcast_to(f(norm_g)[0][None], (2, D))),
            w_kv=w_kv, w_qz=w_qz, w_out=f(w_out)[0], vecs=vecs, lamv=lamv, bf_b=bf_b, subg=subg,
            c_ak=np.ascontiguousarray(f(cache_a_k)[0, c].reshape(PAST, 512)),
            c_av=np.ascontiguousarray(f(cache_a_v)[0, c].reshape(PAST, 512)),
            c_bk=np.ascontiguousarray(f(cache_b_k)[0, c].reshape(PAST, 512)),
            c_bv=np.ascontiguousarray(f(cache_b_v)[0, c].reshape(PAST, 512)),
            c_lf=np.ascontiguousarray(f(cache_b_logf)[0, c].reshape(PAST, 8)),
            rope_all=rope_all, rope_own=np.ascontiguousarray(rope_all[own]), rope_smp=rope_smp,
            k_identb=eye.astype(bf16), k_identf=eye, k_U=Umat, k_E127=E127, k_Psw=Psw, k_W1=W1, k_Lt=Lt, k_sel2=sel2, k_bsel=bsel,
            k_maskA=((mA - 1.0) * 30000.0).astype(bf16), k_maskB=((mB - 1.0) * 30000.0).astype(bf16), k_maskS=maskS,
        ))
    if "nc" not in _NC_CACHE:
        _NC_CACHE["nc"] = build_nc()
    res = run_bass_kernel_spmd(_NC_CACHE["nc"], in_maps, core_ids=list(range(8)))
    R = res.results

    def gather_p(key, width):
        out = np.zeros((NT, 128, width), np.float32)
        for c in range(8):
            out[np.arange(NOWN) * 8 + c] = np.asarray(R[c][key], dtype=np.float32).reshape(NOWN, 128, width)
        return out.reshape(SEQ, width)

    def gather_s(key, width):
        return np.stack([np.asarray(R[c][key], dtype=np.float32).reshape(TDEC, width) for c in range(8)], axis=0)

    y_p = gather_p("o_y", D).reshape(1, SEQ, D)
    y_s = gather_s("o_ys", D)
    return (y_p, y_s,
            gather_p("o_ak", 512).reshape(1, 1, SEQ, 4, 2, 64), gather_p("o_av", 512).reshape(1, 1, SEQ, 4, 128),
            gather_p("o_bk", 512).reshape(1, 1, SEQ, 8, 64), gather_p("o_bv", 512).reshape(1, 1, SEQ, 8, 64),
            gather_p("o_lf", 8).reshape(1, 1, SEQ, 8),
            gather_s("s_ak", 512).reshape(1, 8, TDEC, 4, 2, 64), gather_s("s_av", 512).reshape(1, 8, TDEC, 4, 128),
            gather_s("s_bk", 512).reshape(1, 8, TDEC, 8, 64), gather_s("s_bv", 512).reshape(1, 8, TDEC, 8, 64),
            gather_s("s_lf", 8).reshape(1, 8, TDEC, 8))
```

```python
import math
from contextlib import ExitStack

import numpy as np
import ml_dtypes
import concourse.bass as bass
import concourse.mybir as mybir
from concourse.bass_utils import run_bass_kernel_spmd

F32 = mybir.dt.float32
BF16 = mybir.dt.bfloat16
AF = mybir.ActivationFunctionType
ALU = mybir.AluOpType
AX = mybir.AxisListType

D = 1024
SEQ = 16384
NT = SEQ // 128
NOWN = 16
TDEC = 64
PAST = 1024
EPS = 1e-6
NKV = 2056
NQZ = 2048
LAM_INIT = 0.8 - 0.6 * math.exp(-0.3 * 0)
SNT = 9


class Sched:
    ROT = dict(ldc=8, ldw=2, ldx=4, ldr=2, ldm=1, ldk=4, stk=8, sto=8, stm=1, ldbr=1)

    def __init__(self):
        self.ops = []
        self.last_w = {}
        self.readers = {}
        self.nstream = {}

    @staticmethod
    def dom(op):
        return ("dma", op["stream"], op["slot"]) if op["stream"] is not None else op["eng"]

    def add(self, eng, fn, rd=(), wr=(), stream=None):
        idx = len(self.ops)
        deps = {}
        def need(j):
            d = self.dom(self.ops[j])
            if d not in deps or deps[d] < j:
                deps[d] = j
        if getattr(self, "serial", False) and idx > 0:
            need(idx - 1)
        for b in rd:
            if b in self.last_w:
                need(self.last_w[b])
        for b in wr:
            if b in self.last_w:
                need(self.last_w[b])
            for j in self.readers.get(b, {}).values():
                need(j)
        op = dict(eng=eng, fn=fn, deps=deps, stream=stream, sig=False, slot=None)
        if stream is not None:
            i = self.nstream.get(stream, 0)
            self.nstream[stream] = i + 1
            op["slot"] = i % self.ROT[stream]
        self.ops.append(op)
        mydom = self.dom(op)
        for b in wr:
            self.last_w[b] = idx
            self.readers[b] = {}
        for b in rd:
            if b in wr:
                continue
            self.readers.setdefault(b, {})[mydom] = idx

    def emit(self, nc, stack):
        ops = self.ops
        for op in ops:
            for d, j in op["deps"].items():
                if d == "pe" and op["eng"] == "pe" and op["stream"] is None:
                    continue
                ops[j]["sig"] = True
        cnt = {}
        for op in ops:
            d = self.dom(op)
            if op["stream"] is not None:
                cnt[d] = cnt.get(d, 0) + 16
                op["seq"] = cnt[d]
            elif op["sig"]:
                cnt[d] = cnt.get(d, 0) + 1
                op["seq"] = cnt[d]
        sems = {}
        for d in cnt:
            nm = "s_" + (d if isinstance(d, str) else "d_%s_%d" % (d[1], d[2]))
            sems[d] = stack.enter_context(nc.semaphore(nm))
        self.final = dict(cnt)
        engs = ["pe", "act", "dve", "pool", "sp"]
        per_eng = {e: [op for op in ops if op["eng"] == e] for e in engs}
        block = stack.enter_context(nc.Block())
        final = {d: v for d, v in cnt.items() if not isinstance(d, str)}

        def run(e, eobj):
            waited = {}
            for op in per_eng[e]:
                for d, j in op["deps"].items():
                    if d == "pe" and e == "pe" and op["stream"] is None:
                        continue
                    v = ops[j]["seq"]
                    if waited.get(d, 0) >= v:
                        continue
                    eobj.wait_ge(sems[d], v)
                    waited[d] = v
                if op["stream"] is not None and op["seq"] > 16:
                    d0 = self.dom(op)
                    if waited.get(d0, 0) < op["seq"] - 16:
                        eobj.wait_ge(sems[d0], op["seq"] - 16)
                        waited[d0] = op["seq"] - 16
                ins = op["fn"](eobj)
                if op["stream"] is not None:
                    ins.then_inc(sems[self.dom(op)], 16)
                elif op["sig"]:
                    ins.then_inc(sems[e], 1)
            if e == "sp":
                for d, v in final.items():
                    eobj.wait_ge(sems[d], v)

        @block.tensor
        def _(pe):
            run("pe", pe)

        @block.scalar
        def _(act):
            run("act", act)

        @block.vector
        def _(dve):
            run("dve", dve)

        @block.gpsimd
        def _(pool):
            run("pool", pool)

        @block.sync
        def _(sp):
            run("sp", sp)


def build_nc():
    nc = bass.Bass("TRN2", target_bir_lowering=False)
    S = Sched()
    st = ExitStack()

    def din(name, shape, dt=F32):
        return nc.dram_tensor(name, list(shape), dt, kind="ExternalInput").ap()

    def dout(name, shape, dt=F32):
        return nc.dram_tensor(name, list(shape), dt, kind="ExternalOutput").ap()

    def dscr(name, shape, dt):
        return nc.dram_tensor(name, list(shape), dt, kind="Internal").ap()

    x_all = din("x_all", [SEQ, D])
    x_own = din("x_own", [NOWN * 128, D])
    x_smp = din("x_smp", [TDEC, D])
    cvT = din("cvT", [128, 8, 2])
    w_ada = din("w_ada", [D, 3 * D])
    b_ada2 = din("b_ada2", [2, 3 * D])
    g_row2 = din("g_row2", [2, D])
    w_kv = din("w_kv", [D, NKV])
    w_qz = din("w_qz", [D, NQZ])
    w_out = din("w_out", [D, D])
    vecs = din("vecs", [128, 4, 64])
    lamv = din("lamv", [128, 4, 64])
    bf_b = din("bf_b", [128, 8])
    subg = din("subg", [128, 1])
    c_ak = din("c_ak", [PAST, 512])
    c_av = din("c_av", [PAST, 512])
    c_bk = din("c_bk", [PAST, 512])
    c_bv = din("c_bv", [PAST, 512])
    c_lf = din("c_lf", [PAST, 8])
    rope_all = din("rope_all", [NT, 128, 16])
    rope_own = din("rope_own", [NOWN, 128, 16])
    rope_smp = din("rope_smp", [TDEC, 16])
    k_identb = din("k_identb", [128, 128], BF16)
    k_identf = din("k_identf", [128, 128])
    k_U = din("k_U", [128, 128])
    k_E127 = din("k_E127", [128, 128])
    k_Psw = din("k_Psw", [128, 128])
    k_W1 = din("k_W1", [128, 256])
    k_Lt = din("k_Lt", [128, NOWN])
    k_sel2 = din("k_sel2", [2, 2, 128])
    k_bsel = din("k_bsel", [64, 128])
    k_maskA = din("k_maskA", [128, 8, 128], BF16)
    k_maskB = din("k_maskB", [128, 8, 128], BF16)
    k_maskS = din("k_maskS", [64, 64], BF16)

    o_y = dout("o_y", [NOWN * 128, D])
    o_ys = dout("o_ys", [TDEC, D])
    o_ak = dout("o_ak", [NOWN * 128, 512])
    o_av = dout("o_av", [NOWN * 128, 512])
    o_bk = dout("o_bk", [NOWN * 128, 512])
    o_bv = dout("o_bv", [NOWN * 128, 512])
    o_lf = dout("o_lf", [NOWN * 128, 8])
    s_ak = dout("s_ak", [TDEC, 512])
    s_av = dout("s_av", [TDEC, 512])
    s_bk = dout("s_bk", [TDEC, 512])
    s_bv = dout("s_bv", [TDEC, 512])
    s_lf = dout("s_lf", [TDEC, 8])

    def scr(prefix, ntl):
        return dict(
            KTA=dscr(prefix + "KTA", [4, 128, ntl * 128], BF16),
            VA=dscr(prefix + "VA", [4, 128, ntl, 128], BF16),
            KTB=dscr(prefix + "KTB", [8, 70, ntl * 128], BF16),
            VB=dscr(prefix + "VB", [4, 128, ntl, 192], BF16),
            pre=prefix,
        )
    modscr = dscr("modscr", [2, 3 * D], F32)
    SCP = scr("p", NT)
    SCS = scr("s", SNT)

    def sb(name, shape, dt=F32):
        return st.enter_context(nc.sbuf_tensor(name, list(shape), dt))

    identb = sb("identb", [128, 128], BF16)
    identf = sb("identf", [128, 128])
    Um = sb("Um", [128, 128])
    E127 = sb("E127", [128, 128])
    Psw = sb("Psw", [128, 128])
    W1 = sb("W1", [128, 256])
    Lt = sb("Lt", [128, NOWN])
    TSm = sb("TSm", [128, 8])
    sel2 = sb("sel2", [2, 2, 128])
    bsel = sb("bsel", [64, 128])
    Lsb = sb("Lsb", [64, 512])
    maskA = sb("maskA", [128, 8, 128], BF16)
    maskB = sb("maskB", [128, 8, 128], BF16)
    maskS = sb("maskS", [64, 64], BF16)
    ones_bf = sb("ones_bf", [128, 128], BF16)
    ones_f = sb("ones_f", [128, 128])
    vecs_sb = sb("vecs_sb", [128, 4, 64])
    lam_sb = sb("lam_sb", [128, 4, 64])
    bf_sb = sb("bf_sb", [128, 8])
    subg_sb = sb("subg_sb", [128, 1])
    small = sb("small", [128, 16])
    Wkv = sb("Wkv", [128, 8, NKV], BF16)
    Wqz = sb("Wqz", [128, 8, NQZ], BF16)
    Wout = sb("Wout", [128, 8, D], BF16)
    ZT = sb("ZT", [128, 8, 512])
    ZTf = ZT[:, :, :].rearrange("p a b -> p (a b)")
    wst = [ZTf[:, 0:2048], ZTf[:, 2048:4096]]
    wtail = sb("wtail", [128, 8, 8])
    gmod_b = sb("gmod_b", [128, D])
    shift_b = sb("shift_b", [128, D])
    gate_b = sb("gate_b", [128, D])
    mod_sb = ZTf[0:2, 0:3 * D]
    scT = sb("scT", [128, 8, 2])
    scT2 = sb("scT2", [128, 8, 2])

    xin = [sb("xin0", [128, D]), sb("xin1", [128, D])]
    sqj = sb("sqj", [128, D])
    xbf = sb("xbf", [128, D], BF16)
    xT = sb("xT", [128, 8, 128], BF16)
    KAB = sb("KAB", [128, 2, 512])
    KA = KAB[:, 0, :]
    KB = KAB[:, 1, :]
    VAf = sb("VAf", [128, 512])
    VBf = sb("VBf", [128, 512])
    rtab = sb("rtab", [128, 2, 16])
    rtmp = sb("rtmp", [128, 2, 4, 8, 8])
    st8 = sb("st8", [128, 2, 12, 8])
    sm2 = sb("sm2", [128, 2, 4])
    cumq_sb = sb("cumq_sb", [128, 8])
    logf = sb("logf", [128, 2, 8])
    cum = [sb("cum0", [128, 8]), sb("cum1", [128, 8])]
    TSs = sb("TSs", [128, 8])
    spl = sb("spl", [128, 2, 6, 8])
    splb = sb("splb", [128, 2, 3, 8], BF16)
    kabf = sb("kabf", [128, 512], BF16)
    Ktm = sb("Ktm", [128, 8, 70], BF16)
    Qtm = sb("Qtm", [128, 8, 70], BF16)
    KTas = sb("KTas", [128, 4, 128], BF16)
    KTbs = sb("KTbs", [70, 8, 128], BF16)
    vas = sb("vas", [128, 4, 128], BF16)
    vbs = sb("vbs", [128, 4, 192], BF16)
    QTA = sb("QTA", [128, 4, 512], BF16)
    QTB = sb("QTB", [70, 8, 512], BF16)
    GT = sb("GT", [128, 8, 512], BF16)
    zs = sb("zs", [128, D])
    kbuf = [sb("kbuf0", [128, 2, 1152], BF16), sb("kbuf1", [128, 2, 1152], BF16)]
    vbuf = [sb("vbuf0", [128, SNT, 192], BF16), sb("vbuf1", [128, SNT, 192], BF16)]
    Pt = [sb("Pt0", [128, 2, 512], BF16), sb("Pt1", [128, 2, 512], BF16)]
    mt = [KA, VAf, KB, VBf]
    mtk = ["KA", "VAf", "KB", "VBf"]

    psS = st.enter_context(nc.psum_tensor("psS", [128, 4, 512], F32))
    ps = [psS[:, i, :] for i in range(4)] + [st.enter_context(nc.psum_tensor("ps%d" % i, [128, 512], F32)) for i in range(4, 8)]

    def psb(i):
        return ps[i][:, :].bitcast(BF16)

    def _sn(stream, ap):
        return stream

    def dma_in(out, in_, wr, stream="ld", rd=()):
        S.add("sp", lambda e, o=out, i=in_: e.dma_start(out=o, in_=i), rd=rd, wr=wr, stream=_sn(stream, out))

    def dma_out(out, in_, rd, wr=(), stream="st"):
        S.add("sp", lambda e, o=out, i=in_: e.dma_start(out=o, in_=i), rd=rd, wr=wr, stream=_sn(stream, in_))

    def A(fn, rd, wr):
        S.add("act", fn, rd, wr)

    def V(fn, rd, wr):
        S.add("dve", fn, rd, wr)

    def G(fn, rd, wr):
        S.add("pool", fn, rd, wr)

    def PE(fn, rd, wr):
        S.add("pe", fn, rd, wr)

    def act(out, in_, func, rd, wr, bias=0.0, scale=1.0, accum=None):
        A(lambda e: e.activation(out=out, in_=in_, func=func, bias=bias, scale=scale, accum_out=accum), rd, wr)

    def mm(out, lhsT, rhs, start, stop, rd, wr):
        PE(lambda e: e.matmul(out, lhsT, rhs, start=start, stop=stop), rd, wr)

    def tr(out, in_, ident, rd, wr):
        PE(lambda e: e.transpose(out, in_, ident), rd, wr)

    ctoks = []
    wtoks = []
    for t_, d_ in ((identb, k_identb), (identf, k_identf), (Um, k_U), (E127, k_E127), (Psw, k_Psw),
                   (W1, k_W1), (sel2, k_sel2), (bsel, k_bsel), (maskA, k_maskA), (maskB, k_maskB), (maskS, k_maskS),
                   (vecs_sb, vecs), (lam_sb, lamv), (bf_sb, bf_b), (subg_sb, subg), (Lt, k_Lt)):
        sl = tuple(slice(None) for _ in range(len(d_.shape)))
        ctoks.append("c_%d" % len(ctoks))
        dma_in(t_[sl], d_[sl], wr=[ctoks[-1]], stream="ldc")
    V(lambda e: e.memset(ones_bf[:, :], 1.0), [], ["c_ones"])
    V(lambda e: e.memset(ones_f[:, :], 1.0), ["c_ones"], ["c_ones"])
    V(lambda e: e.memset(small[:, 13:14], 0.0), ctoks + ["c_ones"], ["const"])
    V(lambda e: e.memset(Ktm[:, :, 64:67], 1.0), [], ["Ktm"])
    V(lambda e: e.memset(Qtm[:, :, 67:70], 1.0), [], ["Qtm"])
    V(lambda e: e.memset(vbs[:, :, 64:128], 1.0), [], ["vbs"])
    V(lambda e: e.memset(TSs[:, :], 0.0), [], ["TSs"])

    V(lambda e: e.tensor_tensor(out=sqj[:, 0:64], in0=lam_sb[:, 0, :], in1=lam_sb[:, 1, :], op=ALU.mult), ["const"], ["sqj"])
    V(lambda e: e.tensor_tensor(out=sqj[:, 64:128], in0=lam_sb[:, 2, :], in1=lam_sb[:, 3, :], op=ALU.mult), ["const"], ["sqj"])
    V(lambda e: e.reduce_sum(out=small[:, 2:4], in_=sqj[:, 0:128].rearrange("p (a b) -> p a b", a=2), axis=AX.X), ["sqj"], ["small"])
    act(small[:, 4:6], small[:, 2:4], AF.Exp, ["small"], ["small2"])
    V(lambda e: e.tensor_tensor(out=small[:, 6:7], in0=small[:, 4:5], in1=small[:, 5:6], op=ALU.subtract), ["small2"], ["small3"])
    V(lambda e: e.tensor_scalar(out=small[:, 1:2], in0=small[:, 6:7], scalar1=LAM_INIT, scalar2=-1.0, op0=ALU.add, op1=ALU.mult), ["small3"], ["neglam"])
    V(lambda e: e.tensor_scalar(out=small[:, 7:8], in0=subg_sb[:, 0:1], scalar1=1.0 - LAM_INIT, scalar2=None, op0=ALU.mult), ["const"], ["subg2"])

    slot = 0
    dma_in(wtail[:, :, :], w_kv[:, 2048:2056].rearrange("(c p) n -> p c n", p=128), wr=["wtail"], stream="ldc")
    V(lambda e: e.tensor_copy(out=Wkv[:, :, 2048:2056], in_=wtail[:, :, :]), ["wtail"], ["w_tail"])
    wtoks.append("w_tail")
    for (Wt, wsrc, ncol) in ((Wkv, w_kv, 2048), (Wqz, w_qz, NQZ), (Wout, w_out, D)):
        for ch in range(8):
            s_ = slot % 2
            slot += 1
            dma_in(wst[s_][:, 0:ncol], wsrc[ch * 128:(ch + 1) * 128, 0:ncol], wr=["wst%d" % s_], stream="ldw")
            wtoks.append("w_%d" % len(wtoks))
            if ch % 2 == 0:
                V(lambda e, Wt=Wt, ch=ch, s_=s_, ncol=ncol: e.tensor_copy(out=Wt[:, ch, 0:ncol], in_=wst[s_][:, 0:ncol]), ["wst%d" % s_], [wtoks[-1]])
            else:
                A(lambda e, Wt=Wt, ch=ch, s_=s_, ncol=ncol: e.copy(out=Wt[:, ch, 0:ncol], in_=wst[s_][:, 0:ncol]), ["wst%d" % s_], [wtoks[-1]])

    dma_in(scT[:, :, :], cvT[:, :, :], wr=["scT"], stream="ldc")
    act(scT2[:, :, :], scT[:, :, :], AF.Exp, ["scT"], ["scT2"], scale=-1.0)
    V(lambda e: e.tensor_scalar(out=scT2[:, :, :], in0=scT2[:, :, :], scalar1=1.0, scalar2=None, op0=ALU.add), ["scT2"], ["scT2"])
    V(lambda e: e.reciprocal(out=scT2[:, :, :], in_=scT2[:, :, :]), ["scT2"], ["scT2"])
    V(lambda e: e.tensor_tensor(out=scT[:, :, :], in0=scT[:, :, :], in1=scT2[:, :, :], op=ALU.mult), ["scT2", "scT"], ["scT"])
    for ch in range(8):
        for half in range(2):
            s_ = slot % 2
            slot += 1
            dma_in(wst[s_][:, 0:1536], w_ada[ch * 128:(ch + 1) * 128, half * 1536:(half + 1) * 1536],
                   wr=["wst%d" % s_], stream="ldw")
            for g in range(3):
                mm(ps[half * 3 + g][0:2, :], scT[:, ch, :], wst[s_][:, g * 512:(g + 1) * 512],
                   ch == 0, ch == 7, ["scT", "wst%d" % s_], ["ps%d" % (half * 3 + g)])
    bst = [xin[0], xin[1], zs]
    for i in range(3):
        dma_in(bst[i][0:2, :], b_ada2[:, i * D:(i + 1) * D], wr=["bst%d" % i], stream="ldc")
    dma_in(sqj[0:2, :], g_row2[:, :], wr=["sqj"], stream="ldc")
    for g in range(6):
        V(lambda e, g=g: e.tensor_tensor(out=mod_sb[:, g * 512:(g + 1) * 512], in0=ps[g][0:2, :],
                                         in1=bst[g // 2][0:2, (g % 2) * 512:(g % 2 + 1) * 512], op=ALU.add),
          ["ps%d" % g, "bst%d" % (g // 2)], ["mod_sb", "wst0", "wst1"])
    V(lambda e: e.scalar_tensor_tensor(out=mod_sb[:, D:2 * D], in0=mod_sb[:, D:2 * D], scalar=1.0, in1=sqj[0:2, :],
                                       op0=ALU.add, op1=ALU.mult), ["mod_sb", "sqj"], ["mod_sb"])
    dma_out(modscr[:, :], mod_sb, rd=["mod_sb"], wr=["modscr"], stream="stm")
    S.add("sp", lambda e: e.dma_start(out=small[:, 11:12], in_=subg[:, 0:1]), ["bst0", "bst1", "bst2", "sqj", "wst0", "wst1", "modscr"] + wtoks,
          ["xin0", "xin1", "zs0", "zs1", "ZT", "W"], stream="ldbr")

    def set_mod(r):
        for i, dst in enumerate((shift_b, gmod_b, gate_b)):
            dma_in(dst[:, :], modscr[r:r + 1, i * D:(i + 1) * D].to_broadcast([128, D]), wr=["modb"], rd=["modscr"], stream="ldm")

    def flat(t3):
        return t3[:, :, :].rearrange("p a b -> p (a b)")
    k0f = flat(kbuf[0]).bitcast(F32)
    k1f = flat(kbuf[1])
    v0f = flat(vbuf[0])
    v1f = flat(vbuf[1])
    gtf = flat(GT).bitcast(F32)
    p0f = flat(Pt[0])
    p1f = flat(Pt[1])
    BUF = [
        dict(xin=xin[0], sqj=sqj[:, :], xbf=xbf[:, :], xT=xT[:, :, :], KA=KA[:, :], VAf=VAf[:, :], KB=KB[:, :], VBf=VBf[:, :],
             KAB=KAB[:, :, :].rearrange("p a n -> p (a n)"),
             kabf=kabf[:, :], KTas=KTas[:, :, :], KTbs=KTbs[:, :, :], vas=vas[:, :, :], vbs=vbs[:, :, :], Ktm=Ktm[:, :, :]),
        dict(xin=xin[1], sqj=k0f[:, 0:1024], xbf=p0f, xT=p1f.rearrange("p (c t) -> p c t", c=8),
             KA=gtf[:, 0:512], KB=gtf[:, 512:1024], VAf=gtf[:, 1024:1536], VBf=gtf[:, 1536:2048], KAB=gtf[:, 0:1024],
             kabf=k1f[:, 0:512], KTas=k1f[:, 512:1024].rearrange("p (h t) -> p h t", h=4),
             vas=k1f[:, 1024:1536].rearrange("p (h e) -> p h e", h=4),
             Ktm=k1f[:, 1536:2096].rearrange("p (h d) -> p h d", h=8),
             KTbs=v0f[0:70, 0:1024].rearrange("p (h t) -> p h t", h=8),
             vbs=v1f[:, 0:768].rearrange("p (h e) -> p h e", h=4)),
    ]
    FAM_A = ["kbuf0", "kbuf1", "vbuf0", "vbuf1", "Pt0", "Pt1", "GT"]
    FAM_B = [n_ + "1" for n_ in ("sqj", "xbf", "xT", "KA", "VAf", "KB", "VBf", "kabf", "KTas", "KTbs", "vas", "vbs", "Ktm")]

    def tok(n_, s_):
        return n_ if s_ == 0 else n_ + "1"

    def bridge(to_proj):
        V(lambda e: e.memset(small[:, 12:13], 0.0), [], FAM_A + FAM_B)
        if to_proj:
            V(lambda e: e.memset(BUF[1]["Ktm"][:, :, 64:67], 1.0), [], ["Ktm1"])
            V(lambda e: e.memset(BUF[1]["vbs"][:, :, 64:128], 1.0), [], ["vbs1"])

    def stageL(job):
        if job["kind"] == "cache":
            return
        s_ = job["s"]
        dma_in(BUF[s_]["xin"][0:job["P"], :], job["xsrc"], wr=["xin%d" % s_], stream="ldx")
        job["xloaded"] = True

    def stageF(job):
        s_ = job["s"]
        B = BUF[s_]
        P = job["P"]
        if job["kind"] == "cache":
            r0 = job["r0"]
            dma_in(B["KA"][:, :], c_ak[r0:r0 + 128, :], wr=[tok("KA", s_)], stream="ldx")
            dma_in(B["VAf"][:, :], c_av[r0:r0 + 128, :], wr=[tok("VAf", s_)], stream="ldx")
            dma_in(B["KB"][:, :], c_bk[r0:r0 + 128, :], wr=[tok("KB", s_)], stream="ldx")
            dma_in(B["VBf"][:, :], c_bv[r0:r0 + 128, :], wr=[tok("VBf", s_)], stream="ldx")
            dma_in(logf[:, s_, :], c_lf[r0:r0 + 128, :], wr=[tok("logf", s_)], stream="ldx")
            return
        X = B["xin"]
        tk = "xin%d" % s_
        SQ = B["sqj"]
        if not job.get("xloaded"):
            stageL(job)
        dma_in(rtab[0:P, s_, :], job["rsrc"], wr=[tok("rtab", s_)], stream="ldr")
        act(SQ[0:P, :], X[0:P, :], AF.Square, [tk], [tok("sqj", s_), tok("ssq", s_)], accum=sm2[0:P, s_, 0:1])
        act(sm2[0:P, s_, 1:2], sm2[0:P, s_, 0:1], AF.Ln, [tok("ssq", s_)], [tok("ssq2", s_)], bias=EPS, scale=1.0 / D)
        act(sm2[0:P, s_, 2:3], sm2[0:P, s_, 1:2], AF.Exp, [tok("ssq2", s_)], [tok("rstd", s_)], scale=-0.5)
        V(lambda e: e.scalar_tensor_tensor(out=SQ[0:P, :], in0=X[0:P, :], scalar=sm2[0:P, s_, 2:3], in1=gmod_b[0:P, :],
                                           op0=ALU.mult, op1=ALU.mult), [tk, tok("rstd", s_), "modb", tok("sqj", s_)], [tok("sqj", s_)])
        G(lambda e: e.tensor_tensor(out=B["xbf"][0:P, :], in0=SQ[0:P, :], in1=shift_b[0:P, :], op=ALU.add),
          [tok("sqj", s_), "modb"], [tok("xbf", s_)])

    def stageF2(job):
        if job["kind"] == "cache":
            return
        s_ = job["s"]
        B = BUF[s_]
        P = job["P"]
        xTp = psb(0)
        for ch in range(8):
            tr(xTp[:, ch * 128:ch * 128 + P], B["xbf"][0:P, ch * 128:(ch + 1) * 128], identb[0:P, 0:P], [tok("xbf", s_), "const"], ["ps0"])
        A(lambda e: e.copy(out=B["xT"][:, :, 0:P], in_=xTp.rearrange("p (c t) -> p c t", c=8)[:, :, 0:P]), ["ps0"], [tok("xT", s_)])

    def stageM(job):
        if job["kind"] == "cache":
            return
        s_ = job["s"]
        B = BUF[s_]
        P = job["P"]
        for (Wt, c0, ncol, bank) in job["groups"]:
            for ch in range(8):
                mm(ps[bank][0:P, 0:ncol], B["xT"][:, ch, 0:P], Wt[:, ch, c0:c0 + ncol], ch == 0, ch == 7,
                   [tok("xT", s_), "W"], ["ps%d" % bank])

    def qk_norm(dst, dtok, bank, P, gain_idx, s_):
        pv = ps[bank]
        SQ = BUF[s_]["sqj"]
        act(SQ[0:P, 0:512], pv[0:P, :], AF.Square, ["ps%d" % bank], [tok("sqj", s_)])
        V(lambda e: e.reduce_sum(out=st8[0:P, s_, 0, :], in_=SQ[0:P, 0:512].rearrange("p (g d) -> p g d", g=8), axis=AX.X),
          [tok("sqj", s_)], [tok("st8a", s_)])
        act(st8[0:P, s_, 1, :], st8[0:P, s_, 0, :], AF.Ln, [tok("st8a", s_)], [tok("st8b", s_)], bias=EPS, scale=1.0 / 64)
        act(st8[0:P, s_, 2, :], st8[0:P, s_, 1, :], AF.Exp, [tok("st8b", s_)], [tok("st8c", s_)], scale=-0.5)
        V(lambda e: e.tensor_tensor(out=dst[0:P, :].rearrange("p (g d) -> p g d", g=8),
                                    in0=pv[0:P, :].rearrange("p (g d) -> p g d", g=8),
                                    in1=st8[0:P, s_, 2, :].unsqueeze(2).to_broadcast([P, 8, 64]), op=ALU.mult),
          ["ps%d" % bank, tok("st8c", s_)], [dtok])
        G(lambda e: e.tensor_tensor(out=dst[0:P, :].rearrange("p (g d) -> p g d", g=8),
                                    in0=dst[0:P, :].rearrange("p (g d) -> p g d", g=8),
                                    in1=vecs_sb[0:P, gain_idx, :].unsqueeze(1).to_broadcast([P, 8, 64]), op=ALU.mult),
          [dtok, "const"], [dtok])

    def qk_norm2(P, gain_a, gain_b, s_):
        B = BUF[s_]
        SQ = B["sqj"]
        pv = psS[0:P, 1:3, :].rearrange("p a n -> p (a n)")
        dst = B["KAB"]
        ta, tb = tok("KA", s_), tok("KB", s_)
        act(SQ[0:P, :], pv, AF.Square, ["ps1", "ps2"], [tok("sqj", s_)])
        V(lambda e: e.reduce_sum(out=st8[0:P, s_, 0:2, :].rearrange("p a g -> p (a g)"),
                                 in_=SQ[0:P, :].rearrange("p (g d) -> p g d", g=16), axis=AX.X),
          [tok("sqj", s_)], [tok("st8a", s_)])
        act(st8[0:P, s_, 2:4, :], st8[0:P, s_, 0:2, :], AF.Ln, [tok("st8a", s_)], [tok("st8b", s_)], bias=EPS, scale=1.0 / 64)
        act(st8[0:P, s_, 4:6, :], st8[0:P, s_, 2:4, :], AF.Exp, [tok("st8b", s_)], [tok("st8c", s_)], scale=-0.5)
        V(lambda e: e.tensor_tensor(out=dst[0:P, :].rearrange("p (g d) -> p g d", g=16),
                                    in0=pv.rearrange("p (g d) -> p g d", g=16),
                                    in1=st8[0:P, s_, 4:6, :].rearrange("p a g -> p (a g)").unsqueeze(2).to_broadcast([P, 16, 64]),
                                    op=ALU.mult),
          ["ps1", "ps2", tok("st8c", s_)], [ta, tb])
        for (dd, dt_, gi) in ((B["KA"], ta, gain_a), (B["KB"], tb, gain_b)):
            G(lambda e, dd=dd, gi=gi: e.tensor_tensor(out=dd[0:P, :].rearrange("p (g d) -> p g d", g=8),
                                                      in0=dd[0:P, :].rearrange("p (g d) -> p g d", g=8),
                                                      in1=vecs_sb[0:P, gi, :].unsqueeze(1).to_broadcast([P, 8, 64]), op=ALU.mult),
              [dt_, "const"], [dt_])

    def rope(dst, tk, P, rsrc, s_):
        dv = dst[0:P, :].rearrange("p (g d) -> p g d", g=8)
        x1 = dv[:, :, 0:8]
        x2 = dv[:, :, 8:16]
        cs = rtab[0:P, s_, 0:8].unsqueeze(1).to_broadcast([P, 8, 8])
        sn = rtab[0:P, s_, 8:16].unsqueeze(1).to_broadcast([P, 8, 8])
        rt = tok("rtmp", s_)
        rtk = tok("rtab", s_)
        V(lambda e: e.tensor_tensor(out=rtmp[0:P, s_, 0], in0=x1, in1=cs, op=ALU.mult), [tk, rtk], [rt])
        V(lambda e: e.tensor_tensor(out=rtmp[0:P, s_, 1], in0=x2, in1=sn, op=ALU.mult), [tk, rtk], [rt])
        V(lambda e: e.tensor_tensor(out=rtmp[0:P, s_, 2], in0=x2, in1=cs, op=ALU.mult), [tk, rtk], [rt])
        V(lambda e: e.tensor_tensor(out=rtmp[0:P, s_, 3], in0=x1, in1=sn, op=ALU.mult), [tk, rtk], [rt])
        V(lambda e: e.tensor_tensor(out=x1, in0=rtmp[0:P, s_, 0], in1=rtmp[0:P, s_, 1], op=ALU.subtract), [rt], [tk])
        V(lambda e: e.tensor_tensor(out=x2, in0=rtmp[0:P, s_, 2], in1=rtmp[0:P, s_, 3], op=ALU.add), [rt], [tk])

    def split3(src, P, dsttile, dtok, c0, scale, stok, s_):
        sp_ = spl[:, s_]
        t1, t2 = tok("spl1", s_), tok("spl2", s_)
        hi = dsttile[0:P, :, c0:c0 + 1]
        mid = dsttile[0:P, :, c0 + 1:c0 + 2]
        lo = dsttile[0:P, :, c0 + 2:c0 + 3]
        sv = src.unsqueeze(2)
        V(lambda e: e.tensor_scalar(out=hi, in0=sv, scalar1=scale, scalar2=None, op0=ALU.mult), [stok], [dtok])
        V(lambda e: e.scalar_tensor_tensor(out=sp_[0:P, 1, :].unsqueeze(2), in0=sv, scalar=scale, in1=hi, op0=ALU.mult, op1=ALU.subtract),
          [stok, dtok], [t1])
        V(lambda e: e.tensor_copy(out=mid, in_=sp_[0:P, 1, :].unsqueeze(2)), [t1], [dtok])
        V(lambda e: e.tensor_tensor(out=sp_[0:P, 2, :].unsqueeze(2), in0=sp_[0:P, 1, :].unsqueeze(2), in1=mid, op=ALU.subtract), [t1, dtok], [t2])
        V(lambda e: e.tensor_copy(out=lo, in_=sp_[0:P, 2, :].unsqueeze(2)), [t2], [dtok])

    def pack_kv_cum(kt, P, first, s_):
        B = BUF[s_]
        lg = logf[:, s_, :]
        cprev = cum[(kt + 1) % 2]
        ccur = cum[kt % 2]
        mm(ps[5][0:P, 32:40], Um[0:P, 0:P], lg[0:P, :], True, first, ["const", tok("logf", s_)], ["ps5"])
        if not first:
            mm(ps[5][0:P, 32:40], E127[:, 0:P], cprev[:, :], False, True, ["const", "cum%d" % ((kt + 1) % 2)], ["ps5"])
        V(lambda e: e.tensor_copy(out=ccur[0:P, :], in_=ps[5][0:P, 32:40]), ["ps5"], ["cum%d" % (kt % 2)])
        split3(ccur[0:P, :], P, B["Ktm"], tok("Ktm", s_), 67, -8.0, "cum%d" % (kt % 2), s_)

    def pack_kv_cast(P, s_):
        B = BUF[s_]
        A(lambda e: e.copy(out=B["kabf"][0:P, :], in_=B["KA"][0:P, :]), [tok("KA", s_)], [tok("kabf", s_)])
        A(lambda e: e.copy(out=B["vas"][0:P, :, :], in_=B["VAf"][0:P, :].rearrange("p (h e) -> p h e", h=4)),
          [tok("VAf", s_)], [tok("vas", s_)])
        A(lambda e: e.copy(out=B["Ktm"][0:P, :, 0:64], in_=B["KB"][0:P, :].rearrange("p (h d) -> p h d", h=8)),
          [tok("KB", s_)], [tok("Ktm", s_)])
        vv = B["VBf"][0:P, :].rearrange("p (q a d) -> p q a d", q=4, a=2)
        G(lambda e: e.tensor_copy(out=B["vbs"][0:P, :, 0:64], in_=vv[:, :, 0, :]), [tok("VBf", s_)], [tok("vbs", s_)])
        G(lambda e: e.tensor_copy(out=B["vbs"][0:P, :, 128:192], in_=vv[:, :, 1, :]), [tok("VBf", s_)], [tok("vbs", s_)])

    def pack_kv_b(SC, kt, P, first, s_):
        pre = SC["pre"]
        B = BUF[s_]
        kp = psb(6)
        for h in range(4):
            tr(kp[:, h * 128:h * 128 + P], B["kabf"][0:P, h * 128:(h + 1) * 128], identb[0:P, 0:P], [tok("kabf", s_), "const"], ["ps6"])
        V(lambda e: e.tensor_copy(out=B["KTas"][:, :, 0:P], in_=kp[:, 0:512].rearrange("p (h t) -> p h t", h=4)[:, :, 0:P]),
          ["ps6"], [tok("KTas", s_)])
        dma_out(SC["KTA"][:, :, kt * 128:kt * 128 + P].rearrange("h p t -> p h t"), B["KTas"][:, :, 0:P], rd=[tok("KTas", s_)],
                wr=[pre + "KTA%d" % kt], stream="stk")
        dma_out(SC["VA"][:, 0:P, kt, :].rearrange("h p e -> p h e"), B["vas"][0:P, :, :], rd=[tok("vas", s_)], wr=[pre + "VA%d" % kt], stream="stk")
        kp2 = psb(7)
        for h in range(8):
            tr(kp2[0:70, h * 128:h * 128 + P], B["Ktm"][0:P, h, :], identb[0:P, 0:P], [tok("Ktm", s_), "const"], ["ps7"])
        V(lambda e: e.tensor_copy(out=B["KTbs"][:, :, 0:P], in_=kp2[0:70, :].rearrange("p (h t) -> p h t", h=8)[:, :, 0:P]),
          ["ps7"], [tok("KTbs", s_)])
        dma_out(SC["KTB"][:, :, kt * 128:kt * 128 + P].rearrange("h p t -> p h t"), B["KTbs"][:, :, 0:P], rd=[tok("KTbs", s_)],
                wr=[pre + "KTB%d" % kt], stream="stk")
        dma_out(SC["VB"][:, 0:P, kt, :].rearrange("h p e -> p h e"), B["vbs"][0:P, :, :], rd=[tok("vbs", s_)], wr=[pre + "VB%d" % kt], stream="stk")

    G_KV = [(Wkv, 0, 512, 1), (Wkv, 1024, 512, 2), (Wkv, 512, 512, 3), (Wkv, 1536, 512, 4), (Wkv, 2048, 8, 5)]
    G_QZ = [(Wqz, 0, 512, 1), (Wqz, 1024, 512, 2), (Wqz, 512, 512, 3), (Wqz, 1536, 512, 4)]

    def stageP1(job, part):
        s_ = job["s"]
        B = BUF[s_]
        P = job["P"]
        if job["kind"] == "cache":
            return
        if part == "a":
            qk_norm2(P, 1, 3, s_) if job["kind"] == "kv" else qk_norm2(P, 0, 2, s_)
            return
        if job["kind"] == "kv":
            A(lambda e: e.copy(out=B["VAf"][0:P, :], in_=ps[3][0:P, :]), ["ps3"], [tok("VAf", s_)])
            A(lambda e: e.copy(out=B["VBf"][0:P, :], in_=ps[4][0:P, :]), ["ps4"], [tok("VBf", s_)])
            V(lambda e: e.tensor_tensor(out=st8[0:P, s_, 6, :], in0=ps[5][0:P, 0:8], in1=bf_sb[0:P, :], op=ALU.add),
              ["ps5", "const"], [tok("st8d", s_)])
        else:
            for half, bank in ((0, 3), (1, 4)):
                zc = zs[0:P, half * 512:(half + 1) * 512]
                act(zc, ps[bank][0:P, :], AF.Exp, ["ps%d" % bank], ["zs%d" % half], scale=-1.0)
                act(zc, zc, AF.Ln, ["zs%d" % half], ["zs%d" % half], bias=1.0)
                act(zc, zc, AF.Exp, ["zs%d" % half], ["zs%d" % half], scale=-1.0)
                V(lambda e, zc=zc, bank=bank: e.tensor_tensor(out=zc, in0=ps[bank][0:P, :], in1=zc, op=ALU.mult),
                  ["zs%d" % half, "ps%d" % bank], ["zs%d" % half])

    def stageP2a(job):
        s_ = job["s"]
        B = BUF[s_]
        P = job["P"]
        lg = logf[:, s_, :]
        if job["kind"] in ("kv", "cache"):
            if job["kind"] == "kv":
                act(st8[0:P, s_, 7, :], st8[0:P, s_, 6, :], AF.Exp, [tok("st8d", s_)], [tok("st8e", s_)], scale=-1.0)
                act(st8[0:P, s_, 8, :], st8[0:P, s_, 7, :], AF.Ln, [tok("st8e", s_)], [tok("st8f", s_)], bias=1.0)
                V(lambda e: e.tensor_scalar(out=lg[0:P, :], in0=st8[0:P, s_, 8, :], scalar1=-1.0, scalar2=None, op0=ALU.mult),
                  [tok("st8f", s_)], [tok("logf", s_)])
                rope(B["KA"], tok("KA", s_), P, job["rsrc"], s_)
            if job.get("SC") is not None:
                pack_kv_cast(P, s_)
            if job.get("outs") is not None:
                dak, dav, dbk, dbv, dlf = job["outs"]
                row0 = job["row0"]
                dma_out(dak[row0:row0 + P, :], B["KA"][0:P, :], rd=[tok("KA", s_)], stream="sto")
                dma_out(dav[row0:row0 + P, :], B["VAf"][0:P, :], rd=[tok("VAf", s_)], stream="sto")
                dma_out(dbk[row0:row0 + P, :], B["KB"][0:P, :], rd=[tok("KB", s_)], stream="sto")
                dma_out(dbv[row0:row0 + P, :], B["VBf"][0:P, :], rd=[tok("VBf", s_)], stream="sto")
                dma_out(dlf[row0:row0 + P, :], lg[0:P, :], rd=[tok("logf", s_)], stream="sto")
        else:
            rope(B["KA"], tok("KA", s_), P, job["rsrc"], s_)
            A(lambda e: e.copy(out=B["kabf"][0:P, :], in_=B["KA"][0:P, :]), [tok("KA", s_)], [tok("kabf", s_)])
            G(lambda e: e.tensor_copy(out=Qtm[0:P, :, 0:64], in_=B["KB"][0:P, :].rearrange("p (h d) -> p h d", h=8)), [tok("KB", s_)], ["Qtm"])
            split3(cumq_sb[0:P, :], P, Qtm, "Qtm", 64, 8.0, "cumq", s_)

    def stageP2c(job):
        s_ = job["s"]
        B = BUF[s_]
        P = job["P"]
        lg = logf[:, s_, :]
        if job["kind"] in ("kv", "cache"):
            if job.get("ts_kt") is not None:
                kt = job["ts_kt"]
                mm(ps[5][:, 64:72], W1[:, 127 - kt:255 - kt], lg[:, :], True, True, ["const", tok("logf", s_)], ["ps5"])
                V(lambda e: e.tensor_tensor(out=TSs[:, :], in0=TSs[:, :], in1=ps[5][:, 64:72], op=ALU.add), ["ps5", "TSs"], ["TSs"])
            if job.get("SC") is not None:
                pack_kv_cum(job["kt"], P, job["first"], s_)
            if job.get("cumq") == "own":
                t = job["t"]
                mm(ps[5][:, 96:104], Um[:, :], lg[:, :], True, False, ["const", tok("logf", s_)], ["ps5"])
                V(lambda e, t=t: e.tensor_scalar(out=TSm[:, :], in0=TSs[:, :], scalar1=Lt[:, t:t + 1], scalar2=None, op0=ALU.mult),
                  ["TSs", "const"], ["TSm"])
                mm(ps[5][:, 96:104], ones_f[:, :], TSm[:, :], False, True, ["const", "TSm"], ["ps5"])
                V(lambda e: e.tensor_copy(out=cumq_sb[:, :], in_=ps[5][:, 96:104]), ["ps5"], ["cumq"])
            elif job.get("cumq") == "sample":
                V(lambda e: e.tensor_copy(out=cumq_sb[0:P, :], in_=cum[job["kt"] % 2][0:P, :]), ["cum%d" % (job["kt"] % 2)], ["cumq"])
        else:
            col0 = job["col0"]
            for half in range(2):
                for q in range(4):
                    ci = half * 4 + q
                    tr(ps[0][:, q * 128:q * 128 + P], zs[0:P, ci * 128:(ci + 1) * 128], identf[0:P, 0:P], ["zs%d" % half, "const"], ["ps0"])
                V(lambda e, half=half: e.tensor_copy(out=ZT[:, half * 4:(half + 1) * 4, col0:col0 + P],
                                                     in_=ps[0][:, :].rearrange("p (q t) -> p q t", q=4)[:, :, 0:P]), ["ps0"], ["ZT"])

    def stageP2b(job):
        s_ = job["s"]
        B = BUF[s_]
        P = job["P"]
        if job["kind"] in ("kv", "cache"):
            if job.get("SC") is not None:
                pack_kv_b(job["SC"], job["kt"], P, job["first"], s_)
        else:
            col0 = job["col0"]
            kp = psb(6)
            for h in range(4):
                tr(kp[:, h * 128:h * 128 + P], B["kabf"][0:P, h * 128:(h + 1) * 128], identb[0:P, 0:P], [tok("kabf", s_), "const"], ["ps6"])
            V(lambda e: e.tensor_copy(out=QTA[:, :, col0:col0 + P], in_=kp[:, 0:512].rearrange("p (h t) -> p h t", h=4)[:, :, 0:P]),
              ["ps6"], ["QTA"])
            kp2 = psb(7)
            for h in range(8):
                tr(kp2[0:70, h * 128:h * 128 + P], Qtm[0:P, h, :], identb[0:P, 0:P], ["Qtm", "const"], ["ps7"])
            V(lambda e: e.tensor_copy(out=QTB[:, :, col0:col0 + P], in_=kp2[0:70, :].rearrange("p (h t) -> p h t", h=8)[:, :, 0:P]),
              ["ps7"], ["QTB"])

    jslot = [0]
    xslot = [0]

    def run_jobs(jobs):
        n = len(jobs)
        for jb in jobs:
            jb["s"] = jslot[0] % 2
            jslot[0] += 1
        stageL(jobs[0])
        if n > 1:
            stageL(jobs[1])
        stageF(jobs[0])
        stageF2(jobs[0])
        stageM(jobs[0])
        for k in range(n):
            if k + 2 < n:
                stageL(jobs[k + 2])
            if k + 1 < n:
                stageF(jobs[k + 1])
            stageP1(jobs[k], "a")
            if k + 1 < n:
                stageF2(jobs[k + 1])
            stageP1(jobs[k], "b")
            if k >= 1:
                stageP2c(jobs[k - 1])
            if k >= 2:
                stageP2b(jobs[k - 2])
            stageP2a(jobs[k])
            if k + 1 < n:
                stageM(jobs[k + 1])
        stageP2c(jobs[n - 1])
        if n >= 2:
            stageP2b(jobs[n - 2])
        stageP2b(jobs[n - 1])

    abuf = [0]

    def attention(SC, chunks, N_full, unit):
        pre = SC["pre"]
        isA = unit < 4
        h = unit if isA else unit - 4
        its = []
        for ci, (kt0, nkt, c0, zone, lastP) in enumerate(chunks):
            b = abuf[0] % 2
            abuf[0] += 1
            for r in range(nkt):
                its.append(dict(b=b, r=r, KP=(lastP if r == nkt - 1 else 128), c0=c0,
                                mk=(zone[r] if zone is not None else None),
                                load=((kt0, nkt, lastP) if r == 0 else None)))
        n = len(its)

        def stageA(it, idx):
            b = it["b"]
            kb_, vb_ = kbuf[b], vbuf[b]
            if it["load"] is not None:
                kt0, nkt, lastP = it["load"]
                ncols = (nkt - 1) * 128 + lastP
                if isA:
                    dma_in(kb_[:, 0, 0:ncols], SC["KTA"][h, :, kt0 * 128:kt0 * 128 + ncols],
                           rd=[pre + "KTA%d" % k for k in range(kt0, kt0 + nkt)], wr=["kbuf%d" % b], stream="ldk")
                    nfull = nkt if lastP == 128 else nkt - 1
                    dma_in(vb_[:, 0:nfull, 0:128], SC["VA"][h, :, kt0:kt0 + nfull, :],
                           rd=[pre + "VA%d" % k for k in range(kt0, kt0 + nkt)], wr=["vbuf%d" % b], stream="ldk")
                    if nfull < nkt:
                        dma_in(vb_[0:lastP, nfull:nkt, 0:128], SC["VA"][h, 0:lastP, kt0 + nfull:kt0 + nkt, :],
                               rd=[pre + "VA%d" % k for k in range(kt0, kt0 + nkt)], wr=["vbuf%d" % b], stream="ldk")
                else:
                    dma_in(kb_[0:70, :, 0:ncols], SC["KTB"][2 * h:2 * h + 2, :, kt0 * 128:kt0 * 128 + ncols].rearrange("a p t -> p a t"),
                           rd=[pre + "KTB%d" % k for k in range(kt0, kt0 + nkt)], wr=["kbuf%d" % b], stream="ldk")
                    nfull = nkt if lastP == 128 else nkt - 1
                    dma_in(vb_[:, 0:nfull, :], SC["VB"][h, :, kt0:kt0 + nfull, :],
                           rd=[pre + "VB%d" % k for k in range(kt0, kt0 + nkt)], wr=["vbuf%d" % b], stream="ldk")
                    if nfull < nkt:
                        dma_in(vb_[0:lastP, nfull:nkt, :], SC["VB"][h, 0:lastP, kt0 + nfull:kt0 + nkt, :],
                               rd=[pre + "VB%d" % k for k in range(kt0, kt0 + nkt)], wr=["vbuf%d" % b], stream="ldk")
            r, KP, c0 = it["r"], it["KP"], it["c0"]
            N = N_full - c0
            sb_ = abuf[0] % 2
            abuf[0] += 1
            it["sb"] = sb_
            P_ = Pt[sb_]
            for u in range(2):
                sbank = sb_ * 2 + u
                if isA:
                    lhsT = kb_[u * 64:(u + 1) * 64, 0, r * 128:r * 128 + KP]
                    rhs = QTA[u * 64:(u + 1) * 64, h, c0:N_full]
                else:
                    lhsT = kb_[0:70, u, r * 128:r * 128 + KP]
                    rhs = QTB[0:70, 2 * h + u, c0:N_full]
                mm(ps[sbank][0:KP, 0:N], lhsT, rhs, True, True, ["kbuf%d" % b, "QTA" if isA else "QTB"], ["ps%d" % sbank])
            mk = it["mk"]
            if mk is not None:
                mw = mk.shape[-1]
                for u in range(2):
                    sbank = sb_ * 2 + u
                    V(lambda e, sbank=sbank, KP=KP, mk=mk, mw=mw: e.tensor_tensor(
                        out=ps[sbank][0:KP, 0:mw], in0=ps[sbank][0:KP, 0:mw], in1=mk, op=ALU.add),
                      ["ps%d" % sbank, "const"], ["ps%d" % sbank])
            act(P_[0:KP, :, 0:N], psS[0:KP, 2 * sb_:2 * sb_ + 2, 0:N], AF.Exp, ["ps%d" % (2 * sb_), "ps%d" % (2 * sb_ + 1)],
                ["Pt%d" % sb_], scale=0.125)

        def stageB(it, first, last):
            b, r, KP, c0, sb_ = it["b"], it["r"], it["KP"], it["c0"], it["sb"]
            vb_ = vbuf[b]
            P_ = Pt[sb_]
            N = N_full - c0
            for u in range(2):
                if isA:
                    vl = vb_[0:KP, r, 0:128]
                else:
                    vl = vb_[0:KP, r, u * 64:u * 64 + 128]
                mm(ps[4 + u][:, c0:N_full], vl, P_[0:KP, u, 0:N], first, last, ["vbuf%d" % b, "Pt%d" % sb_], ["ps%d" % (4 + u)])
            if isA:
                for u in range(2):
                    PE(lambda e, u=u, KP=KP, P_=P_, N=N, c0=c0: e.matmul(
                        ps[6 + u][32 * u:32 * u + 32, c0:N_full], ones_bf[0:KP, 0:32], P_[0:KP, u, 0:N],
                        start=first, stop=last, tile_position=(0, 32 * u)),
                       ["const", "Pt%d" % sb_], ["ps%d" % (6 + u)])

        for idx in range(n + 1):
            if idx < n:
                stageA(its[idx], idx)
            if idx >= 1:
                stageB(its[idx - 1], idx - 1 == 0, idx - 1 == n - 1)

    def merge_unit(unit, N):
        if unit < 4:
            V(lambda e: e.tensor_copy(out=Lsb[0:32, 0:N], in_=ps[6][0:32, 0:N]), ["ps6"], ["Lsb"])
            V(lambda e: e.tensor_copy(out=Lsb[32:64, 0:N], in_=ps[7][32:64, 0:N]), ["ps7"], ["Lsb"])
            mm(ps[0][:, 0:N], bsel[0:32, :], Lsb[0:32, 0:N], True, True, ["const", "Lsb"], ["ps0"])
            mm(ps[1][:, 0:N], bsel[32:64, :], Lsb[32:64, 0:N], True, True, ["const", "Lsb"], ["ps1"])
            act(mt[0][:, 0:N], ps[0][:, 0:N], AF.Ln, ["ps0"], ["KA"])
            act(mt[0][:, 0:N], mt[0][:, 0:N], AF.Exp, ["KA"], ["KA"], scale=-1.0)
            act(mt[1][:, 0:N], ps[1][:, 0:N], AF.Ln, ["ps1"], ["VAf"])
            act(mt[1][:, 0:N], mt[1][:, 0:N], AF.Exp, ["VAf"], ["VAf"], scale=-1.0)
            V(lambda e: e.tensor_tensor(out=mt[0][:, 0:N], in0=ps[4][:, 0:N], in1=mt[0][:, 0:N], op=ALU.mult), ["ps4", "KA"], ["KA"])
            V(lambda e: e.tensor_tensor(out=mt[1][:, 0:N], in0=ps[5][:, 0:N], in1=mt[1][:, 0:N], op=ALU.mult), ["ps5", "VAf"], ["VAf"])
            V(lambda e: e.scalar_tensor_tensor(out=mt[2][:, 0:N], in0=mt[1][:, 0:N], scalar=small[:, 1:2], in1=mt[0][:, 0:N],
                                               op0=ALU.mult, op1=ALU.add), ["KA", "VAf", "neglam"], ["KB"])
            act(mt[3][:, 0:N], mt[2][:, 0:N], AF.Square, ["KB"], ["VBf"])
            mm(ps[0][:, 0:N], ones_f[:, :], mt[3][:, 0:N], True, True, ["const", "VBf"], ["ps0"])
            act(mt[0][:, 0:N], ps[0][:, 0:N], AF.Ln, ["ps0"], ["KA"], bias=EPS, scale=1.0 / 128)
            act(mt[1][:, 0:N], mt[0][:, 0:N], AF.Exp, ["KA"], ["VAf"], scale=-0.5)
            V(lambda e: e.scalar_tensor_tensor(out=mt[2][:, 0:N], in0=mt[2][:, 0:N], scalar=small[:, 7:8], in1=mt[1][:, 0:N],
                                               op0=ALU.mult, op1=ALU.mult), ["KB", "VAf", "subg2"], ["KB"])
            V(lambda e: e.tensor_tensor(out=GT[:, unit, 0:N], in0=mt[2][:, 0:N], in1=ZT[:, unit, 0:N], op=ALU.mult),
              ["KB", "ZT"], ["GT"])
        else:
            V(lambda e: e.tensor_copy(out=mt[0][0:64, 0:N], in_=ps[5][0:64, 0:N]), ["ps5"], ["KA"])
            V(lambda e: e.tensor_copy(out=mt[0][64:128, 0:N], in_=ps[4][64:128, 0:N]), ["ps4"], ["KA"])
            mm(ps[0][:, 0:N], Psw[:, :], mt[0][:, 0:N], True, True, ["const", "KA"], ["ps0"])
            act(mt[1][:, 0:N], ps[0][:, 0:N], AF.Ln, ["ps0"], ["VAf"])
            act(mt[1][:, 0:N], mt[1][:, 0:N], AF.Exp, ["VAf"], ["VAf"], scale=-1.0)
            V(lambda e: e.tensor_tensor(out=mt[2][0:64, 0:N], in0=ps[4][0:64, 0:N], in1=mt[1][0:64, 0:N], op=ALU.mult), ["ps4", "VAf"], ["KB"])
            V(lambda e: e.tensor_tensor(out=mt[2][64:128, 0:N], in0=ps[5][64:128, 0:N], in1=mt[1][64:128, 0:N], op=ALU.mult), ["ps5", "VAf"], ["KB"])
            V(lambda e: e.tensor_tensor(out=GT[:, unit, 0:N], in0=mt[2][:, 0:N], in1=ZT[:, unit, 0:N], op=ALU.mult),
              ["KB", "ZT"], ["GT"])

    def out_proj(xsrc, ydst, P, col0):
        s_ = xslot[0] % 2
        xslot[0] += 1
        X = xin[s_]
        tk = "xin%d" % s_
        dma_in(X[0:P, :], xsrc, wr=[tk], stream="ldx")
        for half in range(2):
            bank = 1 + half
            for ch in range(8):
                mm(ps[bank][0:P, :], GT[:, ch, col0:col0 + P], Wout[:, ch, half * 512:(half + 1) * 512], ch == 0, ch == 7,
                   ["GT", "W"], ["ps%d" % bank])
            V(lambda e, half=half, bank=bank: e.tensor_tensor(out=zs[0:P, half * 512:(half + 1) * 512], in0=ps[bank][0:P, :],
                                                              in1=gate_b[0:P, half * 512:(half + 1) * 512], op=ALU.mult),
              ["ps%d" % bank, "modb"], ["zs%d" % half])
            V(lambda e, half=half: e.tensor_tensor(out=zs[0:P, half * 512:(half + 1) * 512], in0=zs[0:P, half * 512:(half + 1) * 512],
                                                   in1=X[0:P, half * 512:(half + 1) * 512], op=ALU.add),
              ["zs%d" % half, tk], ["zs%d" % half])
        dma_out(ydst, zs[0:P, :], rd=["zs0", "zs1"], wr=["yout"], stream="sto")

    bridge(True)
    set_mod(1)
    jobs = [dict(kind="cache", P=128, r0=kt * 128, SC=SCS, kt=kt, first=(kt == 0)) for kt in range(8)]
    jobs.append(dict(kind="kv", P=TDEC, xsrc=x_smp[:, :], groups=G_KV, rsrc=rope_smp[:, :], SC=SCS, kt=8, first=False,
                     outs=(s_ak, s_av, s_bk, s_bv, s_lf), row0=0, cumq="sample"))
    jobs.append(dict(kind="qz", P=TDEC, xsrc=x_smp[:, :], groups=G_QZ, rsrc=rope_smp[:, :], col0=0))
    run_jobs(jobs)
    bridge(False)
    zoneS_A = [None] * 9
    zoneS_B = [None] * 8 + [maskS[:, :]]
    for unit in range(8):
        attention(SCS, [(0, 9, 0, zoneS_A if unit < 4 else zoneS_B, TDEC)], TDEC, unit)
        merge_unit(unit, TDEC)
    out_proj(x_smp[:, :], o_ys[:, :], TDEC, 0)
    bridge(True)

    set_mod(0)
    run_jobs([dict(kind="kv", P=128, xsrc=x_all[kt * 128:(kt + 1) * 128, :], groups=G_KV, rsrc=rope_all[kt, :, :],
                   SC=SCP, kt=kt, first=(kt == 0), ts_kt=kt) for kt in range(NT)])

    for j in range(4):
        jobs = []
        for m in range(4):
            t = 4 * j + m
            rows = slice(t * 128, (t + 1) * 128)
            jobs.append(dict(kind="kv", P=128, xsrc=x_own[rows, :], groups=G_KV, rsrc=rope_own[t, :, :],
                             outs=(o_ak, o_av, o_bk, o_bv, o_lf), row0=t * 128, cumq="own", t=t))
            jobs.append(dict(kind="qz", P=128, xsrc=x_own[rows, :], groups=G_QZ, rsrc=rope_own[t, :, :], col0=m * 128))
        run_jobs(jobs)
        bridge(False)
        for unit in range(8):
            chunks = [(8 * ch, 8, 0, None, 128) for ch in range(4 * j)]
            mtab = maskA if unit < 4 else maskB
            for m in range(4):
                chunks.append((8 * (4 * j + m), 8, m * 128, [mtab[:, r, :] for r in range(8)], 128))
            attention(SCP, chunks, 512, unit)
            merge_unit(unit, 512)
        for m in range(4):
            t = 4 * j + m
            out_proj(x_own[t * 128:(t + 1) * 128, :], o_y[t * 128:(t + 1) * 128, :], 128, m * 128)
        bridge(True)

    S.emit(nc, st)
    st.close()
    return nc


_NC_CACHE = {}


def _rope_table(pos):
    half = 8
    inv = (np.float32(500000.0) ** (-np.arange(0, 16, 2, dtype=np.float32) / np.float32(16))).astype(np.float32)
    ang = pos.astype(np.float32)[:, None] * inv[None, :]
    return np.concatenate([np.cos(ang), np.sin(ang)], axis=1).astype(np.float32)


def kernel(x_prompt, x_sample, cache_a_k, cache_a_v, cache_b_k, cache_b_v, cache_b_logf,
           c_prompt, c_sample, norm_g, w_ada, b_ada, w_in, b_f, qn_a, kn_a,
           lam_q1, lam_k1, lam_q2, lam_k2, subln_g, qn_b, kn_b, w_out):
    f = lambda a: np.ascontiguousarray(np.asarray(a, dtype=np.float32))
    xp = f(x_prompt)[0]
    xs = f(x_sample)
    w_in0 = f(w_in)[0]
    o = np.cumsum([0, 512, 512, 512, 512, 512, 512, 512, 8, 512])
    seg = lambda i: w_in0[:, o[i]:o[i + 1]]
    w_kv = np.ascontiguousarray(np.concatenate([seg(1), seg(2), seg(5), seg(6), seg(7)], axis=1))
    w_qz = np.ascontiguousarray(np.concatenate([seg(0), seg(3), seg(4), seg(8)], axis=1))
    rep = lambda v, n=128: np.ascontiguousarray(np.broadcast_to(f(v).reshape(1, -1), (n, f(v).size)))
    vecs = np.ascontiguousarray(np.stack([rep(qn_a[0]), rep(kn_a[0]), rep(qn_b[0]), rep(kn_b[0])], axis=1))
    lamv = np.ascontiguousarray(np.stack([rep(lam_q1[0]), rep(lam_k1[0]), rep(lam_q2[0]), rep(lam_k2[0])], axis=1))
    bf_b = rep(b_f[0])
    subg = f(subln_g)[0].reshape(128, 1).copy()
    bf16 = ml_dtypes.bfloat16
    eye = np.eye(128, dtype=np.float32)
    ar = np.arange(128)
    Umat = (ar[:, None] <= ar[None, :]).astype(np.float32)
    E127 = np.zeros((128, 128), np.float32); E127[127, :] = 1.0
    Psw = np.zeros((128, 128), np.float32); Psw[ar, (ar + 64) % 128] = 1.0
    W1 = np.zeros((128, 256), np.float32); W1[:, 127] = 1.0
    sel2 = np.zeros((2, 2, 128), np.float32); sel2[0, 0, :] = 1.0; sel2[1, 1, :] = 1.0
    bsel = np.zeros((64, 128), np.float32); bsel[0, :] = 1.0; bsel[32, :] = 1.0
    tril = (ar[:, None] <= ar[None, :])
    chunk_ok = ((ar[:, None] // 64) <= (ar[None, :] // 64))
    maskS = (((np.arange(64)[:, None] <= np.arange(64)[None, :]).astype(np.float32) - 1.0) * 30000.0).astype(bf16)
    rope_all = _rope_table(np.arange(SEQ)).reshape(NT, 128, 16)
    rope_smp = _rope_table(PAST + np.arange(TDEC))
    xt = xp.reshape(NT, 128, D)
    in_maps = []
    for c in range(8):
        own = np.arange(NOWN) * 8 + c
        Lt = np.zeros((128, NOWN), np.float32)
        mA = np.zeros((128, 8, 128), np.float32)
        mB = np.zeros((128, 8, 128), np.float32)
        for t in range(NOWN):
            Lt[:8 * t + c, t] = 1.0
        for r in range(8):
            if r < c:
                mA[:, r, :] = 1.0; mB[:, r, :] = 1.0
            elif r == c:
                mA[:, r, :] = chunk_ok; mB[:, r, :] = tril
        cvT = np.stack([f(c_prompt)[0], f(c_sample)[c]], axis=1).reshape(8, 128, 2).transpose(1, 0, 2)
        in_maps.append(dict(
            x_all=xp, x_own=np.ascontiguousarray(xt[own].reshape(NOWN * 128, D)), x_smp=np.ascontiguousarray(xs[c]),
            cvT=np.ascontiguousarray(cvT), w_ada=f(w_ada)[0], b_ada2=np.ascontiguousarray(np.broadcast_to(f(b_ada)[0][None], (2, 3 * D))),
            g_row2=np.ascontiguousarray(np.broadcast_to(f(norm_g)[0][None], (2, D))),
            w_kv=w_kv, w_qz=w_qz, w_out=f(w_out)[0], vecs=vecs, lamv=lamv, bf_b=bf_b, subg=subg,
            c_ak=np.ascontiguousarray(f(cache_a_k)[0, c].reshape(PAST, 512)),
            c_av=np.ascontiguousarray(f(cache_a_v)[0, c].reshape(PAST, 512)),
            c_bk=np.ascontiguousarray(f(cache_b_k)[0, c].reshape(PAST, 512)),
            c_bv=np.ascontiguousarray(f(cache_b_v)[0, c].reshape(PAST, 512)),
            c_lf=np.ascontiguousarray(f(cache_b_logf)[0, c].reshape(PAST, 8)),
            rope_all=rope_all, rope_own=np.ascontiguousarray(rope_all[own]), rope_smp=rope_smp,
            k_identb=eye.astype(bf16), k_identf=eye, k_U=Umat, k_E127=E127, k_Psw=Psw, k_W1=W1, k_Lt=Lt, k_sel2=sel2, k_bsel=bsel,
            k_maskA=((mA - 1.0) * 30000.0).astype(bf16), k_maskB=((mB - 1.0) * 30000.0).astype(bf16), k_maskS=maskS,
        ))
    if "nc" not in _NC_CACHE:
        _NC_CACHE["nc"] = build_nc()
    res = run_bass_kernel_spmd(_NC_CACHE["nc"], in_maps, core_ids=list(range(8)))
    R = res.results

    def gather_p(key, width):
        out = np.zeros((NT, 128, width), np.float32)
        for c in range(8):
            out[np.arange(NOWN) * 8 + c] = np.asarray(R[c][key], dtype=np.float32).reshape(NOWN, 128, width)
        return out.reshape(SEQ, width)

    def gather_s(key, width):
        return np.stack([np.asarray(R[c][key], dtype=np.float32).reshape(TDEC, width) for c in range(8)], axis=0)

    y_p = gather_p("o_y", D).reshape(1, SEQ, D)
    y_s = gather_s("o_ys", D)
    return (y_p, y_s,
            gather_p("o_ak", 512).reshape(1, 1, SEQ, 4, 2, 64), gather_p("o_av", 512).reshape(1, 1, SEQ, 4, 128),
            gather_p("o_bk", 512).reshape(1, 1, SEQ, 8, 64), gather_p("o_bv", 512).reshape(1, 1, SEQ, 8, 64),
            gather_p("o_lf", 8).reshape(1, 1, SEQ, 8),
            gather_s("s_ak", 512).reshape(1, 8, TDEC, 4, 2, 64), gather_s("s_av", 512).reshape(1, 8, TDEC, 4, 128),
            gather_s("s_bk", 512).reshape(1, 8, TDEC, 8, 64), gather_s("s_bv", 512).reshape(1, 8, TDEC, 8, 64),
            gather_s("s_lf", 8).reshape(1, 8, TDEC, 8))
```

```python
import math
from contextlib import ExitStack

import numpy as np
import ml_dtypes
import concourse.bass as bass
import concourse.mybir as mybir
from concourse.bass_utils import run_bass_kernel_spmd

F32 = mybir.dt.float32
BF16 = mybir.dt.bfloat16
AF = mybir.ActivationFunctionType
ALU = mybir.AluOpType
AX = mybir.AxisListType

D = 1024
SEQ = 16384
NT = SEQ // 128
NOWN = 16
TDEC = 64
PAST = 1024
EPS = 1e-6
NKV = 2056
NQZ = 2048
LAM_INIT = 0.8 - 0.6 * math.exp(-0.3 * 0)
SNT = 9


class Sched:
    ROT = dict(ldc=8, ldw=2, ldx=4, ldr=2, ldm=1, ldk=4, stk=8, sto=8, stm=1, ldbr=1)

    def __init__(self):
        self.ops = []
        self.last_w = {}
        self.readers = {}
        self.nstream = {}

    @staticmethod
    def dom(op):
        return ("dma", op["stream"], op["slot"]) if op["stream"] is not None else op["eng"]

    def add(self, eng, fn, rd=(), wr=(), stream=None):
        idx = len(self.ops)
        deps = {}
        def need(j):
            d = self.dom(self.ops[j])
            if d not in deps or deps[d] < j:
                deps[d] = j
        if getattr(self, "serial", False) and idx > 0:
            need(idx - 1)
        for b in rd:
            if b in self.last_w:
                need(self.last_w[b])
        for b in wr:
            if b in self.last_w:
                need(self.last_w[b])
            for j in self.readers.get(b, {}).values():
                need(j)
        op = dict(eng=eng, fn=fn, deps=deps, stream=stream, sig=False, slot=None)
        if stream is not None:
            i = self.nstream.get(stream, 0)
            self.nstream[stream] = i + 1
            op["slot"] = i % self.ROT[stream]
        self.ops.append(op)
        mydom = self.dom(op)
        for b in wr:
            self.last_w[b] = idx
            self.readers[b] = {}
        for b in rd:
            if b in wr:
                continue
            self.readers.setdefault(b, {})[mydom] = idx

    def emit(self, nc, stack):
        ops = self.ops
        for op in ops:
            for d, j in op["deps"].items():
                if d == "pe" and op["eng"] == "pe" and op["stream"] is None:
                    continue
                ops[j]["sig"] = True
        cnt = {}
        for op in ops:
            d = self.dom(op)
            if op["stream"] is not None:
                cnt[d] = cnt.get(d, 0) + 16
                op["seq"] = cnt[d]
            elif op["sig"]:
                cnt[d] = cnt.get(d, 0) + 1
                op["seq"] = cnt[d]
        sems = {}
        for d in cnt:
            nm = "s_" + (d if isinstance(d, str) else "d_%s_%d" % (d[1], d[2]))
            sems[d] = stack.enter_context(nc.semaphore(nm))
        self.final = dict(cnt)
        engs = ["pe", "act", "dve", "pool", "sp"]
        per_eng = {e: [op for op in ops if op["eng"] == e] for e in engs}
        block = stack.enter_context(nc.Block())
        final = {d: v for d, v in cnt.items() if not isinstance(d, str)}

        def run(e, eobj):
            waited = {}
            for op in per_eng[e]:
                for d, j in op["deps"].items():
                    if d == "pe" and e == "pe" and op["stream"] is None:
                        continue
                    v = ops[j]["seq"]
                    if waited.get(d, 0) >= v:
                        continue
                    eobj.wait_ge(sems[d], v)
                    waited[d] = v
                if op["stream"] is not None and op["seq"] > 16:
                    d0 = self.dom(op)
                    if waited.get(d0, 0) < op["seq"] - 16:
                        eobj.wait_ge(sems[d0], op["seq"] - 16)
                        waited[d0] = op["seq"] - 16
                ins = op["fn"](eobj)
                if op["stream"] is not None:
                    ins.then_inc(sems[self.dom(op)], 16)
                elif op["sig"]:
                    ins.then_inc(sems[e], 1)
            if e == "sp":
                for d, v in final.items():
                    eobj.wait_ge(sems[d], v)

        @block.tensor
        def _(pe):
            run("pe", pe)

        @block.scalar
        def _(act):
            run("act", act)

        @block.vector
        def _(dve):
            run("dve", dve)

        @block.gpsimd
        def _(pool):
            run("pool", pool)

        @block.sync
        def _(sp):
            run("sp", sp)


def build_nc():
    nc = bass.Bass("TRN2", target_bir_lowering=False)
    S = Sched()
    st = ExitStack()

    def din(name, shape, dt=F32):
        return nc.dram_tensor(name, list(shape), dt, kind="ExternalInput").ap()

    def dout(name, shape, dt=F32):
        return nc.dram_tensor(name, list(shape), dt, kind="ExternalOutput").ap()

    def dscr(name, shape, dt):
        return nc.dram_tensor(name, list(shape), dt, kind="Internal").ap()

    x_all = din("x_all", [SEQ, D])
    x_own = din("x_own", [NOWN * 128, D])
    x_smp = din("x_smp", [TDEC, D])
    cvT = din("cvT", [128, 8, 2])
    w_ada = din("w_ada", [D, 3 * D])
    b_ada2 = din("b_ada2", [2, 3 * D])
    g_row2 = din("g_row2", [2, D])
    w_kv = din("w_kv", [D, NKV])
    w_qz = din("w_qz", [D, NQZ])
    w_out = din("w_out", [D, D])
    vecs = din("vecs", [128, 4, 64])
    lamv = din("lamv", [128, 4, 64])
    bf_b = din("bf_b", [128, 8])
    subg = din("subg", [128, 1])
    c_ak = din("c_ak", [PAST, 512])
    c_av = din("c_av", [PAST, 512])
    c_bk = din("c_bk", [PAST, 512])
    c_bv = din("c_bv", [PAST, 512])
    c_lf = din("c_lf", [PAST, 8])
    rope_all = din("rope_all", [NT, 128, 16])
    rope_own = din("rope_own", [NOWN, 128, 16])
    rope_smp = din("rope_smp", [TDEC, 16])
    k_identb = din("k_identb", [128, 128], BF16)
    k_identf = din("k_identf", [128, 128])
    k_U = din("k_U", [128, 128])
    k_E127 = din("k_E127", [128, 128])
    k_Psw = din("k_Psw", [128, 128])
    k_W1 = din("k_W1", [128, 256])
    k_Lt = din("k_Lt", [128, NOWN])
    k_sel2 = din("k_sel2", [2, 2, 128])
    k_bsel = din("k_bsel", [64, 128])
    k_maskA = din("k_maskA", [128, 8, 128], BF16)
    k_maskB = din("k_maskB", [128, 8, 128], BF16)
    k_maskS = din("k_maskS", [64, 64], BF16)

    o_y = dout("o_y", [NOWN * 128, D])
    o_ys = dout("o_ys", [TDEC, D])
    o_ak = dout("o_ak", [NOWN * 128, 512])
    o_av = dout("o_av", [NOWN * 128, 512])
    o_bk = dout("o_bk", [NOWN * 128, 512])
    o_bv = dout("o_bv", [NOWN * 128, 512])
    o_lf = dout("o_lf", [NOWN * 128, 8])
    s_ak = dout("s_ak", [TDEC, 512])
    s_av = dout("s_av", [TDEC, 512])
    s_bk = dout("s_bk", [TDEC, 512])
    s_bv = dout("s_bv", [TDEC, 512])
    s_lf = dout("s_lf", [TDEC, 8])

    def scr(prefix, ntl):
        return dict(
            KTA=dscr(prefix + "KTA", [4, 128, ntl * 128], BF16),
            VA=dscr(prefix + "VA", [4, 128, ntl, 128], BF16),
            KTB=dscr(prefix + "KTB", [8, 70, ntl * 128], BF16),
            VB=dscr(prefix + "VB", [4, 128, ntl, 192], BF16),
            pre=prefix,
        )
    modscr = dscr("modscr", [2, 3 * D], F32)
    SCP = scr("p", NT)
    SCS = scr("s", SNT)

    def sb(name, shape, dt=F32):
        return st.enter_context(nc.sbuf_tensor(name, list(shape), dt))

    identb = sb("identb", [128, 128], BF16)
    identf = sb("identf", [128, 128])
    Um = sb("Um", [128, 128])
    E127 = sb("E127", [128, 128])
    Psw = sb("Psw", [128, 128])
    W1 = sb("W1", [128, 256])
    Lt = sb("Lt", [128, NOWN])
    TSm = sb("TSm", [128, 8])
    sel2 = sb("sel2", [2, 2, 128])
    bsel = sb("bsel", [64, 128])
    Lsb = sb("Lsb", [64, 512])
    maskA = sb("maskA", [128, 8, 128], BF16)
    maskB = sb("maskB", [128, 8, 128], BF16)
    maskS = sb("maskS", [64, 64], BF16)
    ones_bf = sb("ones_bf", [128, 128], BF16)
    ones_f = sb("ones_f", [128, 128])
    vecs_sb = sb("vecs_sb", [128, 4, 64])
    lam_sb = sb("lam_sb", [128, 4, 64])
    bf_sb = sb("bf_sb", [128, 8])
    subg_sb = sb("subg_sb", [128, 1])
    small = sb("small", [128, 16])
    Wkv = sb("Wkv", [128, 8, NKV], BF16)
    Wqz = sb("Wqz", [128, 8, NQZ], BF16)
    Wout = sb("Wout", [128, 8, D], BF16)
    ZT = sb("ZT", [128, 8, 512])
    ZTf = ZT[:, :, :].rearrange("p a b -> p (a b)")
    wst = [ZTf[:, 0:2048], ZTf[:, 2048:4096]]
    wtail = sb("wtail", [128, 8, 8])
    gmod_b = sb("gmod_b", [128, D])
    shift_b = sb("shift_b", [128, D])
    gate_b = sb("gate_b", [128, D])
    mod_sb = ZTf[0:2, 0:3 * D]
    scT = sb("scT", [128, 8, 2])
    scT2 = sb("scT2", [128, 8, 2])

    xin = [sb("xin0", [128, D]), sb("xin1", [128, D])]
    sqj = sb("sqj", [128, D])
    xbf = sb("xbf", [128, D], BF16)
    xT = sb("xT", [128, 8, 128], BF16)
    KAB = sb("KAB", [128, 2, 512])
    KA = KAB[:, 0, :]
    KB = KAB[:, 1, :]
    VAB = sb("VAB", [128, 2, 512])
    VAf = VAB[:, 0, :]
    VBf = VAB[:, 1, :]
    rtab = sb("rtab", [128, 2, 16])
    rtmp = sb("rtmp", [128, 2, 4, 8, 8])
    st8 = sb("st8", [128, 2, 12, 8])
    sm2 = sb("sm2", [128, 2, 4])
    cumq_sb = sb("cumq_sb", [128, 8])
    logf = sb("logf", [128, 2, 8])
    cum = [sb("cum0", [128, 8]), sb("cum1", [128, 8])]
    TSs = sb("TSs", [128, 8])
    spl = sb("spl", [128, 2, 6, 8])
    splb = sb("splb", [128, 2, 3, 8], BF16)
    kabf = sb("kabf", [128, 512], BF16)
    Ktm = sb("Ktm", [128, 8, 70], BF16)
    Qtm = sb("Qtm", [128, 8, 70], BF16)
    KTas = sb("KTas", [128, 4, 128], BF16)
    KTbs = sb("KTbs", [70, 8, 128], BF16)
    vas = sb("vas", [128, 4, 128], BF16)
    vbs = sb("vbs", [128, 4, 192], BF16)
    QTA = sb("QTA", [128, 4, 512], BF16)
    QTB = sb("QTB", [70, 8, 512], BF16)
    GT = sb("GT", [128, 8, 512], BF16)
    zs = sb("zs", [128, D])
    kbuf = [sb("kbuf0", [128, 2, 1152], BF16), sb("kbuf1", [128, 2, 1152], BF16)]
    vbuf = [sb("vbuf0", [128, SNT, 192], BF16), sb("vbuf1", [128, SNT, 192], BF16)]
    Pt = [sb("Pt0", [128, 2, 512], BF16), sb("Pt1", [128, 2, 512], BF16)]
    mt = [KA, VAf, KB, VBf]
    mtk = ["KA", "VAf", "KB", "VBf"]

    psS = st.enter_context(nc.psum_tensor("psS", [128, 4, 512], F32))
    ps = [psS[:, i, :] for i in range(4)] + [st.enter_context(nc.psum_tensor("ps%d" % i, [128, 512], F32)) for i in range(4, 8)]

    def psb(i):
        return ps[i][:, :].bitcast(BF16)

    def _sn(stream, ap):
        return stream

    def dma_in(out, in_, wr, stream="ld", rd=()):
        S.add("sp", lambda e, o=out, i=in_: e.dma_start(out=o, in_=i), rd=rd, wr=wr, stream=_sn(stream, out))

    def dma_out(out, in_, rd, wr=(), stream="st"):
        S.add("sp", lambda e, o=out, i=in_: e.dma_start(out=o, in_=i), rd=rd, wr=wr, stream=_sn(stream, in_))

    def A(fn, rd, wr):
        S.add("act", fn, rd, wr)

    def V(fn, rd, wr):
        S.add("dve", fn, rd, wr)

    def G(fn, rd, wr):
        S.add("pool", fn, rd, wr)

    def PE(fn, rd, wr):
        S.add("pe", fn, rd, wr)

    def act(out, in_, func, rd, wr, bias=0.0, scale=1.0, accum=None):
        A(lambda e: e.activation(out=out, in_=in_, func=func, bias=bias, scale=scale, accum_out=accum), rd, wr)

    def mm(out, lhsT, rhs, start, stop, rd, wr):
        PE(lambda e: e.matmul(out, lhsT, rhs, start=start, stop=stop), rd, wr)

    def tr(out, in_, ident, rd, wr):
        PE(lambda e: e.transpose(out, in_, ident), rd, wr)

    ctoks = []
    wtoks = []
    for t_, d_ in ((identb, k_identb), (identf, k_identf), (Um, k_U), (E127, k_E127), (Psw, k_Psw),
                   (W1, k_W1), (sel2, k_sel2), (bsel, k_bsel), (maskA, k_maskA), (maskB, k_maskB), (maskS, k_maskS),
                   (vecs_sb, vecs), (lam_sb, lamv), (bf_sb, bf_b), (subg_sb, subg), (Lt, k_Lt)):
        sl = tuple(slice(None) for _ in range(len(d_.shape)))
        ctoks.append("c_%d" % len(ctoks))
        dma_in(t_[sl], d_[sl], wr=[ctoks[-1]], stream="ldc")
    V(lambda e: e.memset(ones_bf[:, :], 1.0), [], ["c_ones"])
    V(lambda e: e.memset(ones_f[:, :], 1.0), ["c_ones"], ["c_ones"])
    V(lambda e: e.memset(small[:, 13:14], 0.0), ctoks + ["c_ones"], ["const"])
    V(lambda e: e.memset(Ktm[:, :, 64:67], 1.0), [], ["Ktm"])
    V(lambda e: e.memset(Qtm[:, :, 67:70], 1.0), [], ["Qtm"])
    V(lambda e: e.memset(vbs[:, :, 64:128], 1.0), [], ["vbs"])
    V(lambda e: e.memset(TSs[:, :], 0.0), [], ["TSs"])

    V(lambda e: e.tensor_tensor(out=sqj[:, 0:64], in0=lam_sb[:, 0, :], in1=lam_sb[:, 1, :], op=ALU.mult), ["const"], ["sqj"])
    V(lambda e: e.tensor_tensor(out=sqj[:, 64:128], in0=lam_sb[:, 2, :], in1=lam_sb[:, 3, :], op=ALU.mult), ["const"], ["sqj"])
    V(lambda e: e.reduce_sum(out=small[:, 2:4], in_=sqj[:, 0:128].rearrange("p (a b) -> p a b", a=2), axis=AX.X), ["sqj"], ["small"])
    act(small[:, 4:6], small[:, 2:4], AF.Exp, ["small"], ["small2"])
    V(lambda e: e.tensor_tensor(out=small[:, 6:7], in0=small[:, 4:5], in1=small[:, 5:6], op=ALU.subtract), ["small2"], ["small3"])
    V(lambda e: e.tensor_scalar(out=small[:, 1:2], in0=small[:, 6:7], scalar1=LAM_INIT, scalar2=-1.0, op0=ALU.add, op1=ALU.mult), ["small3"], ["neglam"])
    V(lambda e: e.tensor_scalar(out=small[:, 7:8], in0=subg_sb[:, 0:1], scalar1=1.0 - LAM_INIT, scalar2=None, op0=ALU.mult), ["const"], ["subg2"])

    slot = 0
    dma_in(wtail[:, :, :], w_kv[:, 2048:2056].rearrange("(c p) n -> p c n", p=128), wr=["wtail"], stream="ldc")
    V(lambda e: e.tensor_copy(out=Wkv[:, :, 2048:2056], in_=wtail[:, :, :]), ["wtail"], ["w_tail"])
    wtoks.append("w_tail")
    for (Wt, wsrc, ncol) in ((Wkv, w_kv, 2048), (Wqz, w_qz, NQZ), (Wout, w_out, D)):
        for ch in range(8):
            s_ = slot % 2
            slot += 1
            dma_in(wst[s_][:, 0:ncol], wsrc[ch * 128:(ch + 1) * 128, 0:ncol], wr=["wst%d" % s_], stream="ldw")
            wtoks.append("w_%d" % len(wtoks))
            if ch % 2 == 0:
                V(lambda e, Wt=Wt, ch=ch, s_=s_, ncol=ncol: e.tensor_copy(out=Wt[:, ch, 0:ncol], in_=wst[s_][:, 0:ncol]), ["wst%d" % s_], [wtoks[-1]])
            else:
                A(lambda e, Wt=Wt, ch=ch, s_=s_, ncol=ncol: e.copy(out=Wt[:, ch, 0:ncol], in_=wst[s_][:, 0:ncol]), ["wst%d" % s_], [wtoks[-1]])

    dma_in(scT[:, :, :], cvT[:, :, :], wr=["scT"], stream="ldc")
    act(scT2[:, :, :], scT[:, :, :], AF.Exp, ["scT"], ["scT2"], scale=-1.0)
    V(lambda e: e.tensor_scalar(out=scT2[:, :, :], in0=scT2[:, :, :], scalar1=1.0, scalar2=None, op0=ALU.add), ["scT2"], ["scT2"])
    V(lambda e: e.reciprocal(out=scT2[:, :, :], in_=scT2[:, :, :]), ["scT2"], ["scT2"])
    V(lambda e: e.tensor_tensor(out=scT[:, :, :], in0=scT[:, :, :], in1=scT2[:, :, :], op=ALU.mult), ["scT2", "scT"], ["scT"])
    for ch in range(8):
        for half in range(2):
            s_ = slot % 2
            slot += 1
            dma_in(wst[s_][:, 0:1536], w_ada[ch * 128:(ch + 1) * 128, half * 1536:(half + 1) * 1536],
                   wr=["wst%d" % s_], stream="ldw")
            for g in range(3):
                mm(ps[half * 3 + g][0:2, :], scT[:, ch, :], wst[s_][:, g * 512:(g + 1) * 512],
                   ch == 0, ch == 7, ["scT", "wst%d" % s_], ["ps%d" % (half * 3 + g)])
    bst = [xin[0], xin[1], zs]
    for i in range(3):
        dma_in(bst[i][0:2, :], b_ada2[:, i * D:(i + 1) * D], wr=["bst%d" % i], stream="ldc")
    dma_in(sqj[0:2, :], g_row2[:, :], wr=["sqj"], stream="ldc")
    for g in range(6):
        V(lambda e, g=g: e.tensor_tensor(out=mod_sb[:, g * 512:(g + 1) * 512], in0=ps[g][0:2, :],
                                         in1=bst[g // 2][0:2, (g % 2) * 512:(g % 2 + 1) * 512], op=ALU.add),
          ["ps%d" % g, "bst%d" % (g // 2)], ["mod_sb", "wst0", "wst1"])
    V(lambda e: e.scalar_tensor_tensor(out=mod_sb[:, D:2 * D], in0=mod_sb[:, D:2 * D], scalar=1.0, in1=sqj[0:2, :],
                                       op0=ALU.add, op1=ALU.mult), ["mod_sb", "sqj"], ["mod_sb"])
    dma_out(modscr[:, :], mod_sb, rd=["mod_sb"], wr=["modscr"], stream="stm")
    S.add("sp", lambda e: e.dma_start(out=small[:, 11:12], in_=subg[:, 0:1]), ["bst0", "bst1", "bst2", "sqj", "wst0", "wst1", "modscr"] + wtoks,
          ["xin0", "xin1", "zs0", "zs1", "ZT", "W"], stream="ldbr")

    def set_mod(r):
        for i, dst in enumerate((shift_b, gmod_b, gate_b)):
            dma_in(dst[:, :], modscr[r:r + 1, i * D:(i + 1) * D].to_broadcast([128, D]), wr=["modb"], rd=["modscr"], stream="ldm")

    def flat(t3):
        return t3[:, :, :].rearrange("p a b -> p (a b)")
    k0f = flat(kbuf[0]).bitcast(F32)
    k1f = flat(kbuf[1])
    v0f = flat(vbuf[0])
    v1f = flat(vbuf[1])
    gtf = flat(GT).bitcast(F32)
    p0f = flat(Pt[0])
    p1f = flat(Pt[1])
    BUF = [
        dict(xin=xin[0], sqj=sqj[:, :], xbf=xbf[:, :], xT=xT[:, :, :], KA=KA[:, :], VAf=VAf[:, :], KB=KB[:, :], VBf=VBf[:, :],
             KAB=KAB[:, :, :].rearrange("p a n -> p (a n)"), VAB=VAB[:, :, :].rearrange("p a n -> p (a n)"),
             kabf=kabf[:, :], KTas=KTas[:, :, :], KTbs=KTbs[:, :, :], vas=vas[:, :, :], vbs=vbs[:, :, :], Ktm=Ktm[:, :, :]),
        dict(xin=xin[1], sqj=k0f[:, 0:1024], xbf=p0f, xT=p1f.rearrange("p (c t) -> p c t", c=8),
             KA=gtf[:, 0:512], KB=gtf[:, 512:1024], VAf=gtf[:, 1024:1536], VBf=gtf[:, 1536:2048], KAB=gtf[:, 0:1024], VAB=gtf[:, 1024:2048],
             kabf=k1f[:, 0:512], KTas=k1f[:, 512:1024].rearrange("p (h t) -> p h t", h=4),
             vas=k1f[:, 1024:1536].rearrange("p (h e) -> p h e", h=4),
             Ktm=k1f[:, 1536:2096].rearrange("p (h d) -> p h d", h=8),
             KTbs=v0f[0:70, 0:1024].rearrange("p (h t) -> p h t", h=8),
             vbs=v1f[:, 0:768].rearrange("p (h e) -> p h e", h=4)),
    ]
    FAM_A = ["kbuf0", "kbuf1", "vbuf0", "vbuf1", "Pt0", "Pt1", "GT"]
    FAM_B = [n_ + "1" for n_ in ("sqj", "xbf", "xT", "KA", "VAf", "KB", "VBf", "kabf", "KTas", "KTbs", "vas", "vbs", "Ktm")]

    def tok(n_, s_):
        return n_ if s_ == 0 else n_ + "1"

    def bridge(to_proj):
        V(lambda e: e.memset(small[:, 12:13], 0.0), [], FAM_A + FAM_B)
        if to_proj:
            V(lambda e: e.memset(BUF[1]["Ktm"][:, :, 64:67], 1.0), [], ["Ktm1"])
            V(lambda e: e.memset(BUF[1]["vbs"][:, :, 64:128], 1.0), [], ["vbs1"])

    def stageL(job):
        if job["kind"] == "cache":
            return
        s_ = job["s"]
        dma_in(BUF[s_]["xin"][0:job["P"], :], job["xsrc"], wr=["xin%d" % s_], stream="ldx")
        job["xloaded"] = True

    def stageF(job):
        s_ = job["s"]
        B = BUF[s_]
        P = job["P"]
        if job["kind"] == "cache":
            r0 = job["r0"]
            dma_in(B["KA"][:, :], c_ak[r0:r0 + 128, :], wr=[tok("KA", s_)], stream="ldx")
            dma_in(B["VAf"][:, :], c_av[r0:r0 + 128, :], wr=[tok("VAf", s_)], stream="ldx")
            dma_in(B["KB"][:, :], c_bk[r0:r0 + 128, :], wr=[tok("KB", s_)], stream="ldx")
            dma_in(B["VBf"][:, :], c_bv[r0:r0 + 128, :], wr=[tok("VBf", s_)], stream="ldx")
            dma_in(logf[:, s_, :], c_lf[r0:r0 + 128, :], wr=[tok("logf", s_)], stream="ldx")
            return
        X = B["xin"]
        tk = "xin%d" % s_
        SQ = B["sqj"]
        if not job.get("xloaded"):
            stageL(job)
        act(SQ[0:P, :], X[0:P, :], AF.Square, [tk], [tok("sqj", s_), tok("ssq", s_)], accum=sm2[0:P, s_, 0:1])
        act(sm2[0:P, s_, 1:2], sm2[0:P, s_, 0:1], AF.Ln, [tok("ssq", s_)], [tok("ssq2", s_)], bias=EPS, scale=1.0 / D)
        act(sm2[0:P, s_, 2:3], sm2[0:P, s_, 1:2], AF.Exp, [tok("ssq2", s_)], [tok("rstd", s_)], scale=-0.5)
        V(lambda e: e.scalar_tensor_tensor(out=SQ[0:P, :], in0=X[0:P, :], scalar=sm2[0:P, s_, 2:3], in1=gmod_b[0:P, :],
                                           op0=ALU.mult, op1=ALU.mult), [tk, tok("rstd", s_), "modb", tok("sqj", s_)], [tok("sqj", s_)])
        G(lambda e: e.tensor_tensor(out=B["xbf"][0:P, :], in0=SQ[0:P, :], in1=shift_b[0:P, :], op=ALU.add),
          [tok("sqj", s_), "modb"], [tok("xbf", s_)])

    def stageR(job):
        if job["kind"] == "cache":
            return
        s_ = job["s"]
        dma_in(rtab[0:job["P"], s_, :], job["rsrc"], wr=[tok("rtab", s_)], stream="ldr")

    def stageF2(job):
        if job["kind"] == "cache":
            return
        s_ = job["s"]
        B = BUF[s_]
        P = job["P"]
        xTp = psb(0)
        for ch in range(8):
            tr(xTp[:, ch * 128:ch * 128 + P], B["xbf"][0:P, ch * 128:(ch + 1) * 128], identb[0:P, 0:P], [tok("xbf", s_), "const"], ["ps0"])
        A(lambda e: e.copy(out=B["xT"][:, :, 0:P], in_=xTp.rearrange("p (c t) -> p c t", c=8)[:, :, 0:P]), ["ps0"], [tok("xT", s_)])

    def stageM(job):
        if job["kind"] == "cache":
            return
        s_ = job["s"]
        B = BUF[s_]
        P = job["P"]
        for (Wt, c0, ncol, bank) in job["groups"]:
            for ch in range(8):
                mm(ps[bank][0:P, 0:ncol], B["xT"][:, ch, 0:P], Wt[:, ch, c0:c0 + ncol], ch == 0, ch == 7,
                   [tok("xT", s_), "W"], ["ps%d" % bank])

    def qk_norm(dst, dtok, bank, P, gain_idx, s_):
        pv = ps[bank]
        SQ = BUF[s_]["sqj"]
        act(SQ[0:P, 0:512], pv[0:P, :], AF.Square, ["ps%d" % bank], [tok("sqj", s_)])
        V(lambda e: e.reduce_sum(out=st8[0:P, s_, 0, :], in_=SQ[0:P, 0:512].rearrange("p (g d) -> p g d", g=8), axis=AX.X),
          [tok("sqj", s_)], [tok("st8a", s_)])
        act(st8[0:P, s_, 1, :], st8[0:P, s_, 0, :], AF.Ln, [tok("st8a", s_)], [tok("st8b", s_)], bias=EPS, scale=1.0 / 64)
        act(st8[0:P, s_, 2, :], st8[0:P, s_, 1, :], AF.Exp, [tok("st8b", s_)], [tok("st8c", s_)], scale=-0.5)
        V(lambda e: e.tensor_tensor(out=dst[0:P, :].rearrange("p (g d) -> p g d", g=8),
                                    in0=pv[0:P, :].rearrange("p (g d) -> p g d", g=8),
                                    in1=st8[0:P, s_, 2, :].unsqueeze(2).to_broadcast([P, 8, 64]), op=ALU.mult),
          ["ps%d" % bank, tok("st8c", s_)], [dtok])
        G(lambda e: e.tensor_tensor(out=dst[0:P, :].rearrange("p (g d) -> p g d", g=8),
                                    in0=dst[0:P, :].rearrange("p (g d) -> p g d", g=8),
                                    in1=vecs_sb[0:P, gain_idx, :].unsqueeze(1).to_broadcast([P, 8, 64]), op=ALU.mult),
          [dtok, "const"], [dtok])

    def qk_norm2(P, gain_a, gain_b, s_):
        B = BUF[s_]
        SQ = B["VAB"]
        pv = psS[0:P, 1:3, :].rearrange("p a n -> p (a n)")
        dst = B["KAB"]
        ta, tb = tok("KA", s_), tok("KB", s_)
        act(SQ[0:P, :], pv, AF.Square, ["ps1", "ps2"], [tok("VAf", s_), tok("VBf", s_)])
        V(lambda e: e.reduce_sum(out=st8[0:P, s_, 0:2, :].rearrange("p a g -> p (a g)"),
                                 in_=SQ[0:P, :].rearrange("p (g d) -> p g d", g=16), axis=AX.X),
          [tok("VAf", s_), tok("VBf", s_)], [tok("st8a", s_)])
        act(st8[0:P, s_, 2:4, :], st8[0:P, s_, 0:2, :], AF.Ln, [tok("st8a", s_)], [tok("st8b", s_)], bias=EPS, scale=1.0 / 64)
        act(st8[0:P, s_, 4:6, :], st8[0:P, s_, 2:4, :], AF.Exp, [tok("st8b", s_)], [tok("st8c", s_)], scale=-0.5)
        V(lambda e: e.tensor_tensor(out=dst[0:P, :].rearrange("p (g d) -> p g d", g=16),
                                    in0=pv.rearrange("p (g d) -> p g d", g=16),
                                    in1=st8[0:P, s_, 4:6, :].rearrange("p a g -> p (a g)").unsqueeze(2).to_broadcast([P, 16, 64]),
                                    op=ALU.mult),
          ["ps1", "ps2", tok("st8c", s_)], [ta, tb])
        for (dd, dt_, gi) in ((B["KA"], ta, gain_a), (B["KB"], tb, gain_b)):
            G(lambda e, dd=dd, gi=gi: e.tensor_tensor(out=dd[0:P, :].rearrange("p (g d) -> p g d", g=8),
                                                      in0=dd[0:P, :].rearrange("p (g d) -> p g d", g=8),
                                                      in1=vecs_sb[0:P, gi, :].unsqueeze(1).to_broadcast([P, 8, 64]), op=ALU.mult),
              [dt_, "const"], [dt_])

    def rope(dst, tk, P, rsrc, s_):
        dv = dst[0:P, :].rearrange("p (g d) -> p g d", g=8)
        x1 = dv[:, :, 0:8]
        x2 = dv[:, :, 8:16]
        cs = rtab[0:P, s_, 0:8].unsqueeze(1).to_broadcast([P, 8, 8])
        sn = rtab[0:P, s_, 8:16].unsqueeze(1).to_broadcast([P, 8, 8])
        rt = tok("rtmp", s_)
        rtk = tok("rtab", s_)
        V(lambda e: e.tensor_tensor(out=rtmp[0:P, s_, 0], in0=x1, in1=cs, op=ALU.mult), [tk, rtk], [rt])
        V(lambda e: e.tensor_tensor(out=rtmp[0:P, s_, 1], in0=x2, in1=sn, op=ALU.mult), [tk, rtk], [rt])
        V(lambda e: e.tensor_tensor(out=rtmp[0:P, s_, 2], in0=x2, in1=cs, op=ALU.mult), [tk, rtk], [rt])
        V(lambda e: e.tensor_tensor(out=rtmp[0:P, s_, 3], in0=x1, in1=sn, op=ALU.mult), [tk, rtk], [rt])
        V(lambda e: e.tensor_tensor(out=x1, in0=rtmp[0:P, s_, 0], in1=rtmp[0:P, s_, 1], op=ALU.subtract), [rt], [tk])
        V(lambda e: e.tensor_tensor(out=x2, in0=rtmp[0:P, s_, 2], in1=rtmp[0:P, s_, 3], op=ALU.add), [rt], [tk])

    def split3(src, P, dsttile, dtok, c0, scale, stok, s_):
        sp_ = spl[:, s_]
        t1, t2 = tok("spl1", s_), tok("spl2", s_)
        hi = dsttile[0:P, :, c0:c0 + 1]
        mid = dsttile[0:P, :, c0 + 1:c0 + 2]
        lo = dsttile[0:P, :, c0 + 2:c0 + 3]
        sv = src.unsqueeze(2)
        V(lambda e: e.tensor_scalar(out=hi, in0=sv, scalar1=scale, scalar2=None, op0=ALU.mult), [stok], [dtok])
        V(lambda e: e.scalar_tensor_tensor(out=sp_[0:P, 1, :].unsqueeze(2), in0=sv, scalar=scale, in1=hi, op0=ALU.mult, op1=ALU.subtract),
          [stok, dtok], [t1])
        V(lambda e: e.tensor_copy(out=mid, in_=sp_[0:P, 1, :].unsqueeze(2)), [t1], [dtok])
        V(lambda e: e.tensor_tensor(out=sp_[0:P, 2, :].unsqueeze(2), in0=sp_[0:P, 1, :].unsqueeze(2), in1=mid, op=ALU.subtract), [t1, dtok], [t2])
        V(lambda e: e.tensor_copy(out=lo, in_=sp_[0:P, 2, :].unsqueeze(2)), [t2], [dtok])

    def pack_kv_cum(kt, P, first, s_):
        B = BUF[s_]
        lg = logf[:, s_, :]
        cprev = cum[(kt + 1) % 2]
        ccur = cum[kt % 2]
        mm(ps[5][0:P, 32:40], Um[0:P, 0:P], lg[0:P, :], True, first, ["const", tok("logf", s_)], ["ps5"])
        if not first:
            mm(ps[5][0:P, 32:40], E127[:, 0:P], cprev[:, :], False, True, ["const", "cum%d" % ((kt + 1) % 2)], ["ps5"])
        V(lambda e: e.tensor_copy(out=ccur[0:P, :], in_=ps[5][0:P, 32:40]), ["ps5"], ["cum%d" % (kt % 2)])
        split3(ccur[0:P, :], P, B["Ktm"], tok("Ktm", s_), 67, -8.0, "cum%d" % (kt % 2), s_)

    def pack_kv_cast(P, s_):
        B = BUF[s_]
        A(lambda e: e.copy(out=B["kabf"][0:P, :], in_=B["KA"][0:P, :]), [tok("KA", s_)], [tok("kabf", s_)])
        A(lambda e: e.copy(out=B["vas"][0:P, :, :], in_=B["VAf"][0:P, :].rearrange("p (h e) -> p h e", h=4)),
          [tok("VAf", s_)], [tok("vas", s_)])
        A(lambda e: e.copy(out=B["Ktm"][0:P, :, 0:64], in_=B["KB"][0:P, :].rearrange("p (h d) -> p h d", h=8)),
          [tok("KB", s_)], [tok("Ktm", s_)])
        vv = B["VBf"][0:P, :].rearrange("p (q a d) -> p q a d", q=4, a=2)
        G(lambda e: e.tensor_copy(out=B["vbs"][0:P, :, 0:64], in_=vv[:, :, 0, :]), [tok("VBf", s_)], [tok("vbs", s_)])
        G(lambda e: e.tensor_copy(out=B["vbs"][0:P, :, 128:192], in_=vv[:, :, 1, :]), [tok("VBf", s_)], [tok("vbs", s_)])

    def pack_kv_b(SC, kt, P, first, s_):
        pre = SC["pre"]
        B = BUF[s_]
        kp = psb(6)
        for h in range(4):
            tr(kp[:, h * 128:h * 128 + P], B["kabf"][0:P, h * 128:(h + 1) * 128], identb[0:P, 0:P], [tok("kabf", s_), "const"], ["ps6"])
        V(lambda e: e.tensor_copy(out=B["KTas"][:, :, 0:P], in_=kp[:, 0:512].rearrange("p (h t) -> p h t", h=4)[:, :, 0:P]),
          ["ps6"], [tok("KTas", s_)])
        dma_out(SC["KTA"][:, :, kt * 128:kt * 128 + P].rearrange("h p t -> p h t"), B["KTas"][:, :, 0:P], rd=[tok("KTas", s_)],
                wr=[pre + "KTA%d" % kt], stream="stk")
        dma_out(SC["VA"][:, 0:P, kt, :].rearrange("h p e -> p h e"), B["vas"][0:P, :, :], rd=[tok("vas", s_)], wr=[pre + "VA%d" % kt], stream="stk")
        kp2 = psb(7)
        for h in range(8):
            tr(kp2[0:70, h * 128:h * 128 + P], B["Ktm"][0:P, h, :], identb[0:P, 0:P], [tok("Ktm", s_), "const"], ["ps7"])
        V(lambda e: e.tensor_copy(out=B["KTbs"][:, :, 0:P], in_=kp2[0:70, :].rearrange("p (h t) -> p h t", h=8)[:, :, 0:P]),
          ["ps7"], [tok("KTbs", s_)])
        dma_out(SC["KTB"][:, :, kt * 128:kt * 128 + P].rearrange("h p t -> p h t"), B["KTbs"][:, :, 0:P], rd=[tok("KTbs", s_)],
                wr=[pre + "KTB%d" % kt], stream="stk")
        dma_out(SC["VB"][:, 0:P, kt, :].rearrange("h p e -> p h e"), B["vbs"][0:P, :, :], rd=[tok("vbs", s_)], wr=[pre + "VB%d" % kt], stream="stk")

    G_KV = [(Wkv, 0, 512, 1), (Wkv, 1024, 512, 2), (Wkv, 512, 512, 3), (Wkv, 1536, 512, 4), (Wkv, 2048, 8, 5)]
    G_QZ = [(Wqz, 0, 512, 1), (Wqz, 1024, 512, 2), (Wqz, 512, 512, 3), (Wqz, 1536, 512, 4)]

    def stageP1(job):
        s_ = job["s"]
        B = BUF[s_]
        P = job["P"]
        if job["kind"] == "cache":
            return
        if job["kind"] == "kv":
            qk_norm2(P, 1, 3, s_)
            A(lambda e: e.copy(out=B["VAf"][0:P, :], in_=ps[3][0:P, :]), ["ps3"], [tok("VAf", s_)])
            A(lambda e: e.copy(out=B["VBf"][0:P, :], in_=ps[4][0:P, :]), ["ps4"], [tok("VBf", s_)])
            V(lambda e: e.tensor_tensor(out=st8[0:P, s_, 6, :], in0=ps[5][0:P, 0:8], in1=bf_sb[0:P, :], op=ALU.add),
              ["ps5", "const"], [tok("st8d", s_)])
        else:
            qk_norm2(P, 0, 2, s_)
            for half, bank in ((0, 3), (1, 4)):
                zc = zs[0:P, half * 512:(half + 1) * 512]
                act(zc, ps[bank][0:P, :], AF.Exp, ["ps%d" % bank], ["zs%d" % half], scale=-1.0)
                act(zc, zc, AF.Ln, ["zs%d" % half], ["zs%d" % half], bias=1.0)
                act(zc, zc, AF.Exp, ["zs%d" % half], ["zs%d" % half], scale=-1.0)
                V(lambda e, zc=zc, bank=bank: e.tensor_tensor(out=zc, in0=ps[bank][0:P, :], in1=zc, op=ALU.mult),
                  ["zs%d" % half, "ps%d" % bank], ["zs%d" % half])

    def stageP2a(job):
        s_ = job["s"]
        B = BUF[s_]
        P = job["P"]
        lg = logf[:, s_, :]
        if job["kind"] in ("kv", "cache"):
            if job["kind"] == "kv":
                act(st8[0:P, s_, 7, :], st8[0:P, s_, 6, :], AF.Exp, [tok("st8d", s_)], [tok("st8e", s_)], scale=-1.0)
                act(st8[0:P, s_, 8, :], st8[0:P, s_, 7, :], AF.Ln, [tok("st8e", s_)], [tok("st8f", s_)], bias=1.0)
                V(lambda e: e.tensor_scalar(out=lg[0:P, :], in0=st8[0:P, s_, 8, :], scalar1=-1.0, scalar2=None, op0=ALU.mult),
                  [tok("st8f", s_)], [tok("logf", s_)])
                rope(B["KA"], tok("KA", s_), P, job["rsrc"], s_)
            if job.get("SC") is not None:
                pack_kv_cast(P, s_)
            if job.get("outs") is not None:
                dak, dav, dbk, dbv, dlf = job["outs"]
                row0 = job["row0"]
                dma_out(dak[row0:row0 + P, :], B["KA"][0:P, :], rd=[tok("KA", s_)], stream="sto")
                dma_out(dav[row0:row0 + P, :], B["VAf"][0:P, :], rd=[tok("VAf", s_)], stream="sto")
                dma_out(dbk[row0:row0 + P, :], B["KB"][0:P, :], rd=[tok("KB", s_)], stream="sto")
                dma_out(dbv[row0:row0 + P, :], B["VBf"][0:P, :], rd=[tok("VBf", s_)], stream="sto")
                dma_out(dlf[row0:row0 + P, :], lg[0:P, :], rd=[tok("logf", s_)], stream="sto")
        else:
            rope(B["KA"], tok("KA", s_), P, job["rsrc"], s_)
            A(lambda e: e.copy(out=B["kabf"][0:P, :], in_=B["KA"][0:P, :]), [tok("KA", s_)], [tok("kabf", s_)])
            G(lambda e: e.tensor_copy(out=Qtm[0:P, :, 0:64], in_=B["KB"][0:P, :].rearrange("p (h d) -> p h d", h=8)), [tok("KB", s_)], ["Qtm"])
            split3(cumq_sb[0:P, :], P, Qtm, "Qtm", 64, 8.0, "cumq", s_)

    def stageP2c(job):
        s_ = job["s"]
        B = BUF[s_]
        P = job["P"]
        lg = logf[:, s_, :]
        if job["kind"] in ("kv", "cache"):
            if job.get("ts_kt") is not None:
                kt = job["ts_kt"]
                mm(ps[5][:, 64:72], W1[:, 127 - kt:255 - kt], lg[:, :], True, True, ["const", tok("logf", s_)], ["ps5"])
                V(lambda e: e.tensor_tensor(out=TSs[:, :], in0=TSs[:, :], in1=ps[5][:, 64:72], op=ALU.add), ["ps5", "TSs"], ["TSs"])
            if job.get("SC") is not None:
                pack_kv_cum(job["kt"], P, job["first"], s_)
            if job.get("cumq") == "own":
                t = job["t"]
                mm(ps[5][:, 96:104], Um[:, :], lg[:, :], True, False, ["const", tok("logf", s_)], ["ps5"])
                V(lambda e, t=t: e.tensor_scalar(out=TSm[:, :], in0=TSs[:, :], scalar1=Lt[:, t:t + 1], scalar2=None, op0=ALU.mult),
                  ["TSs", "const"], ["TSm"])
                mm(ps[5][:, 96:104], ones_f[:, :], TSm[:, :], False, True, ["const", "TSm"], ["ps5"])
                V(lambda e: e.tensor_copy(out=cumq_sb[:, :], in_=ps[5][:, 96:104]), ["ps5"], ["cumq"])
            elif job.get("cumq") == "sample":
                V(lambda e: e.tensor_copy(out=cumq_sb[0:P, :], in_=cum[job["kt"] % 2][0:P, :]), ["cum%d" % (job["kt"] % 2)], ["cumq"])
        else:
            col0 = job["col0"]
            for half in range(2):
                for q in range(4):
                    ci = half * 4 + q
                    tr(ps[0][:, q * 128:q * 128 + P], zs[0:P, ci * 128:(ci + 1) * 128], identf[0:P, 0:P], ["zs%d" % half, "const"], ["ps0"])
                V(lambda e, half=half: e.tensor_copy(out=ZT[:, half * 4:(half + 1) * 4, col0:col0 + P],
                                                     in_=ps[0][:, :].rearrange("p (q t) -> p q t", q=4)[:, :, 0:P]), ["ps0"], ["ZT"])

    def stageP2b(job):
        s_ = job["s"]
        B = BUF[s_]
        P = job["P"]
        if job["kind"] in ("kv", "cache"):
            if job.get("SC") is not None:
                pack_kv_b(job["SC"], job["kt"], P, job["first"], s_)
        else:
            col0 = job["col0"]
            kp = psb(6)
            for h in range(4):
                tr(kp[:, h * 128:h * 128 + P], B["kabf"][0:P, h * 128:(h + 1) * 128], identb[0:P, 0:P], [tok("kabf", s_), "const"], ["ps6"])
            V(lambda e: e.tensor_copy(out=QTA[:, :, col0:col0 + P], in_=kp[:, 0:512].rearrange("p (h t) -> p h t", h=4)[:, :, 0:P]),
              ["ps6"], ["QTA"])
            kp2 = psb(7)
            for h in range(8):
                tr(kp2[0:70, h * 128:h * 128 + P], Qtm[0:P, h, :], identb[0:P, 0:P], ["Qtm", "const"], ["ps7"])
            V(lambda e: e.tensor_copy(out=QTB[:, :, col0:col0 + P], in_=kp2[0:70, :].rearrange("p (h t) -> p h t", h=8)[:, :, 0:P]),
              ["ps7"], ["QTB"])

    jslot = [0]
    xslot = [0]

    def run_jobs(jobs):
        n = len(jobs)
        for jb in jobs:
            jb["s"] = jslot[0] % 2
            jslot[0] += 1

        def wf(j):
            return j - 1 if jobs[j]["kind"] == "cache" else j - 2

        for j in range(min(2, n)):
            stageL(jobs[j])
        for j in range(n):
            if wf(j) < 0:
                stageF(jobs[j])
        if n > 2:
            stageL(jobs[2])
        for j in range(min(2, n)):
            stageR(jobs[j])
        stageF2(jobs[0])
        stageM(jobs[0])
        for k in range(n):
            if k + 3 < n:
                stageL(jobs[k + 3])
            for j in (k + 1, k + 2):
                if j < n and wf(j) == k:
                    stageF(jobs[j])
            stageP1(jobs[k])
            if k + 1 < n:
                stageF2(jobs[k + 1])
            if k >= 1:
                stageP2c(jobs[k - 1])
            if k >= 2:
                stageP2b(jobs[k - 2])
            stageP2a(jobs[k])
            if k + 2 < n:
                stageR(jobs[k + 2])
            if k + 1 < n:
                stageM(jobs[k + 1])
        stageP2c(jobs[n - 1])
        if n >= 2:
            stageP2b(jobs[n - 2])
        stageP2b(jobs[n - 1])

    abuf = [0]

    def attention(SC, chunks, N_full, unit):
        pre = SC["pre"]
        isA = unit < 4
        h = unit if isA else unit - 4
        its = []
        for ci, (kt0, nkt, c0, zone, lastP) in enumerate(chunks):
            b = abuf[0] % 2
            abuf[0] += 1
            for r in range(nkt):
                its.append(dict(b=b, r=r, KP=(lastP if r == nkt - 1 else 128), c0=c0,
                                mk=(zone[r] if zone is not None else None),
                                load=((kt0, nkt, lastP) if r == 0 else None)))
        n = len(its)

        def stageA(it, idx):
            b = it["b"]
            kb_, vb_ = kbuf[b], vbuf[b]
            if it["load"] is not None:
                kt0, nkt, lastP = it["load"]
                ncols = (nkt - 1) * 128 + lastP
                if isA:
                    dma_in(kb_[:, 0, 0:ncols], SC["KTA"][h, :, kt0 * 128:kt0 * 128 + ncols],
                           rd=[pre + "KTA%d" % k for k in range(kt0, kt0 + nkt)], wr=["kbuf%d" % b], stream="ldk")
                    nfull = nkt if lastP == 128 else nkt - 1
                    dma_in(vb_[:, 0:nfull, 0:128], SC["VA"][h, :, kt0:kt0 + nfull, :],
                           rd=[pre + "VA%d" % k for k in range(kt0, kt0 + nkt)], wr=["vbuf%d" % b], stream="ldk")
                    if nfull < nkt:
                        dma_in(vb_[0:lastP, nfull:nkt, 0:128], SC["VA"][h, 0:lastP, kt0 + nfull:kt0 + nkt, :],
                               rd=[pre + "VA%d" % k for k in range(kt0, kt0 + nkt)], wr=["vbuf%d" % b], stream="ldk")
                else:
                    dma_in(kb_[0:70, :, 0:ncols], SC["KTB"][2 * h:2 * h + 2, :, kt0 * 128:kt0 * 128 + ncols].rearrange("a p t -> p a t"),
                           rd=[pre + "KTB%d" % k for k in range(kt0, kt0 + nkt)], wr=["kbuf%d" % b], stream="ldk")
                    nfull = nkt if lastP == 128 else nkt - 1
                    dma_in(vb_[:, 0:nfull, :], SC["VB"][h, :, kt0:kt0 + nfull, :],
                           rd=[pre + "VB%d" % k for k in range(kt0, kt0 + nkt)], wr=["vbuf%d" % b], stream="ldk")
                    if nfull < nkt:
                        dma_in(vb_[0:lastP, nfull:nkt, :], SC["VB"][h, 0:lastP, kt0 + nfull:kt0 + nkt, :],
                               rd=[pre + "VB%d" % k for k in range(kt0, kt0 + nkt)], wr=["vbuf%d" % b], stream="ldk")
            r, KP, c0 = it["r"], it["KP"], it["c0"]
            N = N_full - c0
            sb_ = abuf[0] % 2
            abuf[0] += 1
            it["sb"] = sb_
            P_ = Pt[sb_]
            for u in range(2):
                sbank = sb_ * 2 + u
                if isA:
                    lhsT = kb_[u * 64:(u + 1) * 64, 0, r * 128:r * 128 + KP]
                    rhs = QTA[u * 64:(u + 1) * 64, h, c0:N_full]
                else:
                    lhsT = kb_[0:70, u, r * 128:r * 128 + KP]
                    rhs = QTB[0:70, 2 * h + u, c0:N_full]
                mm(ps[sbank][0:KP, 0:N], lhsT, rhs, True, True, ["kbuf%d" % b, "QTA" if isA else "QTB"], ["ps%d" % sbank])
            mk = it["mk"]
            if mk is not None:
                mw = mk.shape[-1]
                for u in range(2):
                    sbank = sb_ * 2 + u
                    V(lambda e, sbank=sbank, KP=KP, mk=mk, mw=mw: e.tensor_tensor(
                        out=ps[sbank][0:KP, 0:mw], in0=ps[sbank][0:KP, 0:mw], in1=mk, op=ALU.add),
                      ["ps%d" % sbank, "const"], ["ps%d" % sbank])
            act(P_[0:KP, :, 0:N], psS[0:KP, 2 * sb_:2 * sb_ + 2, 0:N], AF.Exp, ["ps%d" % (2 * sb_), "ps%d" % (2 * sb_ + 1)],
                ["Pt%d" % sb_], scale=0.125)

        def stageB(it, first, last):
            b, r, KP, c0, sb_ = it["b"], it["r"], it["KP"], it["c0"], it["sb"]
            vb_ = vbuf[b]
            P_ = Pt[sb_]
            N = N_full - c0
            for u in range(2):
                if isA:
                    vl = vb_[0:KP, r, 0:128]
                else:
                    vl = vb_[0:KP, r, u * 64:u * 64 + 128]
                mm(ps[4 + u][:, c0:N_full], vl, P_[0:KP, u, 0:N], first, last, ["vbuf%d" % b, "Pt%d" % sb_], ["ps%d" % (4 + u)])
            if isA:
                for u in range(2):
                    PE(lambda e, u=u, KP=KP, P_=P_, N=N, c0=c0: e.matmul(
                        ps[6 + u][32 * u:32 * u + 32, c0:N_full], ones_bf[0:KP, 0:32], P_[0:KP, u, 0:N],
                        start=first, stop=last, tile_position=(0, 32 * u)),
                       ["const", "Pt%d" % sb_], ["ps%d" % (6 + u)])

        for idx in range(n + 1):
            if idx < n:
                stageA(its[idx], idx)
            if idx >= 1:
                stageB(its[idx - 1], idx - 1 == 0, idx - 1 == n - 1)

    def merge_unit(unit, N):
        if unit < 4:
            V(lambda e: e.tensor_copy(out=Lsb[0:32, 0:N], in_=ps[6][0:32, 0:N]), ["ps6"], ["Lsb"])
            V(lambda e: e.tensor_copy(out=Lsb[32:64, 0:N], in_=ps[7][32:64, 0:N]), ["ps7"], ["Lsb"])
            mm(ps[0][:, 0:N], bsel[0:32, :], Lsb[0:32, 0:N], True, True, ["const", "Lsb"], ["ps0"])
            mm(ps[1][:, 0:N], bsel[32:64, :], Lsb[32:64, 0:N], True, True, ["const", "Lsb"], ["ps1"])
            act(mt[0][:, 0:N], ps[0][:, 0:N], AF.Ln, ["ps0"], ["KA"])
            act(mt[0][:, 0:N], mt[0][:, 0:N], AF.Exp, ["KA"], ["KA"], scale=-1.0)
            act(mt[1][:, 0:N], ps[1][:, 0:N], AF.Ln, ["ps1"], ["VAf"])
            act(mt[1][:, 0:N], mt[1][:, 0:N], AF.Exp, ["VAf"], ["VAf"], scale=-1.0)
            V(lambda e: e.tensor_tensor(out=mt[0][:, 0:N], in0=ps[4][:, 0:N], in1=mt[0][:, 0:N], op=ALU.mult), ["ps4", "KA"], ["KA"])
            V(lambda e: e.tensor_tensor(out=mt[1][:, 0:N], in0=ps[5][:, 0:N], in1=mt[1][:, 0:N], op=ALU.mult), ["ps5", "VAf"], ["VAf"])
            V(lambda e: e.scalar_tensor_tensor(out=mt[2][:, 0:N], in0=mt[1][:, 0:N], scalar=small[:, 1:2], in1=mt[0][:, 0:N],
                                               op0=ALU.mult, op1=ALU.add), ["KA", "VAf", "neglam"], ["KB"])
            act(mt[3][:, 0:N], mt[2][:, 0:N], AF.Square, ["KB"], ["VBf"])
            mm(ps[0][:, 0:N], ones_f[:, :], mt[3][:, 0:N], True, True, ["const", "VBf"], ["ps0"])
            act(mt[0][:, 0:N], ps[0][:, 0:N], AF.Ln, ["ps0"], ["KA"], bias=EPS, scale=1.0 / 128)
            act(mt[1][:, 0:N], mt[0][:, 0:N], AF.Exp, ["KA"], ["VAf"], scale=-0.5)
            V(lambda e: e.scalar_tensor_tensor(out=mt[2][:, 0:N], in0=mt[2][:, 0:N], scalar=small[:, 7:8], in1=mt[1][:, 0:N],
                                               op0=ALU.mult, op1=ALU.mult), ["KB", "VAf", "subg2"], ["KB"])
            V(lambda e: e.tensor_tensor(out=GT[:, unit, 0:N], in0=mt[2][:, 0:N], in1=ZT[:, unit, 0:N], op=ALU.mult),
              ["KB", "ZT"], ["GT"])
        else:
            V(lambda e: e.tensor_copy(out=mt[0][0:64, 0:N], in_=ps[5][0:64, 0:N]), ["ps5"], ["KA"])
            V(lambda e: e.tensor_copy(out=mt[0][64:128, 0:N], in_=ps[4][64:128, 0:N]), ["ps4"], ["KA"])
            mm(ps[0][:, 0:N], Psw[:, :], mt[0][:, 0:N], True, True, ["const", "KA"], ["ps0"])
            act(mt[1][:, 0:N], ps[0][:, 0:N], AF.Ln, ["ps0"], ["VAf"])
            act(mt[1][:, 0:N], mt[1][:, 0:N], AF.Exp, ["VAf"], ["VAf"], scale=-1.0)
            V(lambda e: e.tensor_tensor(out=mt[2][0:64, 0:N], in0=ps[4][0:64, 0:N], in1=mt[1][0:64, 0:N], op=ALU.mult), ["ps4", "VAf"], ["KB"])
            V(lambda e: e.tensor_tensor(out=mt[2][64:128, 0:N], in0=ps[5][64:128, 0:N], in1=mt[1][64:128, 0:N], op=ALU.mult), ["ps5", "VAf"], ["KB"])
            V(lambda e: e.tensor_tensor(out=GT[:, unit, 0:N], in0=mt[2][:, 0:N], in1=ZT[:, unit, 0:N], op=ALU.mult),
              ["KB", "ZT"], ["GT"])

    def out_proj(xsrc, ydst, P, col0):
        s_ = xslot[0] % 2
        xslot[0] += 1
        X = xin[s_]
        tk = "xin%d" % s_
        dma_in(X[0:P, :], xsrc, wr=[tk], stream="ldx")
        for half in range(2):
            bank = 1 + half
            for ch in range(8):
                mm(ps[bank][0:P, :], GT[:, ch, col0:col0 + P], Wout[:, ch, half * 512:(half + 1) * 512], ch == 0, ch == 7,
                   ["GT", "W"], ["ps%d" % bank])
            V(lambda e, half=half, bank=bank: e.tensor_tensor(out=zs[0:P, half * 512:(half + 1) * 512], in0=ps[bank][0:P, :],
                                                              in1=gate_b[0:P, half * 512:(half + 1) * 512], op=ALU.mult),
              ["ps%d" % bank, "modb"], ["zs%d" % half])
            V(lambda e, half=half: e.tensor_tensor(out=zs[0:P, half * 512:(half + 1) * 512], in0=zs[0:P, half * 512:(half + 1) * 512],
                                                   in1=X[0:P, half * 512:(half + 1) * 512], op=ALU.add),
              ["zs%d" % half, tk], ["zs%d" % half])
        dma_out(ydst, zs[0:P, :], rd=["zs0", "zs1"], wr=["yout"], stream="sto")

    bridge(True)
    set_mod(1)
    jobs = [dict(kind="cache", P=128, r0=kt * 128, SC=SCS, kt=kt, first=(kt == 0)) for kt in range(8)]
    jobs.append(dict(kind="kv", P=TDEC, xsrc=x_smp[:, :], groups=G_KV, rsrc=rope_smp[:, :], SC=SCS, kt=8, first=False,
                     outs=(s_ak, s_av, s_bk, s_bv, s_lf), row0=0, cumq="sample"))
    jobs.append(dict(kind="qz", P=TDEC, xsrc=x_smp[:, :], groups=G_QZ, rsrc=rope_smp[:, :], col0=0))
    run_jobs(jobs)
    bridge(False)
    zoneS_A = [None] * 9
    zoneS_B = [None] * 8 + [maskS[:, :]]
    for unit in range(8):
        attention(SCS, [(0, 9, 0, zoneS_A if unit < 4 else zoneS_B, TDEC)], TDEC, unit)
        merge_unit(unit, TDEC)
    out_proj(x_smp[:, :], o_ys[:, :], TDEC, 0)
    bridge(True)

    set_mod(0)
    run_jobs([dict(kind="kv", P=128, xsrc=x_all[kt * 128:(kt + 1) * 128, :], groups=G_KV, rsrc=rope_all[kt, :, :],
                   SC=SCP, kt=kt, first=(kt == 0), ts_kt=kt) for kt in range(NT)])

    for j in range(4):
        jobs = []
        for m in range(4):
            t = 4 * j + m
            rows = slice(t * 128, (t + 1) * 128)
            jobs.append(dict(kind="kv", P=128, xsrc=x_own[rows, :], groups=G_KV, rsrc=rope_own[t, :, :],
                             outs=(o_ak, o_av, o_bk, o_bv, o_lf), row0=t * 128, cumq="own", t=t))
            jobs.append(dict(kind="qz", P=128, xsrc=x_own[rows, :], groups=G_QZ, rsrc=rope_own[t, :, :], col0=m * 128))
        run_jobs(jobs)
        bridge(False)
        for unit in range(8):
            chunks = [(8 * ch, 8, 0, None, 128) for ch in range(4 * j)]
            mtab = maskA if unit < 4 else maskB
            for m in range(4):
                chunks.append((8 * (4 * j + m), 8, m * 128, [mtab[:, r, :] for r in range(8)], 128))
            attention(SCP, chunks, 512, unit)
            merge_unit(unit, 512)
        for m in range(4):
            t = 4 * j + m
            out_proj(x_own[t * 128:(t + 1) * 128, :], o_y[t * 128:(t + 1) * 128, :], 128, m * 128)
        bridge(True)

    S.emit(nc, st)
    st.close()
    return nc


_NC_CACHE = {}


def _rope_table(pos):
    half = 8
    inv = (np.float32(500000.0) ** (-np.arange(0, 16, 2, dtype=np.float32) / np.float32(16))).astype(np.float32)
    ang = pos.astype(np.float32)[:, None] * inv[None, :]
    return np.concatenate([np.cos(ang), np.sin(ang)], axis=1).astype(np.float32)


def kernel(x_prompt, x_sample, cache_a_k, cache_a_v, cache_b_k, cache_b_v, cache_b_logf,
           c_prompt, c_sample, norm_g, w_ada, b_ada, w_in, b_f, qn_a, kn_a,
           lam_q1, lam_k1, lam_q2, lam_k2, subln_g, qn_b, kn_b, w_out):
    f = lambda a: np.ascontiguousarray(np.asarray(a, dtype=np.float32))
    xp = f(x_prompt)[0]
    xs = f(x_sample)
    w_in0 = f(w_in)[0]
    o = np.cumsum([0, 512, 512, 512, 512, 512, 512, 512, 8, 512])
    seg = lambda i: w_in0[:, o[i]:o[i + 1]]
    w_kv = np.ascontiguousarray(np.concatenate([seg(1), seg(2), seg(5), seg(6), seg(7)], axis=1))
    w_qz = np.ascontiguousarray(np.concatenate([seg(0), seg(3), seg(4), seg(8)], axis=1))
    rep = lambda v, n=128: np.ascontiguousarray(np.broadcast_to(f(v).reshape(1, -1), (n, f(v).size)))
    vecs = np.ascontiguousarray(np.stack([rep(qn_a[0]), rep(kn_a[0]), rep(qn_b[0]), rep(kn_b[0])], axis=1))
    lamv = np.ascontiguousarray(np.stack([rep(lam_q1[0]), rep(lam_k1[0]), rep(lam_q2[0]), rep(lam_k2[0])], axis=1))
    bf_b = rep(b_f[0])
    subg = f(subln_g)[0].reshape(128, 1).copy()
    bf16 = ml_dtypes.bfloat16
    eye = np.eye(128, dtype=np.float32)
    ar = np.arange(128)
    Umat = (ar[:, None] <= ar[None, :]).astype(np.float32)
    E127 = np.zeros((128, 128), np.float32); E127[127, :] = 1.0
    Psw = np.zeros((128, 128), np.float32); Psw[ar, (ar + 64) % 128] = 1.0
    W1 = np.zeros((128, 256), np.float32); W1[:, 127] = 1.0
    sel2 = np.zeros((2, 2, 128), np.float32); sel2[0, 0, :] = 1.0; sel2[1, 1, :] = 1.0
    bsel = np.zeros((64, 128), np.float32); bsel[0, :] = 1.0; bsel[32, :] = 1.0
    tril = (ar[:, None] <= ar[None, :])
    chunk_ok = ((ar[:, None] // 64) <= (ar[None, :] // 64))
    maskS = (((np.arange(64)[:, None] <= np.arange(64)[None, :]).astype(np.float32) - 1.0) * 30000.0).astype(bf16)
    rope_all = _rope_table(np.arange(SEQ)).reshape(NT, 128, 16)
    rope_smp = _rope_table(PAST + np.arange(TDEC))
    xt = xp.reshape(NT, 128, D)
    in_maps = []
    for c in range(8):
        own = np.arange(NOWN) * 8 + c
        Lt = np.zeros((128, NOWN), np.float32)
        mA = np.zeros((128, 8, 128), np.float32)
        mB = np.zeros((128, 8, 128), np.float32)
        for t in range(NOWN):
            Lt[:8 * t + c, t] = 1.0
        for r in range(8):
            if r < c:
                mA[:, r, :] = 1.0; mB[:, r, :] = 1.0
            elif r == c:
                mA[:, r, :] = chunk_ok; mB[:, r, :] = tril
        cvT = np.stack([f(c_prompt)[0], f(c_sample)[c]], axis=1).reshape(8, 128, 2).transpose(1, 0, 2)
        in_maps.append(dict(
            x_all=xp, x_own=np.ascontiguousarray(xt[own].reshape(NOWN * 128, D)), x_smp=np.ascontiguousarray(xs[c]),
            cvT=np.ascontiguousarray(cvT), w_ada=f(w_ada)[0], b_ada2=np.ascontiguousarray(np.broadcast_to(f(b_ada)[0][None], (2, 3 * D))),
            g_row2=np.ascontiguousarray(np.broadcast_to(f(norm_g)[0][None], (2, D))),
            w_kv=w_kv, w_qz=w_qz, w_out=f(w_out)[0], vecs=vecs, lamv=lamv, bf_b=bf_b, subg=subg,
            c_ak=np.ascontiguousarray(f(cache_a_k)[0, c].reshape(PAST, 512)),
            c_av=np.ascontiguousarray(f(cache_a_v)[0, c].reshape(PAST, 512)),
            c_bk=np.ascontiguousarray(f(cache_b_k)[0, c].reshape(PAST, 512)),
            c_bv=np.ascontiguousarray(f(cache_b_v)[0, c].reshape(PAST, 512)),
            c_lf=np.ascontiguousarray(f(cache_b_logf)[0, c].reshape(PAST, 8)),
            rope_all=rope_all, rope_own=np.ascontiguousarray(rope_all[own]), rope_smp=rope_smp,
            k_identb=eye.astype(bf16), k_identf=eye, k_U=Umat, k_E127=E127, k_Psw=Psw, k_W1=W1, k_Lt=Lt, k_sel2=sel2, k_bsel=bsel,
            k_maskA=((mA - 1.0) * 30000.0).astype(bf16), k_maskB=((mB - 1.0) * 30000.0).astype(bf16), k_maskS=maskS,
        ))
    if "nc" not in _NC_CACHE:
        _NC_CACHE["nc"] = build_nc()
    res = run_bass_kernel_spmd(_NC_CACHE["nc"], in_maps, core_ids=list(range(8)))
    R = res.results

    def gather_p(key, width):
        out = np.zeros((NT, 128, width), np.float32)
        for c in range(8):
            out[np.arange(NOWN) * 8 + c] = np.asarray(R[c][key], dtype=np.float32).reshape(NOWN, 128, width)
        return out.reshape(SEQ, width)

    def gather_s(key, width):
        return np.stack([np.asarray(R[c][key], dtype=np.float32).reshape(TDEC, width) for c in range(8)], axis=0)

    y_p = gather_p("o_y", D).reshape(1, SEQ, D)
    y_s = gather_s("o_ys", D)
    return (y_p, y_s,
            gather_p("o_ak", 512).reshape(1, 1, SEQ, 4, 2, 64), gather_p("o_av", 512).reshape(1, 1, SEQ, 4, 128),
            gather_p("o_bk", 512).reshape(1, 1, SEQ, 8, 64), gather_p("o_bv", 512).reshape(1, 1, SEQ, 8, 64),
            gather_p("o_lf", 8).reshape(1, 1, SEQ, 8),
            gather_s("s_ak", 512).reshape(1, 8, TDEC, 4, 2, 64), gather_s("s_av", 512).reshape(1, 8, TDEC, 4, 128),
            gather_s("s_bk", 512).reshape(1, 8, TDEC, 8, 64), gather_s("s_bv", 512).reshape(1, 8, TDEC, 8, 64),
            gather_s("s_lf", 8).reshape(1, 8, TDEC, 8))
```
